# Optimizing a Trainium2 kernel written in Bass

```python
import jax, jax.numpy as jnp
from jax import lax
import numpy as np

D_MODEL = 1024
BATCH = 8
SEQ = 2048
DEPTH = 1

HG_HEADS = 4
HG_HD = 128
HG_WIDTH = HG_HEADS * HG_HD
HG_CHUNK = 64
NSA_HEADS = 8
NSA_KV_HEADS = 2
NSA_HD = 64
NSA_GROUP = NSA_HEADS // NSA_KV_HEADS
NSA_WIDTH = NSA_HEADS * NSA_HD
NSA_KV_WIDTH = NSA_KV_HEADS * NSA_HD
N_BRANCH = 3
CMP_BLOCK = 32
CMP_STRIDE = 16
CMP_HIDDEN = 256
SLC_BLOCK = 64
SLC_TOPK = 16
SLC_Q_CHUNK = 64
WIN = 512
WIN_Q_BLOCK = 128
MIX_WIDTH = HG_WIDTH + NSA_WIDTH
ROPE_DIM = NSA_HD // 4
ROPE_THETA = 500000.0
D_FF = 2816
CONV_W = 3
EPS = 1e-6
NEG = -1e30
IN_SIZES = [HG_WIDTH] * 4 + [NSA_WIDTH] + [NSA_KV_WIDTH] * 6 + [N_BRANCH * NSA_HEADS]
IN_COLS = sum(IN_SIZES)

kernel_name = "hymba_hgrn2_nsa_convffn_adaln"


def rmsnorm(x, g):
    xf = x.astype(jnp.float32)
    y = xf * lax.rsqrt(jnp.mean(xf * xf, axis=-1, keepdims=True) + EPS)
    return (y * g.astype(jnp.float32)).astype(x.dtype)


def rope_partial(x, pos):
    half = ROPE_DIM // 2
    inv = ROPE_THETA ** (-jnp.arange(half, dtype=jnp.float32) * 2.0 / ROPE_DIM)
    ang = pos.astype(jnp.float32)[:, None, :, None] * inv
    cos, sin = jnp.cos(ang), jnp.sin(ang)
    xf = x.astype(jnp.float32)
    x1, x2, rest = xf[..., :half], xf[..., half:ROPE_DIM], xf[..., ROPE_DIM:]
    out = jnp.concatenate([x1 * cos - x2 * sin, x2 * cos + x1 * sin, rest], axis=-1)
    return out.astype(x.dtype)


def to_heads(z, n, hd):
    B, T, _ = z.shape
    return z.reshape(B, T, n, hd).transpose(0, 2, 1, 3)


def hgrn2(q_pre, f_pre, i_pre, g_pre, lb, norm_g):
    B, T, _ = q_pre.shape
    f32 = jnp.float32
    nc = T // HG_CHUNK
    zf = f_pre.astype(f32)
    lbf = lb.astype(f32)
    logf = jnp.logaddexp(jnp.log(lbf), jnp.log1p(-lbf) + jax.nn.log_sigmoid(zf))
    k = (1.0 - lbf) * jax.nn.sigmoid(-zf)
    q = jax.nn.silu(q_pre.astype(f32))
    v = i_pre.astype(f32)

    def chunked(z):
        return z.reshape(B, nc, HG_CHUNK, HG_HEADS, HG_HD).transpose(1, 0, 3, 2, 4)

    qc, kc, vc = chunked(q), chunked(k), chunked(v)
    bc = jnp.cumsum(chunked(logf), axis=-2)
    causal = jnp.tril(jnp.ones((HG_CHUNK, HG_CHUNK), dtype=bool))

    def step(S, inp):
        qi, ki, vi, bi = inp
        o_inter = jnp.einsum('bhtk,bhkv->bhtv', qi * jnp.exp(bi), S)
        diff = bi[:, :, :, None, :] - bi[:, :, None, :, :]
        dec = jnp.where(causal[:, :, None], jnp.exp(jnp.minimum(diff, 0.0)), 0.0)
        A = jnp.einsum('bhtsk,bhsk->bhts', dec * qi[:, :, :, None, :], ki)
        o = o_inter + jnp.einsum('bhts,bhsv->bhtv', A, vi)
        bl = bi[:, :, -1:, :]
        S = jnp.exp(bl[:, :, 0, :])[..., None] * S + jnp.einsum('bhsk,bhsv->bhkv', ki * jnp.exp(bl - bi), vi)
        return S, o

    S0 = jnp.zeros((B, HG_HEADS, HG_HD, HG_HD), f32)
    _, o = lax.scan(step, S0, (qc, kc, vc, bc))
    o = o.transpose(1, 0, 3, 2, 4).reshape(B, T, HG_HEADS, HG_HD)
    g = jax.nn.silu(g_pre.astype(f32)).reshape(B, T, HG_HEADS, HG_HD)
    o = rmsnorm(o, norm_g) * g
    return o.reshape(B, T, HG_WIDTH).astype(q_pre.dtype)


def compress(blocks, pe, w1, w2):
    B, G, n, L, hd = blocks.shape
    flat = (blocks + pe).reshape(B, G, n, L * hd)
    return jax.nn.silu(flat @ w1) @ w2


def nsa(q_pre, kc_pre, vc_pre, ks_pre, vs_pre, kw_pre, vw_pre, g_pre, pos, q_g, k_g, pe, w1, w2):
    B, T, _ = q_pre.shape
    f32 = jnp.float32
    G, R, hd = NSA_KV_HEADS, NSA_GROUP, NSA_HD
    scale = NSA_HD ** -0.5
    tpos = jnp.arange(T)

    q = rope_partial(rmsnorm(to_heads(q_pre, NSA_HEADS, hd), q_g), pos).reshape(B, G, R, T, hd)
    k_c = rope_partial(rmsnorm(to_heads(kc_pre, G, hd), k_g[0]), pos)
    k_s = rope_partial(rmsnorm(to_heads(ks_pre, G, hd), k_g[1]), pos)
    k_w = rope_partial(rmsnorm(to_heads(kw_pre, G, hd), k_g[2]), pos)
    v_c, v_s, v_w = to_heads(vc_pre, G, hd), to_heads(vs_pre, G, hd), to_heads(vw_pre, G, hd)

    n_cmp = (T - CMP_BLOCK) // CMP_STRIDE + 1
    cidx = np.arange(n_cmp)[:, None] * CMP_STRIDE + np.arange(CMP_BLOCK)[None]
    k_cmp = compress(k_c[:, :, cidx], pe[0], w1[0], w2[0])
    v_cmp = compress(v_c[:, :, cidx], pe[1], w1[1], w2[1])
    s = jnp.einsum('bgrtd,bgnd->bgrtn', q, k_cmp).astype(f32) * scale
    cvalid = jnp.asarray(cidx[:, -1])[None, :] <= tpos[:, None]
    any_valid = jnp.any(cvalid, axis=-1, keepdims=True)
    p_cmp = jax.nn.softmax(jnp.where(cvalid, s, NEG), axis=-1) * any_valid
    o_cmp = jnp.einsum('bgrtn,bgnd->bgrtd', p_cmp.astype(v_cmp.dtype), v_cmp)

    nb = T // SLC_BLOCK
    n_sel = min(SLC_TOPK, nb)
    cst = np.arange(n_cmp) * CMP_STRIDE
    sst = np.arange(nb) * SLC_BLOCK
    ovl = np.clip(np.minimum(cst[:, None] + CMP_BLOCK, sst[None] + SLC_BLOCK)
                  - np.maximum(cst[:, None], sst[None]), 0, None) / CMP_BLOCK
    M = jnp.asarray(ovl, dtype=f32)
    imp = jnp.einsum('bgtn,nj->bgtj', p_cmp.sum(axis=2), M)
    cur = tpos // SLC_BLOCK
    j = jnp.arange(nb)
    forced = (j[None] == 0) | (j[None] == cur[:, None]) | (j[None] == cur[:, None] - 1)
    blk_causal = j[None] <= cur[:, None]
    imp = jnp.where(blk_causal, jnp.where(forced, jnp.inf, imp), -1.0)
    _, sel = lax.top_k(imp, n_sel)

    kb = k_s.reshape(B, G, nb, SLC_BLOCK, hd)
    vb = v_s.reshape(B, G, nb, SLC_BLOCK, hd)
    nq = T // SLC_Q_CHUNK
    q_ch = q.reshape(B, G, R, nq, SLC_Q_CHUNK, hd).transpose(3, 0, 1, 2, 4, 5)
    sel_ch = sel.reshape(B, G, nq, SLC_Q_CHUNK, n_sel).transpose(2, 0, 1, 3, 4)
    t_ch = tpos.reshape(nq, SLC_Q_CHUNK)
    gather = jax.vmap(jax.vmap(lambda blocks, ix: blocks[ix]))
    offs = jnp.arange(SLC_BLOCK)

    def sel_chunk(args):
        qc, ic, tc = args
        kg = gather(kb, ic)
        vg = gather(vb, ic)
        sc = jnp.einsum('bgrqd,bgqnld->bgrqnl', qc, kg).astype(f32) * scale
        kpos = ic[..., None] * SLC_BLOCK + offs
        m = kpos <= tc[None, None, :, None, None]
        sc = jnp.where(m[:, :, None], sc, NEG)
        shp = sc.shape
        p = jax.nn.softmax(sc.reshape(shp[:-2] + (shp[-2] * shp[-1],)), axis=-1).reshape(shp)
        return jnp.einsum('bgrqnl,bgqnld->bgrqd', p.astype(vg.dtype), vg)

    o_slc = lax.map(sel_chunk, (q_ch, sel_ch, t_ch))
    o_slc = o_slc.transpose(1, 2, 3, 0, 4, 5).reshape(B, G, R, T, hd)

    nwb = T // WIN_Q_BLOCK
    span = WIN + WIN_Q_BLOCK
    widx = np.arange(nwb)[:, None] * WIN_Q_BLOCK + np.arange(span)[None]
    padw = ((0, 0), (0, 0), (WIN, 0), (0, 0))
    kwin = jnp.pad(k_w, padw)[:, :, widx]
    vwin = jnp.pad(v_w, padw)[:, :, widx]
    qw = q.reshape(B, G, R, nwb, WIN_Q_BLOCK, hd)
    sw = jnp.einsum('bgrnqd,bgnkd->bgrnqk', qw, kwin).astype(f32) * scale
    kpos = jnp.asarray(widx - WIN)[:, None, :]
    qpos = (jnp.arange(nwb)[:, None] * WIN_Q_BLOCK + jnp.arange(WIN_Q_BLOCK)[None])[:, :, None]
    mw = (kpos <= qpos) & (qpos - kpos < WIN) & (kpos >= 0)
    pw = jax.nn.softmax(jnp.where(mw, sw, NEG), axis=-1)
    o_win = jnp.einsum('bgrnqk,bgnkd->bgrnqd', pw.astype(vwin.dtype), vwin).reshape(B, G, R, T, hd)

    gates = jax.nn.sigmoid(g_pre.astype(f32)).reshape(B, T, NSA_HEADS, N_BRANCH)
    gates = gates.transpose(0, 2, 1, 3).reshape(B, G, R, T, N_BRANCH)
    o = (gates[..., 0:1] * o_cmp.astype(f32) + gates[..., 1:2] * o_slc.astype(f32)
         + gates[..., 2:3] * o_win.astype(f32))
    o = o.reshape(B, NSA_HEADS, T, hd).transpose(0, 2, 1, 3).reshape(B, T, NSA_WIDTH)
    return o.astype(q_pre.dtype)


def causal_dwconv(u, w, b):
    T = u.shape[1]
    up = jnp.pad(u, ((0, 0), (CONV_W - 1, 0), (0, 0)))
    y = b
    for j in range(CONV_W):
        y = y + up[:, j:j + T] * w[j]
    return y


def setup_inputs(seed: int = 0) -> dict:
    key = jax.random.key(seed)
    ks = jax.random.split(key, 24)
    f32 = jnp.float32

    def nrm(k, shape, s):
        return jax.random.normal(k, shape, f32) * s

    return {
        "x": nrm(ks[0], (BATCH, SEQ, D_MODEL), 1.0),
        "c": nrm(ks[1], (BATCH, D_MODEL), 1.0),
        "positions": (jnp.arange(SEQ, dtype=jnp.int32)[None]
                      + jax.random.randint(ks[2], (BATCH, 1), 0, 4096, dtype=jnp.int32)),
        "w_ada": nrm(ks[3], (DEPTH, D_MODEL, 6 * D_MODEL), D_MODEL ** -0.5),
        "b_ada": nrm(ks[4], (DEPTH, 6 * D_MODEL), 0.02),
        "norm1_g": 1.0 + nrm(ks[5], (DEPTH, D_MODEL), 0.02),
        "w_in": nrm(ks[6], (DEPTH, D_MODEL, IN_COLS), D_MODEL ** -0.5),
        "lb_logits": nrm(ks[7], (DEPTH + 1, HG_WIDTH), 1.0),
        "hg_norm_g": 1.0 + nrm(ks[8], (DEPTH, HG_HD), 0.02),
        "q_norm_g": 1.0 + nrm(ks[9], (DEPTH, NSA_HD), 0.02),
        "k_norm_g": 1.0 + nrm(ks[10], (DEPTH, N_BRANCH, NSA_HD), 0.02),
        "pe_cmp": nrm(ks[11], (DEPTH, 2, CMP_BLOCK, NSA_HD), 0.02),
        "w_cmp1": nrm(ks[12], (DEPTH, 2, CMP_BLOCK * NSA_HD, CMP_HIDDEN), (CMP_BLOCK * NSA_HD) ** -0.5),
        "w_cmp2": nrm(ks[13], (DEPTH, 2, CMP_HIDDEN, NSA_HD), CMP_HIDDEN ** -0.5),
        "w_out": nrm(ks[14], (DEPTH, MIX_WIDTH, D_MODEL), MIX_WIDTH ** -0.5),
        "norm2_g": 1.0 + nrm(ks[15], (DEPTH, D_MODEL), 0.02),
        "w_up": nrm(ks[16], (DEPTH, D_MODEL, 2 * D_FF), D_MODEL ** -0.5),
        "conv_w": nrm(ks[17], (DEPTH, CONV_W, 2 * D_FF), CONV_W ** -0.5),
        "conv_b": nrm(ks[18], (DEPTH, 2 * D_FF), 0.02),
        "w_down": nrm(ks[19], (DEPTH, D_FF, D_MODEL), D_FF ** -0.5),
    }


def reference(x, c, positions, w_ada, b_ada, norm1_g, w_in, lb_logits, hg_norm_g, q_norm_g,
              k_norm_g, pe_cmp, w_cmp1, w_cmp2, w_out, norm2_g, w_up, conv_w, conv_b, w_down):
    offsets = np.cumsum(IN_SIZES)[:-1].tolist()
    lbs = jnp.cumsum(jax.nn.softmax(lb_logits.astype(jnp.float32), axis=0), axis=0)
    cs = jax.nn.silu(c)
    for l in range(DEPTH):
        mod = cs @ w_ada[l] + b_ada[l]
        sh1, sc1, gt1, sh2, sc2, gt2 = [m[:, None, :] for m in jnp.split(mod, 6, axis=-1)]

        h = rmsnorm(x, norm1_g[l]) * (1.0 + sc1) + sh1
        z = h @ w_in[l]
        (hq, hf, hi, hgt, nq_, kc_, vc_, ks_, vs_, kw_, vw_, ng_) = jnp.split(z, offsets, axis=-1)
        o_hg = hgrn2(hq, hf, hi, hgt, lbs[l], hg_norm_g[l])
        o_nsa = nsa(nq_, kc_, vc_, ks_, vs_, kw_, vw_, ng_, positions, q_norm_g[l], k_norm_g[l],
                    pe_cmp[l], w_cmp1[l], w_cmp2[l])
        mix = jnp.concatenate([o_hg, o_nsa], axis=-1) @ w_out[l]
        x = x + gt1 * mix

        h2 = rmsnorm(x, norm2_g[l]) * (1.0 + sc2) + sh2
        u = causal_dwconv(h2 @ w_up[l], conv_w[l], conv_b[l])
        a, v = jnp.split(u, 2, axis=-1)
        x = x + gt2 * ((jax.nn.silu(a) * v) @ w_down[l])
    return x
```

```python
import numpy as np
from contextlib import ExitStack
import concourse.bass as bass
import concourse.mybir as mybir
from concourse.bass_utils import run_bass_kernel_spmd

F32, BF16, I32 = mybir.dt.float32, mybir.dt.bfloat16, mybir.dt.int32
AF = mybir.ActivationFunctionType
ALU = mybir.AluOpType
AX = mybir.AxisListType

T = 2048
D = 1024
NT = 16
IN_COLS = 3352
DFF = 2816
EPS = 1e-6

ENGINES = ("pe", "act", "dve", "pool", "sp")
SAME_ENGINE_SYNC = {"pe": False, "act": True, "dve": True, "pool": True, "sp": False}
N_DMA_SEMS = 32


class Op:
    __slots__ = ("eng", "fn", "deps", "idx", "needs_inc", "val", "is_dma", "dsem", "dval")

    def __init__(self, eng, fn, is_dma):
        self.eng = eng
        self.fn = fn
        self.deps = []
        self.idx = None
        self.needs_inc = False
        self.val = None
        self.is_dma = is_dma
        self.dsem = None
        self.dval = None


class Sched:
    def __init__(self, nc):
        self.nc = nc
        self.q = {e: [] for e in ENGINES}
        self.last_w = {}
        self.readers = {}
        self.known = {e: {f: 0 for f in ENGINES} for e in ENGINES}
        self.known_dma = {e: set() for e in ENGINES}
        self.n_dma = 0
        self.dma_ops = []
        self.dma_by_q = {}

    def add(self, eng, fn, reads=(), writes=(), dma=False):
        op = Op(eng, fn, dma)
        op.idx = len(self.q[eng]) + 1
        deps = []
        for r in reads:
            w = self.last_w.get(r)
            if w is not None:
                deps.append(w)
        for w_ in writes:
            w = self.last_w.get(w_)
            if w is not None:
                deps.append(w)
            deps.extend(self.readers.get(w_, ()))
        kn = self.known[eng]
        kd = self.known_dma[eng]
        seen = set()
        for d in deps:
            if d is op or id(d) in seen:
                continue
            seen.add(id(d))
            if d.is_dma:
                if id(d) in kd:
                    continue
                kd.add(id(d))
                op.deps.append(d)
            else:
                if d.eng == eng and not SAME_ENGINE_SYNC[eng]:
                    continue
                if kn[d.eng] >= d.idx:
                    continue
                kn[d.eng] = d.idx
                d.needs_inc = True
                op.deps.append(d)
        if dma:
            lst = self.dma_by_q.setdefault(eng, [])
            n = len(lst)
            base = 0 if eng == "sp" else N_DMA_SEMS // 2
            half = N_DMA_SEMS // 2
            op.dsem = base + n % half
            op.dval = 16 * (n // half + 1)
            if n >= half:
                prev = lst[n - half]
                if id(prev) not in kd:
                    kd.add(id(prev))
                    op.deps.append(prev)
            lst.append(op)
            self.n_dma += 1
            self.dma_ops.append(op)
        self.q[eng].append(op)
        for r in reads:
            self.readers.setdefault(r, []).append(op)
        for w_ in writes:
            self.last_w[w_] = op
            self.readers[w_] = []
        return op

    def emit(self):
        nc = self.nc
        with ExitStack() as es:
            esem = {e: es.enter_context(nc.semaphore("s_" + e)) for e in ENGINES}
            dsem = [es.enter_context(nc.semaphore("d_%d" % i)) for i in range(N_DMA_SEMS)]
            for e in ENGINES:
                c = 0
                for op in self.q[e]:
                    if (not op.is_dma) and op.needs_inc:
                        c += 1
                        op.val = c
            block = es.enter_context(nc.Block())

            def replay(e):
                def run(eng):
                    for op in self.q[e]:
                        for d in op.deps:
                            if d.is_dma:
                                eng.wait_ge(dsem[d.dsem], d.dval)
                            else:
                                eng.wait_ge(esem[d.eng], d.val)
                        ins = op.fn(eng)
                        if op.is_dma:
                            ins.then_inc(dsem[op.dsem], 16)
                        elif op.needs_inc:
                            ins.then_inc(esem[e], 1)
                    if e == "sp":
                        for lst in self.dma_by_q.values():
                            for d in lst[-min(len(lst), N_DMA_SEMS // 2):]:
                                eng.wait_ge(dsem[d.dsem], d.dval)
                return run

            block.tensor(replay("pe"))
            block.scalar(replay("act"))
            block.vector(replay("dve"))
            block.gpsimd(replay("pool"))
            block.sync(replay("sp"))


def build(dbg=()):
    nc = bass.Bass("TRN2", target_bir_lowering=False)
    S = Sched(nc)
    dram_in = {}

    def din(name, shape, dt=F32):
        dram_in[name] = nc.dram_tensor(name, list(shape), dt, kind="ExternalInput").ap()
        return dram_in[name]

    x_d = din("x", [T, D])
    cT_d = din("cT", [128, 8])
    posT_d = din("posT", [128, NT], I32)
    wada_d = din("w_ada", [D, 6 * D])
    bada_d = din("b_ada", [1, 6 * D])
    n1g_d = din("norm1_g", [1, D])
    win_d = din("w_in", [D, IN_COLS])
    ident_d = din("ident", [128, 128])
    out_d = nc.dram_tensor("out", [T, D], F32, kind="ExternalOutput").ap()
    dbg_d = {}
    for spec in dbg:
        name, shape = spec[0], spec[1]
        ddt = BF16 if (len(spec) > 2 and spec[2] == "bf16") else F32
        dbg_d[name] = nc.dram_tensor("dbg_" + name, list(shape), ddt, kind="ExternalOutput").ap()

    es = ExitStack()

    def sb(name, shape, dt=F32):
        return es.enter_context(nc.sbuf_tensor("s_" + name, list(shape), dt))

    def ps(name, shape, dt=F32):
        return es.enter_context(nc.psum_tensor(name, list(shape), dt))

    def DMA(out, in_, r=(), w=(), eng="sp"):
        return S.add(eng, lambda e: e.dma_start(out=out, in_=in_), reads=r, writes=w, dma=True)

    def ACT(out, in_, func, bias=0.0, scale=1.0, accum=None, r=(), w=()):
        if accum is None:
            return S.add("act", lambda e: e.activation(out, in_, func, bias=bias, scale=scale), reads=r, writes=w)
        return S.add("act", lambda e: e.activation(out, in_, func, bias=bias, scale=scale, accum_out=accum),
                     reads=r, writes=w)

    def TS(out, in0, s1, s2, op0, op1=None, r=(), w=(), eng="dve"):
        if op1 is None:
            return S.add(eng, lambda e: e.tensor_scalar(out, in0, s1, None, op0), reads=r, writes=w)
        return S.add(eng, lambda e: e.tensor_scalar(out, in0, s1, s2, op0, op1), reads=r, writes=w)

    def TT(out, in0, in1, op, r=(), w=(), eng="dve"):
        return S.add(eng, lambda e: e.tensor_tensor(out, in0, in1, op), reads=r, writes=w)

    def STT(out, in0, scalar, in1, op0, op1, r=(), w=()):
        return S.add("dve", lambda e: e.scalar_tensor_tensor(out, in0, scalar, in1, op0, op1), reads=r, writes=w)

    def CP(out, in_, r=(), w=(), eng="dve"):
        return S.add(eng, lambda e: e.tensor_copy(out, in_), reads=r, writes=w)

    def RECIP(out, in_, r=(), w=()):
        return S.add("dve", lambda e: e.reciprocal(out, in_), reads=r, writes=w)

    def MEMSET(ap, val, w=(), eng="pool"):
        return S.add(eng, lambda e: e.memset(ap, val), writes=w)

    def MM(out, lhsT, rhs, start, stop, r=(), w=()):
        return S.add("pe", lambda e: e.matmul(out, lhsT, rhs, start=start, stop=stop), reads=r, writes=w)

    def TR(out, in_, ident, r=(), w=()):
        return S.add("pe", lambda e: e.transpose(out, in_, ident), reads=r, writes=w)

    PB = [ps("pb%d" % i, [128, 512], F32) for i in range(8)]
    pb_ctr = [0]

    def next_pb():
        i = pb_ctr[0] % 8
        pb_ctr[0] += 1
        return PB[i], ("pb", i)

    ident_f = sb("ident_f", [128, 128])
    ident_b = sb("ident_b", [128, 128], BF16)
    DMA(ident_f[:], ident_d, w=["ident_f"])
    CP(ident_b[:], ident_f[:], r=["ident_f"], w=["ident_b"])
    ones_row = sb("ones_row", [1, 128])
    MEMSET(ones_row[:], 1.0, w=["ones_row"])

    bar_scr = sb("bar_scr", [1, 64])
    hT = sb("hT", [128, 8, T], BF16)
    modT = sb("modT", [128, 48])
    SH1, GM1, GT1, SH2, GM2, GT2 = 0, 8, 16, 24, 32, 40

    with ExitStack() as esA:
        def sbA(name, shape, dt=F32):
            return esA.enter_context(nc.sbuf_tensor("s_" + name, list(shape), dt))

        modrow = sbA("modrow", [1, 6 * D])
        cT = sbA("cT", [128, 8])
        cs = sbA("cs", [128, 8])
        DMA(cT[:], cT_d, w=["cT"])
        ACT(cs[:], cT[:], AF.Silu, r=["cT"], w=["cs"])
        bada = sbA("bada", [1, 6 * D])
        DMA(bada[:], bada_d, w=["bada"])
        n1g = sbA("n1g", [1, D])
        DMA(n1g[:], n1g_d, w=["n1g"])
        n2g = sbA("n2g", [1, D])
        DMA(n2g[:], din("norm2_g", [1, D]), w=["n2g"])
        wa = [sbA("wa%d" % i, [128, 8, 512]) for i in range(2)]
        wada_v = wada_d.rearrange("(k p) n -> p k n", p=128)
        xs = [sbA("xs%d" % i, [128, D]) for i in range(3)]
        junk = sbA("junk", [128, D], BF16)
        xn = [sbA("xn%d" % i, [128, D], BF16) for i in range(2)]
        ss = sbA("ss", [128, NT])
        rstd = sbA("rstd", [128, NT])

        def x_tile(i):
            xb = xs[i % 3]
            xk = ("xs", i % 3)
            DMA(xb[:], x_d[128 * i:128 * (i + 1), :], w=[xk], eng="pool")
            ACT(junk[:], xb[:], AF.Square, accum=ss[:, i:i + 1], r=[xk], w=["junk", ("ss", i)])
            TS(rstd[:, i:i + 1], ss[:, i:i + 1], 1.0 / D, EPS, ALU.mult, ALU.add, r=[("ss", i)], w=[("rstd", i)])
            ACT(rstd[:, i:i + 1], rstd[:, i:i + 1], AF.Sqrt, r=[("rstd", i)], w=[("rstd", i)])
            RECIP(rstd[:, i:i + 1], rstd[:, i:i + 1], r=[("rstd", i)], w=[("rstd", i)])
            xnb = xn[i % 2]
            nk = ("xn", i % 2)
            TS(xnb[:], xb[:], rstd[:, i:i + 1], None, ALU.mult, r=[xk, ("rstd", i)], w=[nk])
            pb, pk = next_pb()
            pbb = pb[:].bitcast(BF16)
            for c in range(8):
                TR(pbb[:, 128 * c:128 * (c + 1)], xnb[:, 128 * c:128 * (c + 1)], ident_b[:],
                   r=[nk, "ident_b"], w=[pk])
            ACT(hT[:, :, 128 * i:128 * (i + 1)], pbb[:, :].rearrange("p (c t) -> p c t", c=8), AF.Copy,
                r=[pk], w=[("hT", i)])

        xt = 0
        for blk in range(12):
            wb = wa[blk % 2]
            key = ("wa", blk % 2)
            DMA(wb[:], wada_v[:, :, 512 * blk:512 * (blk + 1)], w=[key])
            for _ in range(2 if blk % 3 == 0 else 1):
                if xt < NT:
                    x_tile(xt)
                    xt += 1
            pb, pk = next_pb()
            for k in range(8):
                MM(pb[0:1, :], cs[:, k:k + 1], wb[:, k, :], k == 0, k == 7, r=[key, "cs"], w=[pk])
            TT(modrow[0:1, 512 * blk:512 * (blk + 1)], pb[0:1, :], bada[0:1, 512 * blk:512 * (blk + 1)],
               ALU.add, r=[pk, "bada"], w=[("modrow", blk)])
        while xt < NT:
            x_tile(xt)
            xt += 1
        mr_all = [("modrow", b) for b in range(12)]
        for (off, gsrc, gkey) in ((D, n1g, "n1g"), (4 * D, n2g, "n2g")):
            STT(modrow[0:1, off:off + D], modrow[0:1, off:off + D], 1.0, gsrc[0:1, :], ALU.add, ALU.mult,
                r=mr_all + [gkey], w=mr_all)
        one11 = ones_row[0:1, 0:1]
        pb, pk = next_pb()
        for j in range(48):
            MM(pb[:, j:j + 1], modrow[0:1, 128 * j:128 * (j + 1)], one11, True, True,
               r=mr_all + ["ones_row"], w=[pk])
        CP(modT[:], pb[:, 0:48], r=[pk], w=["modT"])
        hT_keys = [("hT", i) for i in range(NT)]
        for c in range(8):
            TS(hT[:, c, :], hT[:, c, :], modT[:, GM1 + c:GM1 + c + 1], modT[:, SH1 + c:SH1 + c + 1], ALU.mult, ALU.add,
               r=hT_keys + ["modT"], w=hT_keys)
        if "modrow" in dbg_d:
            DMA(dbg_d["modrow"], modrow[:], r=mr_all)
        if "hT" in dbg_d:
            hTf = sbA("hTf", [128, 8, T])
            CP(hTf[:], hT[:], r=[("hT", i) for i in range(NT)], w=["hTf"])
            DMA(dbg_d["hT"].rearrange("(c p) t -> p c t", p=128), hTf[:], r=["hTf"])

    bar_n = [0]

    def barrier():
        n = bar_n[0]
        bar_n[0] += 1
        o1 = ones_row[0:1, 0:1]
        pb, pk = next_pb()
        MM(pb[0:1, 0:1], o1, o1, True, True, r=["ones_row"], w=[pk, ("bar", "pe", n)])
        ACT(bar_scr[0:1, 0:1], o1, AF.Copy, r=["ones_row"], w=[("bar", "act", n)])
        CP(bar_scr[0:1, 8:9], o1, r=["ones_row"], w=[("bar", "dve", n)])
        CP(bar_scr[0:1, 16:17], o1, r=["ones_row"], w=[("bar", "pool", n)], eng="pool")
        allb = [("bar", e, n) for e in ("pe", "act", "dve", "pool")]
        pb, pk = next_pb()
        MM(pb[0:1, 0:1], o1, o1, True, True, r=allb, w=[pk])
        ACT(bar_scr[0:1, 1:2], o1, AF.Copy, r=allb, w=[("bar2", "act", n)])
        CP(bar_scr[0:1, 9:10], o1, r=allb, w=[("bar2", "dve", n)])
        CP(bar_scr[0:1, 17:18], o1, r=allb, w=[("bar2", "pool", n)], eng="pool")
        DMA(bar_scr[0:1, 32:40], ones_row[0:1, 0:8], r=allb, w=["bar_sp"])

    mixT = sb("mixT", [128, 8, T], BF16)
    hT_all = [("hT", i) for i in range(NT)]
    TWO_PI = 6.283185307179586
    HALF_PI = 1.5707963267948966
    win_v = win_d.rearrange("(k p) n -> p k n", p=128)

    barrier()
    with ExitStack() as esB:
        def sbB(name, shape, dt=F32):
            return esB.enter_context(nc.sbuf_tensor("s_" + name, list(shape), dt))

        tri_f = sbB("tri_f", [128, 128])
        tri_b = sbB("tri_b", [128, 128], BF16)
        anti_b = sbB("anti_b", [128, 128], BF16)
        am = sbB("am", [128, NT, 32])
        Mov = sbB("Mov", [128, 32])
        Qaug = [sbB("Qaug%d" % g, [128, NT, 4, 128], BF16) for g in range(2)]
        Ksaug = [sbB("Ksaug%d" % g, [128, T], BF16) for g in range(2)]
        KwT = [sbB("KwT%d" % g, [128, T], BF16) for g in range(2)]
        Vs = sbB("Vs", [128, NT, 2, 65], BF16)
        Vw = sbB("Vw", [128, NT, 2, 65], BF16)
        gsig = sbB("gsig", [128, NT, 24])
        kcmpT = [sbB("kcmpT%d" % g, [64, 128], BF16) for g in range(2)]
        vcmp = [sbB("vcmp%d" % g, [128, 64]) for g in range(2)]
        esB1 = ExitStack()

        def sbB1(name, shape, dt=F32):
            return esB1.enter_context(nc.sbuf_tensor("s_" + name, list(shape), dt))

        KcT = [sbB1("KcT%d" % g, [64, T], BF16) for g in range(2)]
        VcT = [sbB1("VcT%d" % g, [64, T], BF16) for g in range(2)]
        esB1a = ExitStack()

        def sbB1a(name, shape, dt=F32):
            return esB1a.enter_context(nc.sbuf_tensor("s_" + name, list(shape), dt))

        DMA(tri_f[:], din("tri", [128, 128]), w=["tri_f"])
        CP(tri_b[:], tri_f[:], r=["tri_f"], w=["tri_b"])
        TS(anti_b[:], tri_f[:], -1.0, 1.0, ALU.mult, ALU.add, r=["tri_f"], w=["anti_b"])
        DMA(am[:], din("am", [T, 32]).rearrange("(n p) j -> p n j", p=128), w=["am"])
        DMA(Mov[0:127, :], din("Mov", [127, 32]), w=["Mov"])
        invf = sbB1a("invf", [128, 8])
        DMA(invf[:], din("invf", [128, 8]), w=["invf"])
        qg8 = sbB1a("qg8", [128, 64])
        DMA(qg8[:], din("q_norm_g", [1, 64]).partition_broadcast(128), w=["qg8"])
        TS(qg8[:], qg8[:], 0.125, None, ALU.mult, r=["qg8"], w=["qg8"])
        kg = sbB1a("kg", [128, 3, 64])
        DMA(kg[:].rearrange("p a d -> p (a d)"), din("k_norm_g", [1, 192]).partition_broadcast(128), w=["kg"])

        posi = sbB1a("posi", [128, NT], I32)
        posf = sbB1a("posf", [128, NT])
        DMA(posi[:], posT_d, w=["posi"])
        CP(posf[:], posi[:], r=["posi"], w=["posf"])
        ang = sbB1a("ang", [128, NT, 8])
        TT(ang[:], posf[:].unsqueeze(2).to_broadcast([128, NT, 8]), invf[:].unsqueeze(1).to_broadcast([128, NT, 8]),
           ALU.mult, r=["posf", "invf"], w=["ang"])
        sinT = sbB1a("sinT", [128, NT, 8])
        cosT = sbB1a("cosT", [128, NT, 8])
        rr_t = sbB1a("rr_t", [128, NT, 8])
        rr_i = sbB1a("rr_i", [128, NT, 8], I32)
        rr_k = sbB1a("rr_k", [128, NT, 8])
        for (dst, shift, key) in ((sinT, 0.0, "sinT"), (cosT, HALF_PI, "cosT")):
            TS(rr_t[:], ang[:], shift, 1.0 / TWO_PI, ALU.add, ALU.mult, r=["ang"], w=["rr_t"])
            CP(rr_i[:], rr_t[:], r=["rr_t"], w=["rr_i"])
            CP(rr_k[:], rr_i[:], r=["rr_i"], w=["rr_k"])
            TS(rr_t[:], ang[:], shift, None, ALU.add, r=["ang"], w=["rr_t"])
            STT(rr_t[:], rr_k[:], -TWO_PI, rr_t[:], ALU.mult, ALU.add, r=["rr_k", "rr_t"], w=["rr_t"])
            TS(rr_k[:], rr_t[:], 3.141592653589793, None, ALU.is_gt, r=["rr_t"], w=["rr_k"])
            STT(rr_t[:], rr_k[:], -TWO_PI, rr_t[:], ALU.mult, ALU.add, r=["rr_k", "rr_t"], w=["rr_t"])
            TS(rr_k[:], rr_t[:], -3.141592653589793, None, ALU.is_lt, r=["rr_t"], w=["rr_k"])
            STT(rr_t[:], rr_k[:], TWO_PI, rr_t[:], ALU.mult, ALU.add, r=["rr_k", "rr_t"], w=["rr_t"])
            TS(rr_t[:], rr_t[:], -3.1415925, 3.1415925, ALU.max, ALU.min, r=["rr_t"], w=["rr_t"])
            ACT(dst[:], rr_t[:], AF.Sin, r=["rr_t"], w=[key])

        wq = sbB1a("wq", [128, 8, 512], BF16)
        w1b = sbB1a("w1b", [128, 8, 512], BF16)
        w2b = sbB1a("w2b", [128, 8, 280], BF16)
        DMA(wq[:], win_v[:, :, 2048:2560], w=["wq"], eng="pool")
        DMA(w1b[:], win_v[:, :, 2560:3072], w=["w1b"], eng="pool")
        DMA(w2b[:], win_v[:, :, 3072:3352], w=["w2b"], eng="pool")

        E_d = din("Eblk", [32, T])
        for g in range(2):
            MEMSET(Qaug[g][64:128, :, :, :], 0.0, w=[("Qm", g, i) for i in range(NT)])
            MEMSET(Ksaug[g][64:128, :], 0.0, w=[("KsZ", g), ("KsE", g)])
            MEMSET(KwT[g][64:128, :], 0.0, w=[("KwZ", g)])
            DMA(Ksaug[g][64:96, :], E_d, w=[("KsE", g)], eng="pool")
        MEMSET(Vs[:, :, :, 64:65], 1.0, w=["Vs1"])
        MEMSET(Vw[:, :, :, 64:65], 1.0, w=["Vw1"])

        qf = [sbB1a("qf%d" % i, [128, 8, 64]) for i in range(2)]
        kf = [sbB1a("kf%d" % i, [128, 6, 64]) for i in range(2)]
        sqt2 = [sbB1a("sqt%d" % j, [128, 8, 64]) for j in range(4)]
        ssq2 = [sbB1a("ssq%d" % j, [128, 8]) for j in range(4)]
        rt2 = [[sbB1a("rt%d_%d" % (j, i), [128, 8, 8]) for i in range(4)] for j in range(4)]
        qb = [sbB1a("qb%d" % i, [128, 8, 64], BF16) for i in range(2)]
        kb = [sbB1a("kb%d" % i, [128, 6, 64], BF16) for i in range(2)]
        epsb = sbB1a("epsb", [128, 1])
        MEMSET(epsb[:], EPS, w=["epsb"])

        def normrope(src, skey, nh, gains_bc, gkey, dst, dkey, i, par):
            sqt, ssq, rt = sqt2[par], ssq2[par], rt2[par]
            ksq, kss = ("sqt", par), ("ssq", par)
            krt = [("rt", par, j) for j in range(4)]
            ACT(sqt[:, 0:nh, :], src, AF.Square, r=[skey], w=[ksq])
            yield
            S.add("dve", lambda e: e.tensor_reduce(ssq[:, 0:nh], sqt[:, 0:nh, :], AX.X, ALU.add),
                  reads=[ksq], writes=[kss])
            yield
            ACT(ssq[:, 0:nh], ssq[:, 0:nh], AF.Ln, bias=epsb[:, 0:1], scale=1.0 / 64, r=[kss, "epsb"], w=[kss])
            ACT(ssq[:, 0:nh], ssq[:, 0:nh], AF.Exp, scale=-0.5, r=[kss], w=[kss])
            yield
            TT(src, src, ssq[:, 0:nh].unsqueeze(2).to_broadcast([128, nh, 64]), ALU.mult, r=[skey, kss], w=[skey])
            yield
            TT(src, src, gains_bc, ALU.mult, r=[skey, gkey], w=[skey])
            yield
            cb = cosT[:, i, :].unsqueeze(1).to_broadcast([128, nh, 8])
            sbb = sinT[:, i, :].unsqueeze(1).to_broadcast([128, nh, 8])
            x1 = src[:, :, 0:8]
            x2 = src[:, :, 8:16]
            ACT(dst, src, AF.Copy, r=[skey], w=[dkey])
            TT(rt[0][:, 0:nh, :], x1, cb, ALU.mult, r=[skey, "cosT"], w=[krt[0]])
            yield
            TT(rt[1][:, 0:nh, :], x2, sbb, ALU.mult, r=[skey, "sinT"], w=[krt[1]])
            yield
            TT(rt[2][:, 0:nh, :], x2, cb, ALU.mult, r=[skey, "cosT"], w=[krt[2]])
            yield
            TT(rt[3][:, 0:nh, :], x1, sbb, ALU.mult, r=[skey, "sinT"], w=[krt[3]])
            yield
            TT(dst[:, :, 0:8], rt[0][:, 0:nh, :], rt[1][:, 0:nh, :], ALU.subtract, r=[krt[0], krt[1], dkey], w=[dkey])
            yield
            TT(dst[:, :, 8:16], rt[2][:, 0:nh, :], rt[3][:, 0:nh, :], ALU.add, r=[krt[2], krt[3], dkey], w=[dkey])
            yield

        qg_bc = qg8[:].unsqueeze(1).to_broadcast([128, 8, 64])
        kg6 = sbB1a("kg6", [128, 3, 2, 64])
        CP(kg6[:, :, 0, :], kg[:], r=["kg"], w=["kg6"])
        CP(kg6[:, :, 1, :], kg[:], r=["kg"], w=["kg6"])

        def tile_gen(i):
            par = i % 2
            tsl = slice(128 * i, 128 * (i + 1))
            pq, pqk = PB[4 * par + 0], ("pb", 4 * par + 0)
            p1, p1k = PB[4 * par + 1], ("pb", 4 * par + 1)
            p2, p2k = PB[4 * par + 2], ("pb", 4 * par + 2)
            for k in range(8):
                MM(pq[:, :], hT[:, k, tsl], wq[:, k, :], k == 0, k == 7, r=[("hT", i), "wq"], w=[pqk])
            for k in range(8):
                MM(p1[:, :], hT[:, k, tsl], w1b[:, k, :], k == 0, k == 7, r=[("hT", i), "w1b"], w=[p1k])
            for k in range(8):
                MM(p2[:, 0:280], hT[:, k, tsl], w2b[:, k, :], k == 0, k == 7, r=[("hT", i), "w2b"], w=[p2k])
            yield
            qfb, qfk = qf[par], ("qf", par)
            kfb, kfk = kf[par], ("kf", par)
            ACT(qfb[:].rearrange("p h d -> p (h d)"), pq[:, :], AF.Copy, r=[pqk], w=[qfk])
            ACT(kfb[:, 0:2, :].rearrange("p h d -> p (h d)"), p1[:, 0:128], AF.Copy, r=[p1k], w=[kfk])
            ACT(kfb[:, 2:4, :].rearrange("p h d -> p (h d)"), p1[:, 256:384], AF.Copy, r=[p1k], w=[kfk])
            ACT(kfb[:, 4:6, :].rearrange("p h d -> p (h d)"), p2[:, 0:128], AF.Copy, r=[p2k], w=[kfk])
            yield
            ACT(Vs[:, i, :, 0:64], p1[:, 384:512].rearrange("p (g d) -> p g d", g=2), AF.Copy, r=[p1k], w=[("Vs", i)])
            ACT(Vw[:, i, :, 0:64], p2[:, 128:256].rearrange("p (g d) -> p g d", g=2), AF.Copy, r=[p2k], w=[("Vw", i)])
            ACT(gsig[:, i, :], p2[:, 256:280], AF.Exp, scale=-1.0, r=[p2k], w=[("gsig", i)])
            yield
            TS(gsig[:, i, :], gsig[:, i, :], 1.0, None, ALU.add, r=[("gsig", i)], w=[("gsig", i)])
            RECIP(gsig[:, i, :], gsig[:, i, :], r=[("gsig", i)], w=[("gsig", i)])
            yield
            qbb, qbk = qb[par], ("qb", par)
            kbb, kbk = kb[par], ("kb", par)
            g1 = normrope(qfb[:], qfk, 8, qg_bc, "qg8", qbb[:], qbk, i, 2 * par)
            g2 = normrope(kfb[:], kfk, 6, kg6[:].rearrange("p a g d -> p (a g) d"), "kg6", kbb[:], kbk, i, 2 * par + 1)
            live = [g1, g2]
            while live:
                for g_ in list(live):
                    if next(g_, "done") == "done":
                        live.remove(g_)
                yield
            ptb = pq[:].bitcast(BF16)
            for h in range(8):
                TR(ptb[0:64, 128 * h:128 * (h + 1)], qbb[:, h, :], ident_b[:], r=[qbk, "ident_b"], w=[pqk])
            pt2b = p1[:].bitcast(BF16)
            for s_ in range(6):
                TR(pt2b[0:64, 128 * s_:128 * (s_ + 1)], kbb[:, s_, :], ident_b[:], r=[kbk, "ident_b"], w=[p1k])
            yield
            for g in range(2):
                ACT(Qaug[g][0:64, i, :, :], ptb[0:64, 512 * g:512 * (g + 1)].rearrange("p (h t) -> p h t", h=4),
                    AF.Copy, r=[pqk], w=[("Q", g, i)])
            yield
            for g in range(2):
                CP(KcT[g][:, tsl], pt2b[0:64, 128 * g:128 * (g + 1)], r=[p1k], w=[("KcT", g, i)])
                CP(Ksaug[g][0:64, tsl], pt2b[0:64, 128 * (2 + g):128 * (3 + g)], r=[p1k], w=[("Ks", g, i)])
                CP(KwT[g][0:64, tsl], pt2b[0:64, 128 * (4 + g):128 * (5 + g)], r=[p1k], w=[("Kw", g, i)])
            yield

        nxt = 0
        active = []
        while nxt < NT or active:
            while len(active) < 2 and nxt < NT:
                active.append(tile_gen(nxt))
                nxt += 1
            for g_ in list(active):
                if next(g_, "done") == "done":
                    active.remove(g_)
        for blk in range(4):
            bsl = slice(512 * blk, 512 * (blk + 1))
            for g in range(2):
                pv, pvk = next_pb()
                for k in range(8):
                    MM(pv[0:64, :], w1b[:, k, 128 + 64 * g:128 + 64 * (g + 1)], hT[:, k, bsl], k == 0, k == 7,
                       r=hT_all + ["w1b"], w=[pvk])
                ACT(VcT[g][:, bsl], pv[0:64, :], AF.Copy, r=[pvk], w=[("VcT", g)])

        def dump(name, ap, keys):
            if name in dbg_d:
                DMA(dbg_d[name], ap, r=keys)

        dump("Q0", Qaug[0][0:64, :, :, :], [("Q", 0, i) for i in range(NT)])
        dump("Ks1", Ksaug[1][0:96, :], [("Ks", 1, i) for i in range(NT)] + [("KsE", 1)])
        dump("Kw0", KwT[0][0:64, :], [("Kw", 0, i) for i in range(NT)])
        dump("Vc1", VcT[1][:], [("VcT", 1)])
        dump("Vs", Vs[:], [("Vs", i) for i in range(NT)] + ["Vs1"])
        dump("gsig", gsig[:], [("gsig", i) for i in range(NT)])
        esB1a.close()
        barrier()
        w1c = [sbB1("w1c%d" % kv, [64, 32, 256], BF16) for kv in range(2)]
        w2c = [sbB1("w2c%d" % kv, [128, 2, 64], BF16) for kv in range(2)]
        wc1_d = din("w_cmp1", [2, 2048, 256])
        wc2_d = din("w_cmp2", [2, 256, 64])
        for kv in range(2):
            DMA(w1c[kv][:], wc1_d[kv].rearrange("(l d) c -> d l c", d=64), w=[("w1c", kv)], eng="pool")
            DMA(w2c[kv][:], wc2_d[kv].rearrange("(cc c) d -> c cc d", c=128), w=[("w2c", kv)], eng="pool")
        peTf = sbB1("peTf", [64, 2, 32])
        peTb = sbB1("peTb", [64, 2, 32], BF16)
        DMA(peTf[:], din("peT", [64, 2, 32]), w=["peTf"])
        CP(peTb[:], peTf[:], r=["peTf"], w=["peTb"])
        onesb = sbB1("onesb", [1, 128], BF16)
        MEMSET(onesb[:], 1.0, w=["onesb"])
        brow = sbB1("brow", [1, 512], BF16)
        pbi, pbik = next_pb()
        for kv in range(2):
            for l in range(32):
                MM(pbi[0:1, 256 * kv:256 * (kv + 1)], peTb[:, kv, l:l + 1], w1c[kv][:, l, :], l == 0, l == 31,
                   r=[("w1c", kv), "peTb"], w=[pbik])
        CP(brow[:], pbi[0:1, 0:512], r=[pbik], w=["brow"])
        hc = [sbB1("hc%d" % j, [128, 256], BF16) for j in range(2)]
        hcT = [sbB1("hcT%d" % j, [128, 2, 128], BF16) for j in range(2)]
        KcT_keys = lambda g: [("KcT", g, i) for i in range(NT)]
        cnt = 0
        for g in range(2):
            for kv in range(2):
                src = KcT[g] if kv == 0 else VcT[g]
                skeys = KcT_keys(g) if kv == 0 else [("VcT", g)]
                hcb, hck = hc[cnt % 2], ("hc", cnt % 2)
                htb, htk = hcT[cnt % 2], ("hcT", cnt % 2)
                cnt += 1
                ph, phk = next_pb()
                for l in range(32):
                    MM(ph[0:127, 0:256], src[:, l:l + 16 * 126 + 1:16], w1c[kv][:, l, :], l == 0, False,
                       r=skeys + [("w1c", kv)], w=[phk])
                MM(ph[0:127, 0:256], onesb[0:1, 0:127], brow[0:1, 256 * kv:256 * (kv + 1)], False, True,
                   r=["onesb", "brow"], w=[phk])
                ACT(hcb[0:127, :], ph[0:127, 0:256], AF.Silu, r=[phk], w=[hck])
                pt, ptk = next_pb()
                ptb = pt[:].bitcast(BF16)
                for cc in range(2):
                    TR(ptb[:, 128 * cc:128 * cc + 127], hcb[0:127, 128 * cc:128 * (cc + 1)], ident_b[0:127, 0:127],
                       r=[hck, "ident_b"], w=[ptk])
                CP(htb[:, :, 0:127], ptb[:, 0:256].rearrange("p (c n) -> p c n", c=2)[:, :, 0:127], r=[ptk], w=[htk])
                po, pok = next_pb()
                if kv == 0:
                    for cc in range(2):
                        MM(po[0:64, 0:127], w2c[0][:, cc, :], htb[:, cc, 0:127], cc == 0, cc == 1,
                           r=[htk, ("w2c", 0)], w=[pok])
                    CP(kcmpT[g][:, 0:127], po[0:64, 0:127], r=[pok], w=[("kcmpT", g)])
                else:
                    for cc in range(2):
                        MM(po[0:127, 0:64], htb[:, cc, 0:127], w2c[1][:, cc, :], cc == 0, cc == 1,
                           r=[htk, ("w2c", 1)], w=[pok])
                    CP(vcmp[g][0:127, :], po[0:127, 0:64], r=[pok], w=[("vcmp", g)])
        dump("kcmpT1", kcmpT[1][:, 0:127], [("kcmpT", 1)])
        dump("vcmp0", vcmp[0][0:127, :], [("vcmp", 0)])
        esB1.close()
        barrier()
        esB3 = ExitStack()

        def sbB3(name, shape, dt=F32):
            return esB3.enter_context(nc.sbuf_tensor("s_" + name, list(shape), dt))

        cv_d = din("cvalid", [T, 128])
        cvt = [sbB3("cvt%d" % j, [128, 128]) for j in range(3)]
        esb2 = [sbB3("esb%d" % j, [128, 4, 128]) for j in range(2)]
        mx2 = [sbB3("mx%d" % j, [128, 4]) for j in range(2)]
        sm2 = [sbB3("sm%d" % j, [128, 4]) for j in range(2)]
        pTs2 = [sbB3("pTs%d" % j, [128, 4, 128], BF16) for j in range(2)]
        pbf2 = [sbB3("pbf%d" % j, [128, 4, 128], BF16) for j in range(2)]
        psm2 = [sbB3("psm%d" % j, [128, 128]) for j in range(2)]
        psT2 = [sbB3("psT%d" % j, [128, 128]) for j in range(2)]
        vcmpb = [sbB3("vcmpb%d" % g, [128, 64], BF16) for g in range(2)]
        for g in range(2):
            CP(vcmpb[g][0:127, :], vcmp[g][0:127, :], r=[("vcmp", g)], w=[("vcmpb", g)])
        impv2 = [sbB3("impv%d" % j, [128, 32]) for j in range(2)]
        cmp32 = [sbB3("cmp3_%d" % j, [128, 32, 32]) for j in range(2)]
        rank2 = [sbB3("rank%d" % j, [128, 32]) for j in range(2)]
        negm2 = [sbB3("negm%d" % j, [128, 96], BF16) for j in range(2)]
        for j in range(2):
            MEMSET(negm2[j][:], 0.0, w=[("negm", j)])
        ptS = [sbB3("ptS%d" % j, [128, 4, 128], BF16) for j in range(6)]
        ocs = [sbB3("ocs%d" % j, [128, 4, 64]) for j in range(3)]
        acc = sbB3("acc", [128, 4, 64])
        tmpa = sbB3("tmpa", [128, 4, 64])
        tmpb = sbB3("tmpb", [128, 4, 64])
        rd = sbB3("rd", [128, 2, 4])
        Ocp = [[sbB3("Ocp%d_%d" % (a_, b_), [128, 260]) for b_ in range(2)] for a_ in range(2)]
        onsa = [sbB3("onsa%d" % j, [128, 8, 64], BF16) for j in range(2)]
        st_ctr = [0]
        cm_ctr = [0]

        def st_pb():
            j = 2 + st_ctr[0] % 4
            st_ctr[0] += 1
            return PB[j], ("pb", j)

        def cm_pb():
            j = 6 + cm_ctr[0] % 2
            cm_ctr[0] += 1
            return PB[j], ("pb", j)

        torder = list(range(NT))
        tpos = {t_: p_ for p_, t_ in enumerate(torder)}
        steps = [(i, g) for i in torder for g in range(2)]

        def cmp_gen(i, g, step):
            sp_ = step % 2
            esb, mx, sm, pTs = esb2[sp_], mx2[sp_], sm2[sp_], pTs2[sp_]
            pbf, psm, psT = pbf2[sp_], psm2[sp_], psT2[sp_]
            kB, kPS, kPT = ("pbf", sp_), ("psm", sp_), ("psT", sp_)
            impv, cmp3, rank, negm = impv2[sp_], cmp32[sp_], rank2[sp_], negm2[sp_]
            kE, kM, kS, kP = ("esb", sp_), ("mx", sp_), ("sm", sp_), ("pTs", sp_)
            kI, kC, kR, kN = ("impv", sp_), ("cmp3", sp_), ("rank", sp_), ("negm", sp_)
            cvb = cvt[tpos[i] % 3]
            cvk = ("cvt", tpos[i] % 3)
            if g == 0:
                DMA(cvb[:], cv_d[128 * i:128 * (i + 1), :], w=[cvk])
            ocb = ocs[step % 3]
            ock = ("ocs", step % 3)
            qkeys = [("Q", g, i)]
            bank, bk = PB[6 + sp_], ("pb", 6 + sp_)
            for hh in range(4):
                MM(bank[:, 128 * hh:128 * hh + 127], Qaug[g][0:64, i, hh, :], kcmpT[g][:, 0:127], True, True,
                   r=qkeys + [("kcmpT", g)], w=[bk])
            yield
            pscv = bank[:, :].rearrange("p (h n) -> p h n", h=4)[:, :, 0:127]
            S.add("dve", lambda e, a=pscv: e.tensor_reduce(mx[:], a, AX.X, ALU.max), reads=[bk], writes=[kM])
            TS(mx[:], mx[:], -1.0, None, ALU.mult, r=[kM], w=[kM])
            yield
            for hh in range(4):
                ACT(esb[:, hh, 0:127], bank[:, 128 * hh:128 * hh + 127], AF.Exp, bias=mx[:, hh:hh + 1],
                    r=[bk, kM], w=[kE])
            yield
            ev = esb[:, :, 0:127]
            TT(ev, ev, cvb[:, 0:127].unsqueeze(1).to_broadcast([128, 4, 127]), ALU.mult, r=[kE, cvk], w=[kE])
            S.add("dve", lambda e, a=ev: e.tensor_reduce(sm[:], a, AX.X, ALU.add), reads=[kE], writes=[kS])
            yield
            TS(sm[:], sm[:], 1e-30, None, ALU.max, r=[kS], w=[kS])
            RECIP(sm[:], sm[:], r=[kS], w=[kS])
            TT(ev, ev, sm[:].unsqueeze(2).to_broadcast([128, 4, 127]), ALU.mult, r=[kE, kS], w=[kE])
            yield
            yield
            ACT(pbf[:, :, 0:127], ev, AF.Copy, r=[kE], w=[kB])
            if i >= 8:
                S.add("dve", lambda e, a=ev: e.tensor_reduce(psm[:, 0:127], a.rearrange("p h n -> p n h"), AX.X, ALU.add),
                      reads=[kE], writes=[kPS])
            yield
            yield
            yield
            bankb = bank[:].bitcast(BF16)
            for hh in range(4):
                TR(bankb[0:127, 128 * hh:128 * (hh + 1)], pbf[:, hh, 0:127], ident_b[:], r=[kB, "ident_b"], w=[bk])
            yield
            CP(pTs[0:127, :, :], bankb[0:127, 0:512].rearrange("p (h t) -> p h t", h=4), r=[bk], w=[kP])
            yield
            for hh in range(4):
                MM(bank[:, 64 * hh:64 * (hh + 1)], pTs[0:127, hh, :], vcmpb[g][0:127, :], True, True,
                   r=[kP, ("vcmpb", g)], w=[bk])
            yield
            ACT(ocb[:].rearrange("p h d -> p (h d)"), bank[:, 0:256], AF.Copy, r=[bk], w=[ock])
            if i >= 8:
                yield
                TR(bank[0:127, 0:128], psm[:, 0:127], ident_f[:], r=[kPS, "ident_f"], w=[bk])
                yield
                CP(psT[0:127, :], bank[0:127, 0:128], r=[bk], w=[kPT])
                yield
                MM(bank[:, 0:32], psT[0:127, :], Mov[0:127, :], True, True, r=[kPT, "Mov"], w=[bk])
                yield
                TT(impv[:], bank[:, 0:32], am[:, i, :], ALU.add, r=[bk, "am"], w=[kI])
                TT(cmp3[:], impv[:].unsqueeze(1).to_broadcast([128, 32, 32]),
                   impv[:].unsqueeze(2).to_broadcast([128, 32, 32]), ALU.is_gt, r=[kI], w=[kC])
                yield
                S.add("dve", lambda e: e.tensor_reduce(rank[:], cmp3[:], AX.X, ALU.add), reads=[kC], writes=[kR])
                TS(negm[:, 64:96], rank[:], 16.0, -30000.0, ALU.is_ge, ALU.mult, r=[kR], w=[kN])
                yield
                yield
                yield
                yield
                pmb = bank[:].bitcast(BF16)
                TR(pmb[0:96, 0:128], negm[:, 0:96], ident_b[:], r=[kN, "ident_b"], w=[bk])
                yield
                CP(Qaug[g][64:96, i, :, :], pmb[64:96, 0:128].unsqueeze(1).to_broadcast([32, 4, 128]),
                   r=[bk], w=[("Qm", g, i)])

        pt_ctr = [0]

        def attn_S(it):
            (n, i, g, br, kj, kjs) = it
            ksl = slice(128 * kj, 128 * (kj + 1))
            qkeys = [("Q", g, i)]
            pS, pSk = st_pb()
            if br == "win":
                MM(pS[:, :], KwT[g][0:128, ksl], Qaug[g][0:128, i, :, :].rearrange("p h t -> p (h t)"),
                   True, True, r=qkeys + [("Qm", g, i), ("Kw", g, kj), ("KwZ", g)], w=[pSk])
            else:
                MM(pS[:, :], Ksaug[g][0:128, ksl], Qaug[g][0:128, i, :, :].rearrange("p h t -> p (h t)"),
                   True, True, r=qkeys + [("Qm", g, i), ("Ks", g, kj), ("KsE", g), ("KsZ", g)], w=[pSk])
            pb_ = ptS[pt_ctr[0] % 6]
            pbk_ = ("ptS", pt_ctr[0] % 6)
            pt_ctr[0] += 1
            ACT(pb_[:].rearrange("p h t -> p (h t)"), pS[:, :], AF.Exp, r=[pSk], w=[pbk_])
            if kj == i:
                TT(pb_[:], pb_[:], tri_b[:].unsqueeze(1).to_broadcast([128, 4, 128]), ALU.mult,
                   r=[pbk_, "tri_b"], w=[pbk_])
            elif br == "win" and kj == i - 4:
                TT(pb_[:], pb_[:], anti_b[:].unsqueeze(1).to_broadcast([128, 4, 128]), ALU.mult,
                   r=[pbk_, "anti_b"], w=[pbk_])
            return (pb_, pbk_)

        def attn_PV(it, pbp):
            (n, i, g, br, kj, kjs) = it
            pb_, pbk_ = pbp
            Os, Osk = PB[0], ("pb", 0)
            Ow, Owk = PB[1], ("pb", 1)
            Osv = Os[:, 0:260].rearrange("p (h e) -> p h e", h=4)
            Owv = Ow[:, 0:260].rearrange("p (h e) -> p h e", h=4)
            if br == "win":
                O_, Ok_, V_, Vk_, vone = Owv, Owk, Vw, ("Vw", kj), "Vw1"
            else:
                O_, Ok_, V_, Vk_, vone = Osv, Osk, Vs, ("Vs", kj), "Vs1"
            for hh in range(4):
                MM(O_[:, hh, :], pb_[:, hh, :], V_[:, kj, g, :], (kj == kjs[0] and hh == 0), (kj == kjs[-1] and hh == 3),
                   r=[pbk_, Vk_, vone], w=[Ok_])

        items = []
        for n, (i, g) in enumerate(steps):
            for br in ("win", "slc"):
                kjs = list(range(max(0, i - 4), i + 1)) if br == "win" else list(range(0, i + 1))
                for kj in kjs:
                    items.append((n, i, g, br, kj, kjs))
        pend_tr = []

        def flush_tr():
            while pend_tr:
                (onb_, onk_, tsl_, i_) = pend_tr.pop(0)
                pto, ptok = st_pb()
                ptob = pto[:].bitcast(BF16)
                onf = onb_[:].rearrange("p h d -> p (h d)")
                for c in range(4):
                    TR(ptob[:, 128 * c:128 * (c + 1)], onf[:, 128 * c:128 * (c + 1)], ident_b[:], r=[onk_, "ident_b"],
                       w=[ptok])
                ACT(mixT[:, 4:8, tsl_], ptob[:, 0:512].rearrange("p (c t) -> p c t", c=4), AF.Copy, r=[ptok],
                    w=[("mixT", 1, i_)])

        def mk_gen(n_):
            if n_ < len(steps):
                return cmp_gen(steps[n_][0], steps[n_][1], n_)
            return None

        for _ in cmp_gen(steps[0][0], steps[0][1], 0):
            pass
        gen_a = mk_gen(1)
        gen_b = mk_gen(2)
        LOOK = 3
        pendq = [attn_S(items[j_]) for j_ in range(min(LOOK, len(items)))]
        idx = 0
        tick = 0
        for n, (i, g) in enumerate(steps):
            tsl = slice(128 * i, 128 * (i + 1))
            onb = onsa[tpos[i] % 2]
            onk = ("onsa", tpos[i] % 2)
            n_in_step = 0
            while idx < len(items) and items[idx][0] == n:
                cur = items[idx]
                curp = pendq.pop(0)
                last_of_step = (idx + 1 >= len(items)) or (items[idx + 1][0] != n)
                if last_of_step and gen_a is not None:
                    for _ in gen_a:
                        pass
                    gen_a = None
                if idx + LOOK < len(items):
                    nxt_it = items[idx + LOOK]
                    if nxt_it[0] != n and nxt_it[3] == "slc" and gen_a is not None and nxt_it[0] == n + 1:
                        for _ in gen_a:
                            pass
                        gen_a = None
                    pendq.append(attn_S(nxt_it))
                attn_PV(cur, curp)
                tick += 1
                n_in_step += 1
                if n_in_step == 3:
                    flush_tr()
                if gen_a is not None:
                    if next(gen_a, "done") == "done":
                        gen_a = None
                if gen_b is not None:
                    if next(gen_b, "done") == "done":
                        gen_b = None
                idx += 1
            if gen_a is not None:
                for _ in gen_a:
                    pass
            gen_a = gen_b
            gen_b = mk_gen(n + 3)
            ocb = ocs[n % 3]
            ock = ("ocs", n % 3)
            gs0 = gsig[:, i, 12 * g + 0:12 * g + 12:3]
            gs1 = gsig[:, i, 12 * g + 1:12 * g + 12:3]
            gs2 = gsig[:, i, 12 * g + 2:12 * g + 12:3]
            Osb, Owb = Ocp[n % 2][0], Ocp[n % 2][1]
            Osk, Owk = ("Ocp", n % 2, 0), ("Ocp", n % 2, 1)
            ACT(Osb[:], PB[0][:, 0:260], AF.Copy, r=[("pb", 0)], w=[Osk])
            ACT(Owb[:], PB[1][:, 0:260], AF.Copy, r=[("pb", 1)], w=[Owk])
            Osv = Osb[:].rearrange("p (h e) -> p h e", h=4)
            Owv = Owb[:].rearrange("p (h e) -> p h e", h=4)
            RECIP(rd[:, 0, :], Osv[:, :, 64], r=[Osk], w=["rd"])
            RECIP(rd[:, 1, :], Owv[:, :, 64], r=[Owk], w=["rd"])
            TT(rd[:, 0, :], rd[:, 0, :], gs1, ALU.mult, r=["rd", ("gsig", i)], w=["rd"])
            TT(rd[:, 1, :], rd[:, 1, :], gs2, ALU.mult, r=["rd", ("gsig", i)], w=["rd"])
            TT(acc[:], ocb[:], gs0.unsqueeze(2).to_broadcast([128, 4, 64]), ALU.mult, r=[ock, ("gsig", i)], w=["acc"])
            TT(tmpa[:], Osv[:, :, 0:64], rd[:, 0, :].unsqueeze(2).to_broadcast([128, 4, 64]), ALU.mult,
               r=[Osk, "rd"], w=["tmpa"])
            TT(tmpb[:], Owv[:, :, 0:64], rd[:, 1, :].unsqueeze(2).to_broadcast([128, 4, 64]), ALU.mult,
               r=[Owk, "rd"], w=["tmpb"])
            TT(acc[:], acc[:], tmpa[:], ALU.add, r=["acc", "tmpa"], w=["acc"], eng="pool")
            TT(onb[:, 4 * g:4 * g + 4, :], acc[:], tmpb[:], ALU.add, r=["acc", "tmpb"], w=[onk], eng="pool")
            if g == 1:
                pend_tr.append((onb, onk, tsl, i))
        flush_tr()
        dump("onsaT", mixT[:, 4:8, :], [("mixT", 1, i) for i in range(NT)])
        esB3.close()
    barrier()
    with ExitStack() as esC:
        def sbC(name, shape, dt=F32):
            return esC.enter_context(nc.sbuf_tensor("s_" + name, list(shape), dt))

        TQ = 512
        NTQ = 4
        NCQ = 8
        FQ = 4 * TQ
        lbl = sbC("lbl", [128, 2, 4])
        DMA(lbl[:], din("lbT", [128, 2, 4]), w=["lbl"])
        lb = sbC("lb", [128, 4])
        oml = sbC("oml", [128, 4])
        noml = sbC("noml", [128, 4])
        TT(lb[:], lbl[:, 0, :], lbl[:, 1, :], ALU.subtract, r=["lbl"], w=["lb"])
        ACT(lb[:], lb[:], AF.Sigmoid, r=["lb"], w=["lb"])
        TS(oml[:], lb[:], -1.0, 1.0, ALU.mult, ALU.add, r=["lb"], w=["oml"])
        TS(noml[:], oml[:], -1.0, None, ALU.mult, r=["oml"], w=["noml"])
        bmask = sbC("bmask", [128, 128], BF16)
        DMA(bmask[:], din("bmask", [128, 128]), w=["bmask"], eng="pool")
        ones_f = sbC("ones_f", [128, 128])
        MEMSET(ones_f[:], 1.0, w=["ones_f"])
        eps_c = sbC("eps_c", [128, 1])
        MEMSET(eps_c[:], EPS, w=["eps_c"])
        rmask = sbC("rmask", [128, FQ], BF16)
        MEMSET(rmask[:], 1.0, w=["rmask"])
        MEMSET(rmask[:, 0:FQ:64], 0.0, w=["rmask"])
        wh = sbC("wh", [128, 8, 4, 512], BF16)
        for j in range(4):
            DMA(wh[:, :, j, :], win_v[:, :, 512 * j:512 * (j + 1)], w=[("wh", j)], eng="pool")
        sg = sbC("sg", [128, 4, TQ])
        lf = sbC("lf", [128, 4, TQ])
        kk = sbC("kk", [128, 4, TQ])
        bb = sbC("bb", [128, 4, TQ])
        qq = sbC("qq", [128, 4, TQ])
        t1 = sbC("t1", [128, 4, TQ])
        qe = sbC("qe", [128, 4, TQ], BF16)
        qd = sbC("qd", [128, 4, TQ], BF16)
        kd = sbC("kd", [128, 4, TQ], BF16)
        kdz = sbC("kdz", [128, 4, TQ], BF16)
        gg = sbC("gg", [128, 4, TQ], BF16)
        keT = sbC("keT", [128, 4, TQ], BF16)
        v_tm = sbC("v_tm", [128, NTQ, 512], BF16)
        ke_tm = sbC("ke_tm", [128, 4, NTQ, 128], BF16)
        oT32 = sbC("oT32", [128, 4, TQ])
        bmid = sbC("bmid", [128, 4 * NCQ])
        blast = sbC("blast", [128, 4 * NCQ])
        dec = sbC("dec", [128, 4 * NCQ])
        S32 = [sbC("S32_%d" % h, [128, 128]) for h in range(4)]
        Sbf = [[sbC("Sbf%d_%d" % (h, j), [128, 128], BF16) for j in range(2)] for h in range(4)]
        Am = [[sbC("Am%d_%d" % (h, j), [128, 128], BF16) for j in range(2)] for h in range(4)]
        for h in range(4):
            MEMSET(S32[h][:], 0.0, w=[("S32", h)])
            MEMSET(Sbf[h][0][:], 0.0, w=[("Sbf", h, 0)])
        sidx = [0, 0, 0, 0]

        def fl(t_):
            return t_[:].rearrange("p h t -> p (h t)")

        def v3(t_):
            return t_[:].rearrange("p h (c t) -> p (h c) t", t=64)

        for qt in range(4):
            t0 = qt * TQ
            qsl = slice(t0, t0 + TQ)
            hkeys = [("hT", 4 * qt + j) for j in range(4)]
            for (j, dst, func, dkey) in ((1, sg, AF.Sigmoid, "sg"), (0, qq, AF.Silu, "qq"), (3, gg, AF.Silu, "gg")):
                for h in range(4):
                    pp, ppk = next_pb()
                    for k in range(8):
                        MM(pp[:, :], wh[:, k, j, 128 * h:128 * (h + 1)], hT[:, k, qsl], k == 0, k == 7,
                           r=hkeys + [("wh", j)], w=[ppk])
                    ACT(dst[:, h, :], pp[:, :], func, r=[ppk], w=[(dkey, h // 2)])
            for i in range(NTQ):
                tsl = slice(t0 + 128 * i, t0 + 128 * (i + 1))
                pp, ppk = next_pb()
                for k in range(8):
                    MM(pp[:, :], hT[:, k, tsl], wh[:, k, 2, :], k == 0, k == 7, r=hkeys + [("wh", 2)], w=[ppk])
                ACT(v_tm[:, i, :], pp[:, :], AF.Copy, r=[ppk], w=["v_tm"])
            def e_chain(pr):
                hs = slice(2 * pr, 2 * pr + 2)
                HF = 2 * TQ
                cs_ = slice(2 * NCQ * pr, 2 * NCQ * (pr + 1))

                def fl2(t_):
                    return t_[:, hs, :].rearrange("p h t -> p (h t)")

                def v32(t_):
                    return t_[:, hs, :].rearrange("p h (c t) -> p (h c) t", t=64)

                K = lambda nm: (nm, pr)
                for h in range(2 * pr, 2 * pr + 2):
                    ACT(lf[:, h, :], sg[:, h, :], AF.Ln, bias=lb[:, h:h + 1], scale=oml[:, h:h + 1],
                        r=[K("sg"), "lb", "oml"], w=[K("lf")])
                    TS(kk[:, h, :], sg[:, h, :], noml[:, h:h + 1], oml[:, h:h + 1], ALU.mult, ALU.add,
                       r=[K("sg"), "noml", "oml"], w=[K("kk")])
                    yield
                S.add("dve", lambda e: e.tensor_tensor_scan(fl2(bb), rmask[:, 0:HF], fl2(lf), 0.0, ALU.mult, ALU.add),
                      reads=["rmask", K("lf")], writes=[K("bb")])
                yield
                ACT(fl2(t1), fl2(bb), AF.Exp, r=[K("bb")], w=[K("t1")])
                CP(bmid[:, cs_], fl2(bb)[:, 31:HF:64], r=[K("bb")], w=[K("bmid")])
                yield
                CP(blast[:, cs_], fl2(bb)[:, 63:HF:64], r=[K("bb")], w=[K("blast")])
                yield
                TT(fl2(qe), fl2(qq), fl2(t1), ALU.mult, r=[K("qq"), K("t1")], w=[K("qe")])
                ACT(dec[:, cs_], blast[:, cs_], AF.Exp, r=[K("blast")], w=[K("dec")])
                yield
                TT(v32(sg), v32(bb), bmid[:, cs_].unsqueeze(2).to_broadcast([128, 2 * NCQ, 64]), ALU.subtract,
                   r=[K("bb"), K("bmid"), K("sg")], w=[K("sg")])
                yield
                TS(fl2(lf), fl2(sg), -1.0, 75.0, ALU.mult, ALU.min, r=[K("sg"), K("lf")], w=[K("lf")])
                yield
                TS(fl2(sg), fl2(sg), 75.0, None, ALU.min, r=[K("sg")], w=[K("sg")])
                ACT(fl2(lf), fl2(lf), AF.Exp, r=[K("lf")], w=[K("lf")])
                yield
                ACT(fl2(sg), fl2(sg), AF.Exp, r=[K("sg")], w=[K("sg")])
                TT(fl2(kd), fl2(kk), fl2(lf), ALU.mult, r=[K("kk"), K("lf")], w=[K("kd")])
                yield
                TT(v32(t1), blast[:, cs_].unsqueeze(2).to_broadcast([128, 2 * NCQ, 64]), v32(bb), ALU.subtract,
                   r=[K("bb"), K("blast"), K("t1")], w=[K("t1")])
                ACT(fl2(kdz), fl2(kd), AF.Copy, r=[K("kd")], w=[K("kdz")])
                yield
                TT(fl2(qd), fl2(qq), fl2(sg), ALU.mult, r=[K("qq"), K("sg")], w=[K("qd")])
                ACT(fl2(t1), fl2(t1), AF.Exp, r=[K("t1")], w=[K("t1")])
                MEMSET(v32(kdz)[:, :, 32:64], 0.0, w=[K("kdz")])
                yield
                TT(fl2(keT), fl2(kk), fl2(t1), ALU.mult, r=[K("kk"), K("t1")], w=[K("keT")])
                yield
                for h in range(2 * pr, 2 * pr + 2):
                    pp, ppk = PB[4 + h], ("pb", 4 + h)
                    ppb = pp[:].bitcast(BF16)
                    for i in range(NTQ):
                        TR(ppb[:, 128 * i:128 * (i + 1)], keT[:, h, 128 * i:128 * (i + 1)], ident_b[:],
                           r=[K("keT"), "ident_b"], w=[ppk])
                    yield
                    CP(ke_tm[:, h, :, :], ppb[:, 0:512].rearrange("p (j k) -> p j k", j=4), r=[ppk], w=[K("ke_tm")])
                    yield

            live = [e_chain(0), e_chain(1)]
            while live:
                for g_ in list(live):
                    if next(g_, "done") == "done":
                        live.remove(g_)
            for i in range(NTQ):
                tl = slice(128 * i, 128 * (i + 1))
                for h in range(4):
                    pa, pak = PB[4 + h], ("pb", 4 + h)
                    pav = pa[:, 0:128].rearrange("p (c a t) -> p c a t", c=2, a=2)
                    qdv = qd[:, h, tl].rearrange("p (c a t) -> p c a t", c=2, a=2)
                    for cc_ in range(2):
                        MM(pav[:, cc_, 0, :], kdz[:, h, tl], qdv[:, cc_, 0, :], True, True, r=[("kdz", h // 2), ("qd", h // 2)], w=[pak])
                        MM(pav[:, cc_, 1, :], kd[:, h, tl], qdv[:, cc_, 1, :], True, True, r=[("kd", h // 2), ("qd", h // 2)], w=[pak])
                for h in range(4):
                    TT(Am[h][i % 2][:], PB[4 + h][:, 0:128], bmask[:], ALU.mult, r=[("pb", 4 + h), "bmask"],
                       w=[("Am", h, i % 2)])
                for h in range(4):
                    MM(PB[h][:, 0:128], v_tm[:, i, 128 * h:128 * (h + 1)], Am[h][i % 2][:], True, False,
                       r=["v_tm", ("Am", h, i % 2)], w=[("pb", h)])
                for cc in range(2):
                    c = 2 * i + cc
                    gc = NCQ * qt + c
                    rows = slice(64 * cc, 64 * (cc + 1))
                    for h in range(4):
                        MM(PB[h][:, 64 * cc:64 * (cc + 1)], Sbf[h][sidx[h] % 2][:], qe[:, h, 64 * c:64 * (c + 1)], False,
                           cc == 1, r=[("Sbf", h, sidx[h] % 2), ("qe", h // 2)], w=[("pb", h)])
                        if gc < 31:
                            MM(PB[4 + h][:, 0:128], ke_tm[rows, h, i, :], v_tm[rows, i, 128 * h:128 * (h + 1)], True, True,
                               r=[("ke_tm", h // 2), "v_tm"], w=[("pb", 4 + h)])
                    if gc < 31:
                        for h in range(4):
                            STT(S32[h][:], S32[h][:], dec[:, NCQ * h + c:NCQ * h + c + 1], PB[4 + h][:, 0:128], ALU.mult,
                                ALU.add, r=[("S32", h), ("dec", h // 2), ("pb", 4 + h)], w=[("S32", h)])
                        for h in range(4):
                            sidx[h] += 1
                            if h % 2 == 0:
                                ACT(Sbf[h][sidx[h] % 2][:], S32[h][:], AF.Copy, r=[("S32", h)], w=[("Sbf", h, sidx[h] % 2)])
                            else:
                                CP(Sbf[h][sidx[h] % 2][:], S32[h][:], r=[("S32", h)], w=[("Sbf", h, sidx[h] % 2)])
                for h in range(4):
                    ACT(oT32[:, h, tl], PB[h][:, 0:128], AF.Copy, r=[("pb", h)], w=["oT32"])
            def post_chain(pr):
                hs = slice(2 * pr, 2 * pr + 2)

                def fl2(t_):
                    return t_[:, hs, :].rearrange("p h t -> p (h t)")

                K = lambda nm: (nm, pr)
                ACT(fl2(sg), fl2(oT32), AF.Square, r=["oT32", K("sg")], w=[K("sg")])
                yield
                for h in range(2 * pr, 2 * pr + 2):
                    pn, pnk = PB[4 + h], ("pb", 4 + h)
                    MM(pn[:, :], ones_f[:], sg[:, h, :], True, True, r=["ones_f", K("sg")], w=[pnk])
                    yield
                    ACT(lf[:, h, :], pn[:, :], AF.Ln, bias=eps_c[:, 0:1], scale=1.0 / 128, r=[pnk, K("lf"), "eps_c"], w=[K("lf")])
                    yield
                ACT(fl2(lf), fl2(lf), AF.Exp, scale=-0.5, r=[K("lf")], w=[K("lf")])
                yield
                TT(fl2(lf), fl2(oT32), fl2(lf), ALU.mult, r=[K("lf"), "oT32"], w=[K("lf")])
                yield
                TT(mixT[:, hs, qsl], lf[:, hs, :], gg[:, hs, :], ALU.mult, r=[K("lf"), K("gg")], w=[("mixT", 0, qt, pr)])
                yield

            live = [post_chain(0), post_chain(1)]
            while live:
                for g_ in list(live):
                    if next(g_, "done") == "done":
                        live.remove(g_)
        if "ohgT" in dbg_d:
            DMA(dbg_d["ohgT"], mixT[:, 0:4, :], r=[("mixT", 0, q_, p_) for q_ in range(4) for p_ in range(2)])
    barrier()
    with ExitStack() as esD:
        def sbD(name, shape, dt=F32):
            return esD.enter_context(nc.sbuf_tensor("s_" + name, list(shape), dt))

        X1 = sbD("X1", [128, NT, D])
        gtbc = sbD("gtbc", [128, 2, D])
        ones_d = sbD("ones_d", [128, 128])
        MEMSET(ones_d[:], 1.0, w=["ones_d"])
        dg = [sbD("dg%d" % j, [128, 128]) for j in range(2)]
        gsbc2 = sbD("gsbc2", [128, 2, D])
        for gi, (col0, dstt, dkey_) in enumerate(((GT1, gtbc[:, 0, :], "gtbc"), (GT2, gtbc[:, 1, :], "gtbc"))):
            for hf in range(2):
                pp, ppk = next_pb()
                for cq in range(4):
                    c = 4 * hf + cq
                    dgb = dg[c % 2]
                    dgk = ("dg", c % 2)
                    TS(dgb[:], ident_f[:], modT[:, col0 + c:col0 + c + 1], None, ALU.mult, r=["ident_f", "modT"], w=[dgk])
                    MM(pp[:, 128 * cq:128 * (cq + 1)], ones_d[:], dgb[:], True, True, r=["ones_d", dgk], w=[ppk])
                CP(dstt[:, 512 * hf:512 * (hf + 1)], pp[:, :], r=[ppk], w=[dkey_])
        GROUPS = [(0, 4), (4, 8), (8, 12), (12, 16), (16, 19), (19, 22)]
        wu = [sbD("wu0", [128, 8, 2, 4, 128], BF16), None]
        wd = [sbD("wd0", [128, 4, D], BF16), None]
        wup_v = din("w_up", [D, 2 * DFF]).rearrange("(k p) n -> p k n", p=128)
        wdn_v = din("w_down", [DFF, D]).rearrange("(c p) n -> p c n", p=128)

        loaded = set()

        def grp(gi):
            c0, c1 = GROUPS[gi]
            return c0, c1 - c0, wu[gi % 2], ("wu", gi % 2), wd[gi % 2], ("wd", gi % 2)

        def load_wu(gi):
            if gi >= len(GROUPS) or ("wu", gi) in loaded:
                return
            loaded.add(("wu", gi))
            c0, ncg, wub, wuk, wdb, wdk = grp(gi)
            for s_ in range(2):
                DMA(wub[:, :, s_, 0:ncg, :],
                    wup_v[:, :, DFF * s_ + 128 * c0:DFF * s_ + 128 * (c0 + ncg)].rearrange("p k (c n) -> p k c n", n=128),
                    w=[wuk], eng="pool")

        def load_wd(gi):
            if gi >= len(GROUPS) or ("wd", gi) in loaded:
                return
            loaded.add(("wd", gi))
            c0, ncg, wub, wuk, wdb, wdk = grp(gi)
            DMA(wdb[:, 0:ncg, :], wdn_v[:, c0:c0 + ncg, :], w=[wdk], eng="pool")

        wo_holder = []
        ss2 = sbD("ss2", [128, NT])
        junk2 = sbD("junk2", [128, D], BF16)
        mix_all = [("mixT", 0, q_, p_) for q_ in range(4) for p_ in range(2)] + [("mixT", 1, i) for i in range(NT)]
        with ExitStack() as esD1:
            def sbD1(name, shape, dt=F32):
                return esD1.enter_context(nc.sbuf_tensor("s_" + name, list(shape), dt))

            wo = sbD1("wo", [128, 8, D], BF16)
            DMA(wo[:], din("w_out", [D, D]).rearrange("(c p) n -> p c n", p=128), w=["wo"], eng="pool")
            load_wu(0)
            load_wd(0)
            hgng = sbD1("hgng", [128, 1])
            DMA(hgng[:], din("hgng", [128, 1]), w=["hgng"])
            for c in range(8):
                if c < 4:
                    STT(wo[:, c, :], wo[:, c, :], hgng[:, 0:1], gtbc[:, 0, :], ALU.mult, ALU.mult,
                        r=["wo", "hgng", "gtbc"], w=["wo"])
                else:
                    TT(wo[:, c, :], wo[:, c, :], gtbc[:, 0, :], ALU.mult, r=["wo", "gtbc"], w=["wo"])
            xs2 = [sbD1("xs2_%d" % j, [128, D]) for j in range(2)]
            cnt = 0
            for i in range(NT):
                tsl = slice(128 * i, 128 * (i + 1))
                xb = xs2[i % 2]
                xk = ("xs2", i % 2)
                DMA(xb[:], x_d[tsl, :], w=[xk])
                for hf in range(2):
                    cs_ = slice(512 * hf, 512 * (hf + 1))
                    pp, ppk = next_pb()
                    for c in range(8):
                        MM(pp[:, :], mixT[:, c, tsl], wo[:, c, cs_], c == 0, c == 7, r=mix_all + ["wo"], w=[ppk])
                    TT(X1[:, i, cs_], pp[:, :], xb[:, cs_], ALU.add, r=[ppk, xk], w=[("X1", i)])
                ACT(junk2[:], X1[:, i, :], AF.Square, accum=ss2[:, i:i + 1], r=[("X1", i)], w=["junk2", ("ss2", i)])
        if "X1" in dbg_d:
            DMA(dbg_d["X1"].rearrange("(n p) d -> p n d", p=128), X1[:], r=[("X1", i) for i in range(NT)])
        barrier()
        xn2 = [sbD("xn2_%d" % j, [128, D], BF16) for j in range(2)]
        rstd2 = sbD("rstd2", [128, NT])
        wu[1] = sbD("wu1", [128, 8, 2, 4, 128], BF16)
        wd[1] = sbD("wd1", [128, 4, D], BF16)
        load_wu(1)
        load_wd(1)
        TS(rstd2[:], ss2[:], 1.0 / D, EPS, ALU.mult, ALU.add, r=[("ss2", i) for i in range(NT)], w=["rstd2"])
        ACT(rstd2[:], rstd2[:], AF.Sqrt, r=["rstd2"], w=["rstd2"])
        RECIP(rstd2[:], rstd2[:], r=["rstd2"], w=["rstd2"])
        for i in range(NT):
            xnb = xn2[i % 2]
            nk = ("xn2", i % 2)
            TS(xnb[:], X1[:, i, :], rstd2[:, i:i + 1], None, ALU.mult, r=[("X1", i), "rstd2"], w=[nk])
            pb, pk = next_pb()
            pbb = pb[:].bitcast(BF16)
            for c in range(8):
                TR(pbb[:, 128 * c:128 * (c + 1)], xnb[:, 128 * c:128 * (c + 1)], ident_b[:], r=[nk, "ident_b"], w=[pk])
            ACT(hT[:, :, 128 * i:128 * (i + 1)], pbb[:, :].rearrange("p (c t) -> p c t", c=8), AF.Copy,
                r=[pk], w=[("hT", i)])
        hT_keys2 = [("hT", i) for i in range(NT)]
        for c in range(8):
            TS(hT[:, c, :], hT[:, c, :], modT[:, GM2 + c:GM2 + c + 1], modT[:, SH2 + c:SH2 + c + 1], ALU.mult, ALU.add,
               r=hT_keys2 + ["modT"], w=hT_keys2)
        cw = sbD("cw", [128, 3, 44])
        cb = sbD("cb", [128, 44])
        DMA(cw[:], din("cwT", [128, 3, 44]), w=["cw"])
        DMA(cb[:], din("cbT", [128, 44]), w=["cb"])
        halo = sbD("halo", [128, 44, 2])
        MEMSET(halo[:], 0.0, w=[("halo", c) for c in range(44)])
        hid_t = gsbc2[:].rearrange("p a n -> p (a n)").bitcast(BF16).rearrange("p (j c t) -> p j c t", j=2, c=4)
        mixf = mixT[:].rearrange("p c t -> p (c t)").bitcast(F32)
        U = [[mixf[:, 516 * (3 * s_ + j):516 * (3 * s_ + j) + 514] for j in range(3)] for s_ in range(2)]
        UA = [[mixf[:, 3096 + 512 * (3 * s_ + j):3096 + 512 * (3 * s_ + j + 1)] for j in range(3)] for s_ in range(2)]
        SA = [mixf[:, 6168 + 512 * j:6168 + 512 * (j + 1)] for j in range(2)]
        pend_banks = set()
        ffn_ctr = [0]

        def ffn_pb(hold=False):
            for _ in range(8):
                j = ffn_ctr[0] % 8
                ffn_ctr[0] += 1
                if j not in pend_banks:
                    if hold:
                        pend_banks.add(j)
                    return PB[j], ("pb", j)
            raise RuntimeError("no free PSUM bank")

        jobs = []
        for gi, (c0, c1) in enumerate(GROUPS):
            for blk in range(4):
                for ci in range(c1 - c0):
                    jobs.append((gi, blk, ci))
        NJ = len(jobs)
        pps_of = {}
        wd_scaled = set()

        def st_up(k):
            gi, blk, ci = jobs[k]
            load_wu(gi)
            if blk == 0 and ci == 0:
                load_wu(gi + 1)
            c0, ncg, wub, wuk, wdb, wdk = grp(gi)
            bsl = slice(512 * blk, 512 * (blk + 1))
            res_ = []
            for s_ in range(2):
                pp, ppk = ffn_pb(hold=True)
                for kk_ in range(8):
                    MM(pp[:, :], wub[:, kk_, s_, ci, :], hT[:, kk_, bsl], kk_ == 0, kk_ == 7,
                       r=[("hT", 4 * blk + j) for j in range(4)] + [wuk], w=[ppk])
                res_.append((pp, ppk))
            pps_of[k] = res_

        def bufs(k):
            gi, blk, ci = jobs[k]
            c0 = GROUPS[gi][0]
            st = k % 3
            out = []
            for s_ in range(2):
                chidx = 22 * s_ + c0 + ci
                out.append((chidx, U[s_][st], ("U", s_, st), UA[s_][st], ("UA", s_, st), ("halo", chidx)))
            return out

        def st_A(k):
            info = bufs(k)
            pps = pps_of.pop(k)
            for (_pp, _ppk) in pps:
                pend_banks.discard(_ppk[1])
            for s_ in range(2):
                chidx, Ub, Uk, ua, uak, hkey = info[s_]
                CP(Ub[:, 0:2], halo[:, chidx, :], r=[hkey], w=[Uk], eng="pool")
            for s_ in range(2):
                chidx, Ub, Uk, ua, uak, hkey = info[s_]
                ACT(Ub[:, 2:514], pps[s_][0][:, :], AF.Copy, r=[pps[s_][1]], w=[Uk])
            for s_ in range(2):
                chidx, Ub, Uk, ua, uak, hkey = info[s_]
                ACT(ua, pps[s_][0][:, :], AF.Identity, bias=cb[:, chidx:chidx + 1],
                    scale=cw[:, 2, chidx:chidx + 1], r=[pps[s_][1], "cb", "cw"], w=[uak])
            for s_ in range(2):
                chidx, Ub, Uk, ua, uak, hkey = info[s_]
                CP(halo[:, chidx, :], Ub[:, 512:514], r=[Uk], w=[hkey], eng="pool")

        def st_B(k):
            info = bufs(k)
            for tap in (1, 0):
                for s_ in range(2):
                    chidx, Ub, Uk, ua, uak, hkey = info[s_]
                    STT(ua, Ub[:, tap:tap + 512], cw[:, tap, chidx:chidx + 1], ua, ALU.mult, ALU.add,
                        r=[Uk, uak, "cw"], w=[uak])

        def st_C(k):
            info = bufs(k)
            ACT(SA[k % 2], info[0][3], AF.Silu, r=[info[0][4]], w=[("SA", k % 2)])

        def blk_index(k):
            gi, blk, ci = jobs[k]
            return 4 * gi + blk

        def st_D(k):
            info = bufs(k)
            gi, blk, ci = jobs[k]
            bi = blk_index(k)
            TT(hid_t[:, bi % 2, ci, :], SA[k % 2], info[1][3], ALU.mult, r=[("SA", k % 2), info[1][4]],
               w=[("hid", bi % 2), "gsbc2"])

        def st_down(k_last):
            gi, blk, ci = jobs[k_last]
            c0, ncg, wub, wuk, wdb, wdk = grp(gi)
            bi = blk_index(k_last)
            if gi not in wd_scaled:
                wd_scaled.add(gi)
                for cj in range(ncg):
                    TT(wdb[:, cj, :], wdb[:, cj, :], gtbc[:, 1, :], ALU.mult, r=[wdk, "gtbc"], w=[wdk])
            hk = ("hid", bi % 2)
            for ii in range(4):
                i = 4 * blk + ii
                for hf in range(2):
                    cs_ = slice(512 * hf, 512 * (hf + 1))
                    pp, ppk = ffn_pb()
                    for cj in range(ncg):
                        MM(pp[:, :], hid_t[:, bi % 2, cj, 128 * ii:128 * (ii + 1)], wdb[:, cj, cs_], cj == 0, cj == ncg - 1,
                           r=[hk, wdk], w=[ppk])
                    TT(X1[:, i, cs_], pp[:, :], X1[:, i, cs_], ALU.add, r=[ppk, ("X1", i)], w=[("X1", i)])
            if blk == 3:
                load_wd(gi + 2)

        down_at = {}
        st_up(0)
        st_up(1)
        st_A(0)
        for k in range(NJ):
            if k + 2 < NJ:
                st_up(k + 2)
            st_B(k)
            if k + 1 < NJ:
                st_A(k + 1)
            st_C(k)
            if k >= 1:
                st_D(k - 1)
                last_of_blk = (blk_index(k - 1) != blk_index(k))
                if last_of_blk:
                    down_at[k + 1] = k - 1
            if k in down_at:
                st_down(down_at.pop(k))
        st_D(NJ - 1)
        for k_ in sorted(down_at):
            st_down(down_at[k_])
        st_down(NJ - 1)
        for i in range(NT):
            DMA(out_d[128 * i:128 * (i + 1), :], X1[:, i, :], r=[("X1", i)])

    S.emit()
    es.close()
    return nc


def _consts():
    f = np.float32
    c = {}
    a = np.arange(128)
    c["tri"] = (a[:, None] <= a[None, :]).astype(f)
    tt = np.arange(T)
    cur = tt // 64
    j = np.arange(32)
    am = np.zeros((T, 32), f)
    am[j[None, :] > cur[:, None]] = -10.0
    forced = (j[None] == 0) | (j[None] == cur[:, None]) | (j[None] == cur[:, None] - 1)
    am[forced] = 1e9
    c["am"] = am
    n_cmp = 127
    cst = np.arange(n_cmp) * 16
    sst = np.arange(32) * 64
    ovl = np.clip(np.minimum(cst[:, None] + 32, sst[None] + 64) - np.maximum(cst[:, None], sst[None]), 0, None) / 32
    c["Mov"] = ovl.astype(f)
    inv = (np.float32(500000.0) ** (-np.arange(8, dtype=f) * np.float32(2.0) / np.float32(16))).astype(f)
    c["invf"] = np.ascontiguousarray(np.broadcast_to(inv[None, :], (128, 8))).astype(f)
    nn = np.arange(128)
    c["cvalid"] = ((16 * nn[None, :] + 31 <= tt[:, None]) & (nn[None, :] < 127)).astype(f)
    c["bmask"] = ((a[:, None] <= a[None, :]) & (a[:, None] // 64 == a[None, :] // 64)).astype(f)
    c["Eblk"] = (np.arange(T)[None, :] // 64 == j[:, None]).astype(f)
    return c


CONSTS = _consts()


def make_inputs(inp, b):
    f = np.float32
    m = {}
    m["x"] = np.ascontiguousarray(inp["x"][b], dtype=f)
    m["cT"] = np.ascontiguousarray(np.asarray(inp["c"][b], dtype=f).reshape(8, 128).T)
    m["posT"] = np.ascontiguousarray(np.asarray(inp["positions"][b], dtype=np.int32).reshape(NT, 128).T)
    m["w_ada"] = np.ascontiguousarray(inp["w_ada"][0], dtype=f)
    m["b_ada"] = np.ascontiguousarray(inp["b_ada"], dtype=f).reshape(1, -1)
    m["norm1_g"] = np.ascontiguousarray(inp["norm1_g"], dtype=f).reshape(1, -1)
    m["w_in"] = np.ascontiguousarray(inp["w_in"][0], dtype=f)
    m["ident"] = np.eye(128, dtype=f)
    m.update(CONSTS)
    m["q_norm_g"] = np.ascontiguousarray(inp["q_norm_g"], dtype=f).reshape(1, 64)
    m["w_cmp1"] = np.ascontiguousarray(inp["w_cmp1"][0], dtype=f)
    m["w_cmp2"] = np.ascontiguousarray(inp["w_cmp2"][0], dtype=f)
    m["peT"] = np.ascontiguousarray(np.asarray(inp["pe_cmp"][0], dtype=f).transpose(2, 0, 1))
    m["lbT"] = np.ascontiguousarray(np.asarray(inp["lb_logits"], dtype=f).reshape(2, 4, 128).transpose(2, 0, 1))
    m["norm2_g"] = np.ascontiguousarray(inp["norm2_g"], dtype=f).reshape(1, -1)
    m["w_out"] = np.ascontiguousarray(inp["w_out"][0], dtype=f)
    m["hgng"] = np.ascontiguousarray(inp["hg_norm_g"], dtype=f).reshape(128, 1)
    m["w_up"] = np.ascontiguousarray(inp["w_up"][0], dtype=f)
    m["w_down"] = np.ascontiguousarray(inp["w_down"][0], dtype=f)
    m["cwT"] = np.ascontiguousarray(np.asarray(inp["conv_w"][0], dtype=f).reshape(3, 44, 128).transpose(2, 0, 1))
    m["cbT"] = np.ascontiguousarray(np.asarray(inp["conv_b"][0], dtype=f).reshape(44, 128).T)
    m["k_norm_g"] = np.ascontiguousarray(inp["k_norm_g"], dtype=f).reshape(1, 192)
    return m


def run(inp, dbg=(), cores=8):
    nc = build(dbg)
    in_maps = [make_inputs(inp, b) for b in range(cores)]
    res = run_bass_kernel_spmd(nc, in_maps, core_ids=list(range(cores)))
    return res


def kernel(**inputs):
    inp = {k: np.asarray(v) for k, v in inputs.items()}
    res = run(inp)
    return np.stack([r["out"] for r in res.results], axis=0).astype(np.float32)
```

```python
import numpy as np
from contextlib import ExitStack
import concourse.bass as bass
import concourse.mybir as mybir
from concourse.bass_utils import run_bass_kernel_spmd

F32, BF16, I32 = mybir.dt.float32, mybir.dt.bfloat16, mybir.dt.int32
AF = mybir.ActivationFunctionType
ALU = mybir.AluOpType
AX = mybir.AxisListType

T = 2048
D = 1024
NT = 16
IN_COLS = 3352
DFF = 2816
EPS = 1e-6

ENGINES = ("pe", "act", "dve", "pool", "sp")
SAME_ENGINE_SYNC = {"pe": False, "act": True, "dve": True, "pool": True, "sp": False}
N_DMA_SEMS = 32


class Op:
    __slots__ = ("eng", "fn", "deps", "idx", "needs_inc", "val", "is_dma", "dsem", "dval")

    def __init__(self, eng, fn, is_dma):
        self.eng = eng
        self.fn = fn
        self.deps = []
        self.idx = None
        self.needs_inc = False
        self.val = None
        self.is_dma = is_dma
        self.dsem = None
        self.dval = None


class Sched:
    def __init__(self, nc):
        self.nc = nc
        self.q = {e: [] for e in ENGINES}
        self.last_w = {}
        self.readers = {}
        self.known = {e: {f: 0 for f in ENGINES} for e in ENGINES}
        self.known_dma = {e: set() for e in ENGINES}
        self.n_dma = 0
        self.dma_ops = []
        self.dma_by_q = {}

    def add(self, eng, fn, reads=(), writes=(), dma=False):
        op = Op(eng, fn, dma)
        op.idx = len(self.q[eng]) + 1
        deps = []
        for r in reads:
            w = self.last_w.get(r)
            if w is not None:
                deps.append(w)
        for w_ in writes:
            w = self.last_w.get(w_)
            if w is not None:
                deps.append(w)
            deps.extend(self.readers.get(w_, ()))
        kn = self.known[eng]
        kd = self.known_dma[eng]
        seen = set()
        for d in deps:
            if d is op or id(d) in seen:
                continue
            seen.add(id(d))
            if d.is_dma:
                if id(d) in kd:
                    continue
                kd.add(id(d))
                op.deps.append(d)
            else:
                if d.eng == eng and not SAME_ENGINE_SYNC[eng]:
                    continue
                if kn[d.eng] >= d.idx:
                    continue
                kn[d.eng] = d.idx
                d.needs_inc = True
                op.deps.append(d)
        if dma:
            lst = self.dma_by_q.setdefault(eng, [])
            n = len(lst)
            base = 0 if eng == "sp" else N_DMA_SEMS // 2
            half = N_DMA_SEMS // 2
            op.dsem = base + n % half
            op.dval = 16 * (n // half + 1)
            if n >= half:
                prev = lst[n - half]
                if id(prev) not in kd:
                    kd.add(id(prev))
                    op.deps.append(prev)
            lst.append(op)
            self.n_dma += 1
            self.dma_ops.append(op)
        self.q[eng].append(op)
        for r in reads:
            self.readers.setdefault(r, []).append(op)
        for w_ in writes:
            self.last_w[w_] = op
            self.readers[w_] = []
        return op

    def emit(self):
        nc = self.nc
        with ExitStack() as es:
            esem = {e: es.enter_context(nc.semaphore("s_" + e)) for e in ENGINES}
            dsem = [es.enter_context(nc.semaphore("d_%d" % i)) for i in range(N_DMA_SEMS)]
            for e in ENGINES:
                c = 0
                for op in self.q[e]:
                    if (not op.is_dma) and op.needs_inc:
                        c += 1
                        op.val = c
            block = es.enter_context(nc.Block())

            def replay(e):
                def run(eng):
                    for op in self.q[e]:
                        for d in op.deps:
                            if d.is_dma:
                                eng.wait_ge(dsem[d.dsem], d.dval)
                            else:
                                eng.wait_ge(esem[d.eng], d.val)
                        ins = op.fn(eng)
                        if op.is_dma:
                            ins.then_inc(dsem[op.dsem], 16)
                        elif op.needs_inc:
                            ins.then_inc(esem[e], 1)
                    if e == "sp":
                        for lst in self.dma_by_q.values():
                            for d in lst[-min(len(lst), N_DMA_SEMS // 2):]:
                                eng.wait_ge(dsem[d.dsem], d.dval)
                return run

            block.tensor(replay("pe"))
            block.scalar(replay("act"))
            block.vector(replay("dve"))
            block.gpsimd(replay("pool"))
            block.sync(replay("sp"))


def build(dbg=()):
    nc = bass.Bass("TRN2", target_bir_lowering=False)
    S = Sched(nc)
    dram_in = {}

    def din(name, shape, dt=F32):
        dram_in[name] = nc.dram_tensor(name, list(shape), dt, kind="ExternalInput").ap()
        return dram_in[name]

    x_d = din("x", [T, D])
    cT_d = din("cT", [128, 8])
    posT_d = din("posT", [128, NT], I32)
    wada_d = din("w_ada", [D, 6 * D])
    bada_d = din("b_ada", [1, 6 * D])
    n1g_d = din("norm1_g", [1, D])
    win_d = din("w_in", [D, IN_COLS])
    ident_d = din("ident", [128, 128])
    out_d = nc.dram_tensor("out", [T, D], F32, kind="ExternalOutput").ap()
    dbg_d = {}
    for spec in dbg:
        name, shape = spec[0], spec[1]
        ddt = BF16 if (len(spec) > 2 and spec[2] == "bf16") else F32
        dbg_d[name] = nc.dram_tensor("dbg_" + name, list(shape), ddt, kind="ExternalOutput").ap()

    es = ExitStack()

    def sb(name, shape, dt=F32):
        return es.enter_context(nc.sbuf_tensor("s_" + name, list(shape), dt))

    def ps(name, shape, dt=F32):
        return es.enter_context(nc.psum_tensor(name, list(shape), dt))

    def DMA(out, in_, r=(), w=(), eng="sp"):
        return S.add(eng, lambda e: e.dma_start(out=out, in_=in_), reads=r, writes=w, dma=True)

    def ACT(out, in_, func, bias=0.0, scale=1.0, accum=None, r=(), w=()):
        if accum is None:
            return S.add("act", lambda e: e.activation(out, in_, func, bias=bias, scale=scale), reads=r, writes=w)
        return S.add("act", lambda e: e.activation(out, in_, func, bias=bias, scale=scale, accum_out=accum),
                     reads=r, writes=w)

    def TS(out, in0, s1, s2, op0, op1=None, r=(), w=(), eng="dve"):
        if op1 is None:
            return S.add(eng, lambda e: e.tensor_scalar(out, in0, s1, None, op0), reads=r, writes=w)
        return S.add(eng, lambda e: e.tensor_scalar(out, in0, s1, s2, op0, op1), reads=r, writes=w)

    def TT(out, in0, in1, op, r=(), w=(), eng="dve"):
        return S.add(eng, lambda e: e.tensor_tensor(out, in0, in1, op), reads=r, writes=w)

    def STT(out, in0, scalar, in1, op0, op1, r=(), w=()):
        return S.add("dve", lambda e: e.scalar_tensor_tensor(out, in0, scalar, in1, op0, op1), reads=r, writes=w)

    def CP(out, in_, r=(), w=(), eng="dve"):
        return S.add(eng, lambda e: e.tensor_copy(out, in_), reads=r, writes=w)

    def RECIP(out, in_, r=(), w=()):
        return S.add("dve", lambda e: e.reciprocal(out, in_), reads=r, writes=w)

    def MEMSET(ap, val, w=(), eng="pool"):
        return S.add(eng, lambda e: e.memset(ap, val), writes=w)

    def MM(out, lhsT, rhs, start, stop, r=(), w=()):
        return S.add("pe", lambda e: e.matmul(out, lhsT, rhs, start=start, stop=stop), reads=r, writes=w)

    def TR(out, in_, ident, r=(), w=()):
        return S.add("pe", lambda e: e.transpose(out, in_, ident), reads=r, writes=w)

    PB = [ps("pb%d" % i, [128, 512], F32) for i in range(8)]
    pb_ctr = [0]

    def next_pb():
        i = pb_ctr[0] % 8
        pb_ctr[0] += 1
        return PB[i], ("pb", i)

    ident_f = sb("ident_f", [128, 128])
    ident_b = sb("ident_b", [128, 128], BF16)
    DMA(ident_f[:], ident_d, w=["ident_f"])
    CP(ident_b[:], ident_f[:], r=["ident_f"], w=["ident_b"])
    ones_row = sb("ones_row", [1, 128])
    MEMSET(ones_row[:], 1.0, w=["ones_row"])

    bar_scr = sb("bar_scr", [1, 64])
    hT = sb("hT", [128, 8, T], BF16)
    modT = sb("modT", [128, 48])
    SH1, GM1, GT1, SH2, GM2, GT2 = 0, 8, 16, 24, 32, 40

    with ExitStack() as esA:
        def sbA(name, shape, dt=F32):
            return esA.enter_context(nc.sbuf_tensor("s_" + name, list(shape), dt))

        modrow = sbA("modrow", [1, 6 * D])
        cT = sbA("cT", [128, 8])
        cs = sbA("cs", [128, 8])
        DMA(cT[:], cT_d, w=["cT"])
        ACT(cs[:], cT[:], AF.Silu, r=["cT"], w=["cs"])
        bada = sbA("bada", [1, 6 * D])
        DMA(bada[:], bada_d, w=["bada"])
        n1g = sbA("n1g", [1, D])
        DMA(n1g[:], n1g_d, w=["n1g"])
        n2g = sbA("n2g", [1, D])
        DMA(n2g[:], din("norm2_g", [1, D]), w=["n2g"])
        wa = [sbA("wa%d" % i, [128, 8, 512]) for i in range(2)]
        wada_v = wada_d.rearrange("(k p) n -> p k n", p=128)
        xs = [sbA("xs%d" % i, [128, D]) for i in range(3)]
        junk = sbA("junk", [128, D], BF16)
        xn = [sbA("xn%d" % i, [128, D], BF16) for i in range(2)]
        ss = sbA("ss", [128, NT])
        rstd = sbA("rstd", [128, NT])

        def x_tile(i):
            xb = xs[i % 3]
            xk = ("xs", i % 3)
            DMA(xb[:], x_d[128 * i:128 * (i + 1), :], w=[xk], eng="pool")
            ACT(junk[:], xb[:], AF.Square, accum=ss[:, i:i + 1], r=[xk], w=["junk", ("ss", i)])
            TS(rstd[:, i:i + 1], ss[:, i:i + 1], 1.0 / D, EPS, ALU.mult, ALU.add, r=[("ss", i)], w=[("rstd", i)])
            ACT(rstd[:, i:i + 1], rstd[:, i:i + 1], AF.Sqrt, r=[("rstd", i)], w=[("rstd", i)])
            RECIP(rstd[:, i:i + 1], rstd[:, i:i + 1], r=[("rstd", i)], w=[("rstd", i)])
            xnb = xn[i % 2]
            nk = ("xn", i % 2)
            TS(xnb[:], xb[:], rstd[:, i:i + 1], None, ALU.mult, r=[xk, ("rstd", i)], w=[nk])
            pb, pk = next_pb()
            pbb = pb[:].bitcast(BF16)
            for c in range(8):
                TR(pbb[:, 128 * c:128 * (c + 1)], xnb[:, 128 * c:128 * (c + 1)], ident_b[:],
                   r=[nk, "ident_b"], w=[pk])
            ACT(hT[:, :, 128 * i:128 * (i + 1)], pbb[:, :].rearrange("p (c t) -> p c t", c=8), AF.Copy,
                r=[pk], w=[("hT", i)])

        xt = 0
        for blk in range(12):
            wb = wa[blk % 2]
            key = ("wa", blk % 2)
            DMA(wb[:], wada_v[:, :, 512 * blk:512 * (blk + 1)], w=[key])
            for _ in range(2 if blk % 3 == 0 else 1):
                if xt < NT:
                    x_tile(xt)
                    xt += 1
            pb, pk = next_pb()
            for k in range(8):
                MM(pb[0:1, :], cs[:, k:k + 1], wb[:, k, :], k == 0, k == 7, r=[key, "cs"], w=[pk])
            TT(modrow[0:1, 512 * blk:512 * (blk + 1)], pb[0:1, :], bada[0:1, 512 * blk:512 * (blk + 1)],
               ALU.add, r=[pk, "bada"], w=[("modrow", blk)])
        while xt < NT:
            x_tile(xt)
            xt += 1
        mr_all = [("modrow", b) for b in range(12)]
        for (off, gsrc, gkey) in ((D, n1g, "n1g"), (4 * D, n2g, "n2g")):
            STT(modrow[0:1, off:off + D], modrow[0:1, off:off + D], 1.0, gsrc[0:1, :], ALU.add, ALU.mult,
                r=mr_all + [gkey], w=mr_all)
        one11 = ones_row[0:1, 0:1]
        pb, pk = next_pb()
        for j in range(48):
            MM(pb[:, j:j + 1], modrow[0:1, 128 * j:128 * (j + 1)], one11, True, True,
               r=mr_all + ["ones_row"], w=[pk])
        CP(modT[:], pb[:, 0:48], r=[pk], w=["modT"])
        hT_keys = [("hT", i) for i in range(NT)]
        for c in range(8):
            TS(hT[:, c, :], hT[:, c, :], modT[:, GM1 + c:GM1 + c + 1], modT[:, SH1 + c:SH1 + c + 1], ALU.mult, ALU.add,
               r=hT_keys + ["modT"], w=hT_keys)
        if "modrow" in dbg_d:
            DMA(dbg_d["modrow"], modrow[:], r=mr_all)
        if "hT" in dbg_d:
            hTf = sbA("hTf", [128, 8, T])
            CP(hTf[:], hT[:], r=[("hT", i) for i in range(NT)], w=["hTf"])
            DMA(dbg_d["hT"].rearrange("(c p) t -> p c t", p=128), hTf[:], r=["hTf"])

    bar_n = [0]

    def barrier():
        n = bar_n[0]
        bar_n[0] += 1
        o1 = ones_row[0:1, 0:1]
        pb, pk = next_pb()
        MM(pb[0:1, 0:1], o1, o1, True, True, r=["ones_row"], w=[pk, ("bar", "pe", n)])
        ACT(bar_scr[0:1, 0:1], o1, AF.Copy, r=["ones_row"], w=[("bar", "act", n)])
        CP(bar_scr[0:1, 8:9], o1, r=["ones_row"], w=[("bar", "dve", n)])
        CP(bar_scr[0:1, 16:17], o1, r=["ones_row"], w=[("bar", "pool", n)], eng="pool")
        allb = [("bar", e, n) for e in ("pe", "act", "dve", "pool")]
        pb, pk = next_pb()
        MM(pb[0:1, 0:1], o1, o1, True, True, r=allb, w=[pk])
        ACT(bar_scr[0:1, 1:2], o1, AF.Copy, r=allb, w=[("bar2", "act", n)])
        CP(bar_scr[0:1, 9:10], o1, r=allb, w=[("bar2", "dve", n)])
        CP(bar_scr[0:1, 17:18], o1, r=allb, w=[("bar2", "pool", n)], eng="pool")
        DMA(bar_scr[0:1, 32:40], ones_row[0:1, 0:8], r=allb, w=["bar_sp"])

    mixT = sb("mixT", [128, 8, T], BF16)
    hT_all = [("hT", i) for i in range(NT)]
    TWO_PI = 6.283185307179586
    HALF_PI = 1.5707963267948966
    win_v = win_d.rearrange("(k p) n -> p k n", p=128)

    barrier()
    with ExitStack() as esB:
        def sbB(name, shape, dt=F32):
            return esB.enter_context(nc.sbuf_tensor("s_" + name, list(shape), dt))

        tri_f = sbB("tri_f", [128, 128])
        tri_b = sbB("tri_b", [128, 128], BF16)
        anti_b = sbB("anti_b", [128, 128], BF16)
        am = sbB("am", [128, NT, 32])
        Mov = sbB("Mov", [128, 32])
        Qaug = [sbB("Qaug%d" % g, [128, NT, 4, 128], BF16) for g in range(2)]
        Ksaug = [sbB("Ksaug%d" % g, [128, T], BF16) for g in range(2)]
        KwT = [sbB("KwT%d" % g, [128, T], BF16) for g in range(2)]
        Vs = sbB("Vs", [128, NT, 2, 65], BF16)
        Vw = sbB("Vw", [128, NT, 2, 65], BF16)
        gsig = sbB("gsig", [128, NT, 24])
        kcmpT = [sbB("kcmpT%d" % g, [64, 128], BF16) for g in range(2)]
        vcmp = [sbB("vcmp%d" % g, [128, 64]) for g in range(2)]
        esB1 = ExitStack()

        def sbB1(name, shape, dt=F32):
            return esB1.enter_context(nc.sbuf_tensor("s_" + name, list(shape), dt))

        KcT = [sbB1("KcT%d" % g, [64, T], BF16) for g in range(2)]
        VcT = [sbB1("VcT%d" % g, [64, T], BF16) for g in range(2)]
        esB1a = ExitStack()

        def sbB1a(name, shape, dt=F32):
            return esB1a.enter_context(nc.sbuf_tensor("s_" + name, list(shape), dt))

        DMA(tri_f[:], din("tri", [128, 128]), w=["tri_f"])
        CP(tri_b[:], tri_f[:], r=["tri_f"], w=["tri_b"])
        TS(anti_b[:], tri_f[:], -1.0, 1.0, ALU.mult, ALU.add, r=["tri_f"], w=["anti_b"])
        DMA(am[:], din("am", [T, 32]).rearrange("(n p) j -> p n j", p=128), w=["am"])
        DMA(Mov[0:127, :], din("Mov", [127, 32]), w=["Mov"])
        invf = sbB1a("invf", [128, 8])
        DMA(invf[:], din("invf", [128, 8]), w=["invf"])
        qg8 = sbB1a("qg8", [128, 64])
        DMA(qg8[:], din("q_norm_g", [1, 64]).partition_broadcast(128), w=["qg8"])
        TS(qg8[:], qg8[:], 0.125, None, ALU.mult, r=["qg8"], w=["qg8"])
        kg = sbB1a("kg", [128, 3, 64])
        DMA(kg[:].rearrange("p a d -> p (a d)"), din("k_norm_g", [1, 192]).partition_broadcast(128), w=["kg"])

        posi = sbB1a("posi", [128, NT], I32)
        posf = sbB1a("posf", [128, NT])
        DMA(posi[:], posT_d, w=["posi"])
        CP(posf[:], posi[:], r=["posi"], w=["posf"])
        ang = sbB1a("ang", [128, NT, 8])
        TT(ang[:], posf[:].unsqueeze(2).to_broadcast([128, NT, 8]), invf[:].unsqueeze(1).to_broadcast([128, NT, 8]),
           ALU.mult, r=["posf", "invf"], w=["ang"])
        sinT = sbB1a("sinT", [128, NT, 8])
        cosT = sbB1a("cosT", [128, NT, 8])
        rr_t = sbB1a("rr_t", [128, NT, 8])
        rr_i = sbB1a("rr_i", [128, NT, 8], I32)
        rr_k = sbB1a("rr_k", [128, NT, 8])
        for (dst, shift, key) in ((sinT, 0.0, "sinT"), (cosT, HALF_PI, "cosT")):
            TS(rr_t[:], ang[:], shift, 1.0 / TWO_PI, ALU.add, ALU.mult, r=["ang"], w=["rr_t"])
            CP(rr_i[:], rr_t[:], r=["rr_t"], w=["rr_i"])
            CP(rr_k[:], rr_i[:], r=["rr_i"], w=["rr_k"])
            TS(rr_t[:], ang[:], shift, None, ALU.add, r=["ang"], w=["rr_t"])
            STT(rr_t[:], rr_k[:], -TWO_PI, rr_t[:], ALU.mult, ALU.add, r=["rr_k", "rr_t"], w=["rr_t"])
            TS(rr_k[:], rr_t[:], 3.141592653589793, None, ALU.is_gt, r=["rr_t"], w=["rr_k"])
            STT(rr_t[:], rr_k[:], -TWO_PI, rr_t[:], ALU.mult, ALU.add, r=["rr_k", "rr_t"], w=["rr_t"])
            TS(rr_k[:], rr_t[:], -3.141592653589793, None, ALU.is_lt, r=["rr_t"], w=["rr_k"])
            STT(rr_t[:], rr_k[:], TWO_PI, rr_t[:], ALU.mult, ALU.add, r=["rr_k", "rr_t"], w=["rr_t"])
            TS(rr_t[:], rr_t[:], -3.1415925, 3.1415925, ALU.max, ALU.min, r=["rr_t"], w=["rr_t"])
            ACT(dst[:], rr_t[:], AF.Sin, r=["rr_t"], w=[key])

        wq = sbB1a("wq", [128, 8, 512], BF16)
        w1b = sbB1a("w1b", [128, 8, 512], BF16)
        w2b = sbB1a("w2b", [128, 8, 280], BF16)
        DMA(wq[:], win_v[:, :, 2048:2560], w=["wq"], eng="pool")
        DMA(w1b[:], win_v[:, :, 2560:3072], w=["w1b"], eng="pool")
        DMA(w2b[:], win_v[:, :, 3072:3352], w=["w2b"], eng="pool")

        E_d = din("Eblk", [32, T])
        for g in range(2):
            MEMSET(Qaug[g][64:128, :, :, :], 0.0, w=[("Qm", g, i) for i in range(NT)])
            MEMSET(Ksaug[g][64:128, :], 0.0, w=[("KsZ", g), ("KsE", g)])
            MEMSET(KwT[g][64:128, :], 0.0, w=[("KwZ", g)])
            DMA(Ksaug[g][64:96, :], E_d, w=[("KsE", g)], eng="pool")
        MEMSET(Vs[:, :, :, 64:65], 1.0, w=["Vs1"])
        MEMSET(Vw[:, :, :, 64:65], 1.0, w=["Vw1"])

        qf = [sbB1a("qf%d" % i, [128, 8, 64]) for i in range(2)]
        kf = [sbB1a("kf%d" % i, [128, 6, 64]) for i in range(2)]
        sqt2 = [sbB1a("sqt%d" % j, [128, 8, 64]) for j in range(4)]
        ssq2 = [sbB1a("ssq%d" % j, [128, 8]) for j in range(4)]
        rt2 = [[sbB1a("rt%d_%d" % (j, i), [128, 8, 8]) for i in range(4)] for j in range(4)]
        qb = [sbB1a("qb%d" % i, [128, 8, 64], BF16) for i in range(2)]
        kb = [sbB1a("kb%d" % i, [128, 6, 64], BF16) for i in range(2)]
        epsb = sbB1a("epsb", [128, 1])
        MEMSET(epsb[:], EPS, w=["epsb"])

        def normrope(src, skey, nh, gains_bc, gkey, dst, dkey, i, par):
            sqt, ssq, rt = sqt2[par], ssq2[par], rt2[par]
            ksq, kss = ("sqt", par), ("ssq", par)
            krt = [("rt", par, j) for j in range(4)]
            ACT(sqt[:, 0:nh, :], src, AF.Square, r=[skey], w=[ksq])
            yield
            S.add("dve", lambda e: e.tensor_reduce(ssq[:, 0:nh], sqt[:, 0:nh, :], AX.X, ALU.add),
                  reads=[ksq], writes=[kss])
            yield
            ACT(ssq[:, 0:nh], ssq[:, 0:nh], AF.Ln, bias=epsb[:, 0:1], scale=1.0 / 64, r=[kss, "epsb"], w=[kss])
            ACT(ssq[:, 0:nh], ssq[:, 0:nh], AF.Exp, scale=-0.5, r=[kss], w=[kss])
            yield
            TT(src, src, ssq[:, 0:nh].unsqueeze(2).to_broadcast([128, nh, 64]), ALU.mult, r=[skey, kss], w=[skey])
            yield
            TT(src, src, gains_bc, ALU.mult, r=[skey, gkey], w=[skey])
            yield
            cb = cosT[:, i, :].unsqueeze(1).to_broadcast([128, nh, 8])
            sbb = sinT[:, i, :].unsqueeze(1).to_broadcast([128, nh, 8])
            x1 = src[:, :, 0:8]
            x2 = src[:, :, 8:16]
            ACT(dst, src, AF.Copy, r=[skey], w=[dkey])
            TT(rt[0][:, 0:nh, :], x1, cb, ALU.mult, r=[skey, "cosT"], w=[krt[0]])
            yield
            TT(rt[1][:, 0:nh, :], x2, sbb, ALU.mult, r=[skey, "sinT"], w=[krt[1]])
            yield
            TT(rt[2][:, 0:nh, :], x2, cb, ALU.mult, r=[skey, "cosT"], w=[krt[2]])
            yield
            TT(rt[3][:, 0:nh, :], x1, sbb, ALU.mult, r=[skey, "sinT"], w=[krt[3]])
            yield
            TT(dst[:, :, 0:8], rt[0][:, 0:nh, :], rt[1][:, 0:nh, :], ALU.subtract, r=[krt[0], krt[1], dkey], w=[dkey])
            yield
            TT(dst[:, :, 8:16], rt[2][:, 0:nh, :], rt[3][:, 0:nh, :], ALU.add, r=[krt[2], krt[3], dkey], w=[dkey])
            yield

        qg_bc = qg8[:].unsqueeze(1).to_broadcast([128, 8, 64])
        kg6 = sbB1a("kg6", [128, 3, 2, 64])
        CP(kg6[:, :, 0, :], kg[:], r=["kg"], w=["kg6"])
        CP(kg6[:, :, 1, :], kg[:], r=["kg"], w=["kg6"])

        def tile_gen(i):
            par = i % 2
            tsl = slice(128 * i, 128 * (i + 1))
            pq, pqk = PB[4 * par + 0], ("pb", 4 * par + 0)
            p1, p1k = PB[4 * par + 1], ("pb", 4 * par + 1)
            p2, p2k = PB[4 * par + 2], ("pb", 4 * par + 2)
            for k in range(8):
                MM(pq[:, :], hT[:, k, tsl], wq[:, k, :], k == 0, k == 7, r=[("hT", i), "wq"], w=[pqk])
            for k in range(8):
                MM(p1[:, :], hT[:, k, tsl], w1b[:, k, :], k == 0, k == 7, r=[("hT", i), "w1b"], w=[p1k])
            for k in range(8):
                MM(p2[:, 0:280], hT[:, k, tsl], w2b[:, k, :], k == 0, k == 7, r=[("hT", i), "w2b"], w=[p2k])
            yield
            qfb, qfk = qf[par], ("qf", par)
            kfb, kfk = kf[par], ("kf", par)
            ACT(qfb[:].rearrange("p h d -> p (h d)"), pq[:, :], AF.Copy, r=[pqk], w=[qfk])
            ACT(kfb[:, 0:2, :].rearrange("p h d -> p (h d)"), p1[:, 0:128], AF.Copy, r=[p1k], w=[kfk])
            ACT(kfb[:, 2:4, :].rearrange("p h d -> p (h d)"), p1[:, 256:384], AF.Copy, r=[p1k], w=[kfk])
            ACT(kfb[:, 4:6, :].rearrange("p h d -> p (h d)"), p2[:, 0:128], AF.Copy, r=[p2k], w=[kfk])
            yield
            ACT(Vs[:, i, :, 0:64], p1[:, 384:512].rearrange("p (g d) -> p g d", g=2), AF.Copy, r=[p1k], w=[("Vs", i)])
            ACT(Vw[:, i, :, 0:64], p2[:, 128:256].rearrange("p (g d) -> p g d", g=2), AF.Copy, r=[p2k], w=[("Vw", i)])
            ACT(gsig[:, i, :], p2[:, 256:280], AF.Exp, scale=-1.0, r=[p2k], w=[("gsig", i)])
            yield
            TS(gsig[:, i, :], gsig[:, i, :], 1.0, None, ALU.add, r=[("gsig", i)], w=[("gsig", i)])
            RECIP(gsig[:, i, :], gsig[:, i, :], r=[("gsig", i)], w=[("gsig", i)])
            yield
            qbb, qbk = qb[par], ("qb", par)
            kbb, kbk = kb[par], ("kb", par)
            g1 = normrope(qfb[:], qfk, 8, qg_bc, "qg8", qbb[:], qbk, i, 2 * par)
            g2 = normrope(kfb[:], kfk, 6, kg6[:].rearrange("p a g d -> p (a g) d"), "kg6", kbb[:], kbk, i, 2 * par + 1)
            live = [g1, g2]
            while live:
                for g_ in list(live):
                    if next(g_, "done") == "done":
                        live.remove(g_)
                yield
            ptb = pq[:].bitcast(BF16)
            for h in range(8):
                TR(ptb[0:64, 128 * h:128 * (h + 1)], qbb[:, h, :], ident_b[:], r=[qbk, "ident_b"], w=[pqk])
            pt2b = p1[:].bitcast(BF16)
            for s_ in range(6):
                TR(pt2b[0:64, 128 * s_:128 * (s_ + 1)], kbb[:, s_, :], ident_b[:], r=[kbk, "ident_b"], w=[p1k])
            yield
            for g in range(2):
                ACT(Qaug[g][0:64, i, :, :], ptb[0:64, 512 * g:512 * (g + 1)].rearrange("p (h t) -> p h t", h=4),
                    AF.Copy, r=[pqk], w=[("Q", g, i)])
            yield
            for g in range(2):
                CP(KcT[g][:, tsl], pt2b[0:64, 128 * g:128 * (g + 1)], r=[p1k], w=[("KcT", g, i)])
                CP(Ksaug[g][0:64, tsl], pt2b[0:64, 128 * (2 + g):128 * (3 + g)], r=[p1k], w=[("Ks", g, i)])
                CP(KwT[g][0:64, tsl], pt2b[0:64, 128 * (4 + g):128 * (5 + g)], r=[p1k], w=[("Kw", g, i)])
            yield

        nxt = 0
        active = []
        while nxt < NT or active:
            while len(active) < 2 and nxt < NT:
                active.append(tile_gen(nxt))
                nxt += 1
            for g_ in list(active):
                if next(g_, "done") == "done":
                    active.remove(g_)
        for blk in range(4):
            bsl = slice(512 * blk, 512 * (blk + 1))
            for g in range(2):
                pv, pvk = next_pb()
                for k in range(8):
                    MM(pv[0:64, :], w1b[:, k, 128 + 64 * g:128 + 64 * (g + 1)], hT[:, k, bsl], k == 0, k == 7,
                       r=hT_all + ["w1b"], w=[pvk])
                ACT(VcT[g][:, bsl], pv[0:64, :], AF.Copy, r=[pvk], w=[("VcT", g)])

        def dump(name, ap, keys):
            if name in dbg_d:
                DMA(dbg_d[name], ap, r=keys)

        dump("Q0", Qaug[0][0:64, :, :, :], [("Q", 0, i) for i in range(NT)])
        dump("Ks1", Ksaug[1][0:96, :], [("Ks", 1, i) for i in range(NT)] + [("KsE", 1)])
        dump("Kw0", KwT[0][0:64, :], [("Kw", 0, i) for i in range(NT)])
        dump("Vc1", VcT[1][:], [("VcT", 1)])
        dump("Vs", Vs[:], [("Vs", i) for i in range(NT)] + ["Vs1"])
        dump("gsig", gsig[:], [("gsig", i) for i in range(NT)])
        esB1a.close()
        barrier()
        w1c = [sbB1("w1c%d" % kv, [64, 32, 256], BF16) for kv in range(2)]
        w2c = [sbB1("w2c%d" % kv, [128, 2, 64], BF16) for kv in range(2)]
        wc1_d = din("w_cmp1", [2, 2048, 256])
        wc2_d = din("w_cmp2", [2, 256, 64])
        for kv in range(2):
            DMA(w1c[kv][:], wc1_d[kv].rearrange("(l d) c -> d l c", d=64), w=[("w1c", kv)], eng="pool")
            DMA(w2c[kv][:], wc2_d[kv].rearrange("(cc c) d -> c cc d", c=128), w=[("w2c", kv)], eng="pool")
        peTf = sbB1("peTf", [64, 2, 32])
        peTb = sbB1("peTb", [64, 2, 32], BF16)
        DMA(peTf[:], din("peT", [64, 2, 32]), w=["peTf"])
        CP(peTb[:], peTf[:], r=["peTf"], w=["peTb"])
        onesb = sbB1("onesb", [1, 128], BF16)
        MEMSET(onesb[:], 1.0, w=["onesb"])
        brow = sbB1("brow", [1, 512], BF16)
        pbi, pbik = next_pb()
        for kv in range(2):
            for l in range(32):
                MM(pbi[0:1, 256 * kv:256 * (kv + 1)], peTb[:, kv, l:l + 1], w1c[kv][:, l, :], l == 0, l == 31,
                   r=[("w1c", kv), "peTb"], w=[pbik])
        CP(brow[:], pbi[0:1, 0:512], r=[pbik], w=["brow"])
        hc = [sbB1("hc%d" % j, [128, 256], BF16) for j in range(2)]
        hcT = [sbB1("hcT%d" % j, [128, 2, 128], BF16) for j in range(2)]
        KcT_keys = lambda g: [("KcT", g, i) for i in range(NT)]
        cnt = 0
        for g in range(2):
            for kv in range(2):
                src = KcT[g] if kv == 0 else VcT[g]
                skeys = KcT_keys(g) if kv == 0 else [("VcT", g)]
                hcb, hck = hc[cnt % 2], ("hc", cnt % 2)
                htb, htk = hcT[cnt % 2], ("hcT", cnt % 2)
                cnt += 1
                ph, phk = next_pb()
                for l in range(32):
                    MM(ph[0:127, 0:256], src[:, l:l + 16 * 126 + 1:16], w1c[kv][:, l, :], l == 0, False,
                       r=skeys + [("w1c", kv)], w=[phk])
                MM(ph[0:127, 0:256], onesb[0:1, 0:127], brow[0:1, 256 * kv:256 * (kv + 1)], False, True,
                   r=["onesb", "brow"], w=[phk])
                ACT(hcb[0:127, :], ph[0:127, 0:256], AF.Silu, r=[phk], w=[hck])
                pt, ptk = next_pb()
                ptb = pt[:].bitcast(BF16)
                for cc in range(2):
                    TR(ptb[:, 128 * cc:128 * cc + 127], hcb[0:127, 128 * cc:128 * (cc + 1)], ident_b[0:127, 0:127],
                       r=[hck, "ident_b"], w=[ptk])
                CP(htb[:, :, 0:127], ptb[:, 0:256].rearrange("p (c n) -> p c n", c=2)[:, :, 0:127], r=[ptk], w=[htk])
                po, pok = next_pb()
                if kv == 0:
                    for cc in range(2):
                        MM(po[0:64, 0:127], w2c[0][:, cc, :], htb[:, cc, 0:127], cc == 0, cc == 1,
                           r=[htk, ("w2c", 0)], w=[pok])
                    CP(kcmpT[g][:, 0:127], po[0:64, 0:127], r=[pok], w=[("kcmpT", g)])
                else:
                    for cc in range(2):
                        MM(po[0:127, 0:64], htb[:, cc, 0:127], w2c[1][:, cc, :], cc == 0, cc == 1,
                           r=[htk, ("w2c", 1)], w=[pok])
                    CP(vcmp[g][0:127, :], po[0:127, 0:64], r=[pok], w=[("vcmp", g)])
        dump("kcmpT1", kcmpT[1][:, 0:127], [("kcmpT", 1)])
        dump("vcmp0", vcmp[0][0:127, :], [("vcmp", 0)])
        esB1.close()
        barrier()
        esB3 = ExitStack()

        def sbB3(name, shape, dt=F32):
            return esB3.enter_context(nc.sbuf_tensor("s_" + name, list(shape), dt))

        cv_d = din("cvalid", [T, 128])
        cvt = [sbB3("cvt%d" % j, [128, 128]) for j in range(3)]
        esb2 = [sbB3("esb%d" % j, [128, 4, 128]) for j in range(2)]
        mx2 = [sbB3("mx%d" % j, [128, 4]) for j in range(2)]
        sm2 = [sbB3("sm%d" % j, [128, 4]) for j in range(2)]
        pTs2 = [sbB3("pTs%d" % j, [128, 4, 128], BF16) for j in range(2)]
        pbf2 = [sbB3("pbf%d" % j, [128, 4, 128], BF16) for j in range(2)]
        psm2 = [sbB3("psm%d" % j, [128, 128]) for j in range(2)]
        psT2 = [sbB3("psT%d" % j, [128, 128]) for j in range(2)]
        vcmpb = [sbB3("vcmpb%d" % g, [128, 64], BF16) for g in range(2)]
        for g in range(2):
            CP(vcmpb[g][0:127, :], vcmp[g][0:127, :], r=[("vcmp", g)], w=[("vcmpb", g)])
        impv2 = [sbB3("impv%d" % j, [128, 32]) for j in range(2)]
        cmp32 = [sbB3("cmp3_%d" % j, [128, 32, 32]) for j in range(2)]
        rank2 = [sbB3("rank%d" % j, [128, 32]) for j in range(2)]
        negm2 = [sbB3("negm%d" % j, [128, 96], BF16) for j in range(2)]
        for j in range(2):
            MEMSET(negm2[j][:], 0.0, w=[("negm", j)])
        ptS = [sbB3("ptS%d" % j, [128, 4, 128], BF16) for j in range(6)]
        ocs = [sbB3("ocs%d" % j, [128, 4, 64]) for j in range(3)]
        acc = sbB3("acc", [128, 4, 64])
        tmpa = sbB3("tmpa", [128, 4, 64])
        tmpb = sbB3("tmpb", [128, 4, 64])
        rd = sbB3("rd", [128, 2, 4])
        Ocp = [[sbB3("Ocp%d_%d" % (a_, b_), [128, 260]) for b_ in range(2)] for a_ in range(2)]
        onsa = [sbB3("onsa%d" % j, [128, 8, 64], BF16) for j in range(2)]
        st_ctr = [0]
        cm_ctr = [0]

        def st_pb():
            j = 2 + st_ctr[0] % 4
            st_ctr[0] += 1
            return PB[j], ("pb", j)

        def cm_pb():
            j = 6 + cm_ctr[0] % 2
            cm_ctr[0] += 1
            return PB[j], ("pb", j)

        torder = list(range(NT))
        tpos = {t_: p_ for p_, t_ in enumerate(torder)}
        steps = [(i, g) for i in torder for g in range(2)]

        def cmp_gen(i, g, step):
            sp_ = step % 2
            esb, mx, sm, pTs = esb2[sp_], mx2[sp_], sm2[sp_], pTs2[sp_]
            pbf, psm, psT = pbf2[sp_], psm2[sp_], psT2[sp_]
            kB, kPS, kPT = ("pbf", sp_), ("psm", sp_), ("psT", sp_)
            impv, cmp3, rank, negm = impv2[sp_], cmp32[sp_], rank2[sp_], negm2[sp_]
            kE, kM, kS, kP = ("esb", sp_), ("mx", sp_), ("sm", sp_), ("pTs", sp_)
            kI, kC, kR, kN = ("impv", sp_), ("cmp3", sp_), ("rank", sp_), ("negm", sp_)
            cvb = cvt[tpos[i] % 3]
            cvk = ("cvt", tpos[i] % 3)
            if g == 0:
                DMA(cvb[:], cv_d[128 * i:128 * (i + 1), :], w=[cvk])
            ocb = ocs[step % 3]
            ock = ("ocs", step % 3)
            qkeys = [("Q", g, i)]
            bank, bk = PB[6 + sp_], ("pb", 6 + sp_)
            for hh in range(4):
                MM(bank[:, 128 * hh:128 * hh + 127], Qaug[g][0:64, i, hh, :], kcmpT[g][:, 0:127], True, True,
                   r=qkeys + [("kcmpT", g)], w=[bk])
            yield
            pscv = bank[:, :].rearrange("p (h n) -> p h n", h=4)[:, :, 0:127]
            S.add("dve", lambda e, a=pscv: e.tensor_reduce(mx[:], a, AX.X, ALU.max), reads=[bk], writes=[kM])
            TS(mx[:], mx[:], -1.0, None, ALU.mult, r=[kM], w=[kM])
            yield
            for hh in range(4):
                ACT(esb[:, hh, 0:127], bank[:, 128 * hh:128 * hh + 127], AF.Exp, bias=mx[:, hh:hh + 1],
                    r=[bk, kM], w=[kE])
            yield
            ev = esb[:, :, 0:127]
            TT(ev, ev, cvb[:, 0:127].unsqueeze(1).to_broadcast([128, 4, 127]), ALU.mult, r=[kE, cvk], w=[kE])
            S.add("dve", lambda e, a=ev: e.tensor_reduce(sm[:], a, AX.X, ALU.add), reads=[kE], writes=[kS])
            yield
            TS(sm[:], sm[:], 1e-30, None, ALU.max, r=[kS], w=[kS])
            RECIP(sm[:], sm[:], r=[kS], w=[kS])
            TT(ev, ev, sm[:].unsqueeze(2).to_broadcast([128, 4, 127]), ALU.mult, r=[kE, kS], w=[kE])
            yield
            yield
            ACT(pbf[:, :, 0:127], ev, AF.Copy, r=[kE], w=[kB])
            if i >= 8:
                S.add("dve", lambda e, a=ev: e.tensor_reduce(psm[:, 0:127], a.rearrange("p h n -> p n h"), AX.X, ALU.add),
                      reads=[kE], writes=[kPS])
            yield
            yield
            yield
            bankb = bank[:].bitcast(BF16)
            for hh in range(4):
                TR(bankb[0:127, 128 * hh:128 * (hh + 1)], pbf[:, hh, 0:127], ident_b[:], r=[kB, "ident_b"], w=[bk])
            yield
            CP(pTs[0:127, :, :], bankb[0:127, 0:512].rearrange("p (h t) -> p h t", h=4), r=[bk], w=[kP])
            yield
            for hh in range(4):
                MM(bank[:, 64 * hh:64 * (hh + 1)], pTs[0:127, hh, :], vcmpb[g][0:127, :], True, True,
                   r=[kP, ("vcmpb", g)], w=[bk])
            yield
            ACT(ocb[:].rearrange("p h d -> p (h d)"), bank[:, 0:256], AF.Copy, r=[bk], w=[ock])
            if i >= 8:
                yield
                TR(bank[0:127, 0:128], psm[:, 0:127], ident_f[:], r=[kPS, "ident_f"], w=[bk])
                yield
                CP(psT[0:127, :], bank[0:127, 0:128], r=[bk], w=[kPT])
                yield
                MM(bank[:, 0:32], psT[0:127, :], Mov[0:127, :], True, True, r=[kPT, "Mov"], w=[bk])
                yield
                TT(impv[:], bank[:, 0:32], am[:, i, :], ALU.add, r=[bk, "am"], w=[kI])
                TT(cmp3[:], impv[:].unsqueeze(1).to_broadcast([128, 32, 32]),
                   impv[:].unsqueeze(2).to_broadcast([128, 32, 32]), ALU.is_gt, r=[kI], w=[kC])
                yield
                S.add("dve", lambda e: e.tensor_reduce(rank[:], cmp3[:], AX.X, ALU.add), reads=[kC], writes=[kR])
                TS(negm[:, 64:96], rank[:], 16.0, -30000.0, ALU.is_ge, ALU.mult, r=[kR], w=[kN])
                yield
                yield
                yield
                yield
                pmb = bank[:].bitcast(BF16)
                TR(pmb[0:96, 0:128], negm[:, 0:96], ident_b[:], r=[kN, "ident_b"], w=[bk])
                yield
                CP(Qaug[g][64:96, i, :, :], pmb[64:96, 0:128].unsqueeze(1).to_broadcast([32, 4, 128]),
                   r=[bk], w=[("Qm", g, i)])

        pt_ctr = [0]

        def attn_S(it):
            (n, i, g, br, kj, kjs) = it
            ksl = slice(128 * kj, 128 * (kj + 1))
            qkeys = [("Q", g, i)]
            pS, pSk = st_pb()
            if br == "win":
                MM(pS[:, :], KwT[g][0:128, ksl], Qaug[g][0:128, i, :, :].rearrange("p h t -> p (h t)"),
                   True, True, r=qkeys + [("Qm", g, i), ("Kw", g, kj), ("KwZ", g)], w=[pSk])
            else:
                MM(pS[:, :], Ksaug[g][0:128, ksl], Qaug[g][0:128, i, :, :].rearrange("p h t -> p (h t)"),
                   True, True, r=qkeys + [("Qm", g, i), ("Ks", g, kj), ("KsE", g), ("KsZ", g)], w=[pSk])
            pb_ = ptS[pt_ctr[0] % 6]
            pbk_ = ("ptS", pt_ctr[0] % 6)
            pt_ctr[0] += 1
            ACT(pb_[:].rearrange("p h t -> p (h t)"), pS[:, :], AF.Exp, r=[pSk], w=[pbk_])
            if kj == i:
                TT(pb_[:], pb_[:], tri_b[:].unsqueeze(1).to_broadcast([128, 4, 128]), ALU.mult,
                   r=[pbk_, "tri_b"], w=[pbk_])
            elif br == "win" and kj == i - 4:
                TT(pb_[:], pb_[:], anti_b[:].unsqueeze(1).to_broadcast([128, 4, 128]), ALU.mult,
                   r=[pbk_, "anti_b"], w=[pbk_])
            return (pb_, pbk_)

        def attn_PV(it, pbp):
            (n, i, g, br, kj, kjs) = it
            pb_, pbk_ = pbp
            Os, Osk = PB[0], ("pb", 0)
            Ow, Owk = PB[1], ("pb", 1)
            Osv = Os[:, 0:260].rearrange("p (h e) -> p h e", h=4)
            Owv = Ow[:, 0:260].rearrange("p (h e) -> p h e", h=4)
            if br == "win":
                O_, Ok_, V_, Vk_, vone = Owv, Owk, Vw, ("Vw", kj), "Vw1"
            else:
                O_, Ok_, V_, Vk_, vone = Osv, Osk, Vs, ("Vs", kj), "Vs1"
            for hh in range(4):
                MM(O_[:, hh, :], pb_[:, hh, :], V_[:, kj, g, :], (kj == kjs[0] and hh == 0), (kj == kjs[-1] and hh == 3),
                   r=[pbk_, Vk_, vone], w=[Ok_])

        items = []
        for n, (i, g) in enumerate(steps):
            for br in ("win", "slc"):
                kjs = list(range(max(0, i - 4), i + 1)) if br == "win" else list(range(0, i + 1))
                for kj in kjs:
                    items.append((n, i, g, br, kj, kjs))
        pend_tr = []

        def flush_tr():
            while pend_tr:
                (onb_, onk_, tsl_, i_) = pend_tr.pop(0)
                pto, ptok = st_pb()
                ptob = pto[:].bitcast(BF16)
                onf = onb_[:].rearrange("p h d -> p (h d)")
                for c in range(4):
                    TR(ptob[:, 128 * c:128 * (c + 1)], onf[:, 128 * c:128 * (c + 1)], ident_b[:], r=[onk_, "ident_b"],
                       w=[ptok])
                ACT(mixT[:, 4:8, tsl_], ptob[:, 0:512].rearrange("p (c t) -> p c t", c=4), AF.Copy, r=[ptok],
                    w=[("mixT", 1, i_)])

        def mk_gen(n_):
            if n_ < len(steps):
                return cmp_gen(steps[n_][0], steps[n_][1], n_)
            return None

        for _ in cmp_gen(steps[0][0], steps[0][1], 0):
            pass
        gen_a = mk_gen(1)
        gen_b = mk_gen(2)
        LOOK = 3
        pendq = [attn_S(items[j_]) for j_ in range(min(LOOK, len(items)))]
        idx = 0
        tick = 0
        for n, (i, g) in enumerate(steps):
            tsl = slice(128 * i, 128 * (i + 1))
            onb = onsa[tpos[i] % 2]
            onk = ("onsa", tpos[i] % 2)
            n_in_step = 0
            while idx < len(items) and items[idx][0] == n:
                cur = items[idx]
                curp = pendq.pop(0)
                last_of_step = (idx + 1 >= len(items)) or (items[idx + 1][0] != n)
                if last_of_step and gen_a is not None:
                    for _ in gen_a:
                        pass
                    gen_a = None
                if idx + LOOK < len(items):
                    nxt_it = items[idx + LOOK]
                    if nxt_it[0] != n and nxt_it[3] == "slc" and gen_a is not None and nxt_it[0] == n + 1:
                        for _ in gen_a:
                            pass
                        gen_a = None
                    pendq.append(attn_S(nxt_it))
                attn_PV(cur, curp)
                tick += 1
                n_in_step += 1
                if n_in_step == 3:
                    flush_tr()
                if gen_a is not None:
                    if next(gen_a, "done") == "done":
                        gen_a = None
                if gen_b is not None:
                    if next(gen_b, "done") == "done":
                        gen_b = None
                idx += 1
            if gen_a is not None:
                for _ in gen_a:
                    pass
            gen_a = gen_b
            gen_b = mk_gen(n + 3)
            ocb = ocs[n % 3]
            ock = ("ocs", n % 3)
            gs0 = gsig[:, i, 12 * g + 0:12 * g + 12:3]
            gs1 = gsig[:, i, 12 * g + 1:12 * g + 12:3]
            gs2 = gsig[:, i, 12 * g + 2:12 * g + 12:3]
            Osb, Owb = Ocp[n % 2][0], Ocp[n % 2][1]
            Osk, Owk = ("Ocp", n % 2, 0), ("Ocp", n % 2, 1)
            ACT(Osb[:], PB[0][:, 0:260], AF.Copy, r=[("pb", 0)], w=[Osk])
            ACT(Owb[:], PB[1][:, 0:260], AF.Copy, r=[("pb", 1)], w=[Owk])
            Osv = Osb[:].rearrange("p (h e) -> p h e", h=4)
            Owv = Owb[:].rearrange("p (h e) -> p h e", h=4)
            RECIP(rd[:, 0, :], Osv[:, :, 64], r=[Osk], w=["rd"])
            RECIP(rd[:, 1, :], Owv[:, :, 64], r=[Owk], w=["rd"])
            TT(rd[:, 0, :], rd[:, 0, :], gs1, ALU.mult, r=["rd", ("gsig", i)], w=["rd"])
            TT(rd[:, 1, :], rd[:, 1, :], gs2, ALU.mult, r=["rd", ("gsig", i)], w=["rd"])
            TT(acc[:], ocb[:], gs0.unsqueeze(2).to_broadcast([128, 4, 64]), ALU.mult, r=[ock, ("gsig", i)], w=["acc"],
               eng="pool")
            TT(tmpa[:], Osv[:, :, 0:64], rd[:, 0, :].unsqueeze(2).to_broadcast([128, 4, 64]), ALU.mult,
               r=[Osk, "rd"], w=["tmpa"])
            TT(tmpb[:], Owv[:, :, 0:64], rd[:, 1, :].unsqueeze(2).to_broadcast([128, 4, 64]), ALU.mult,
               r=[Owk, "rd"], w=["tmpb"])
            TT(acc[:], acc[:], tmpa[:], ALU.add, r=["acc", "tmpa"], w=["acc"], eng="pool")
            TT(onb[:, 4 * g:4 * g + 4, :], acc[:], tmpb[:], ALU.add, r=["acc", "tmpb"], w=[onk], eng="pool")
            if g == 1:
                pend_tr.append((onb, onk, tsl, i))
        flush_tr()
        dump("onsaT", mixT[:, 4:8, :], [("mixT", 1, i) for i in range(NT)])
        esB3.close()
    barrier()
    with ExitStack() as esC:
        def sbC(name, shape, dt=F32):
            return esC.enter_context(nc.sbuf_tensor("s_" + name, list(shape), dt))

        TQ = 512
        NTQ = 4
        NCQ = 8
        FQ = 4 * TQ
        lbl = sbC("lbl", [128, 2, 4])
        DMA(lbl[:], din("lbT", [128, 2, 4]), w=["lbl"])
        lb = sbC("lb", [128, 4])
        oml = sbC("oml", [128, 4])
        noml = sbC("noml", [128, 4])
        TT(lb[:], lbl[:, 0, :], lbl[:, 1, :], ALU.subtract, r=["lbl"], w=["lb"])
        ACT(lb[:], lb[:], AF.Sigmoid, r=["lb"], w=["lb"])
        TS(oml[:], lb[:], -1.0, 1.0, ALU.mult, ALU.add, r=["lb"], w=["oml"])
        TS(noml[:], oml[:], -1.0, None, ALU.mult, r=["oml"], w=["noml"])
        bmask = sbC("bmask", [128, 128], BF16)
        DMA(bmask[:], din("bmask", [128, 128]), w=["bmask"], eng="pool")
        ones_f = sbC("ones_f", [128, 128])
        MEMSET(ones_f[:], 1.0, w=["ones_f"])
        eps_c = sbC("eps_c", [128, 1])
        MEMSET(eps_c[:], EPS, w=["eps_c"])
        rmask = sbC("rmask", [128, FQ], BF16)
        MEMSET(rmask[:], 1.0, w=["rmask"])
        MEMSET(rmask[:, 0:FQ:64], 0.0, w=["rmask"])
        wh = sbC("wh", [128, 8, 4, 512], BF16)
        for j in range(4):
            DMA(wh[:, :, j, :], win_v[:, :, 512 * j:512 * (j + 1)], w=[("wh", j)], eng="pool")
        sg = sbC("sg", [128, 4, TQ])
        lf = sbC("lf", [128, 4, TQ])
        kk = sbC("kk", [128, 4, TQ])
        bb = sbC("bb", [128, 4, TQ])
        qq = sbC("qq", [128, 4, TQ])
        t1 = sbC("t1", [128, 4, TQ])
        qe = sbC("qe", [128, 4, TQ], BF16)
        qd = sbC("qd", [128, 4, TQ], BF16)
        kd = sbC("kd", [128, 4, TQ], BF16)
        kdz = sbC("kdz", [128, 4, TQ], BF16)
        gg = sbC("gg", [128, 4, TQ], BF16)
        keT = sbC("keT", [128, 4, TQ], BF16)
        v_tm = sbC("v_tm", [128, NTQ, 512], BF16)
        ke_tm = sbC("ke_tm", [128, 4, NTQ, 128], BF16)
        oT32 = sbC("oT32", [128, 4, TQ])
        bmid = sbC("bmid", [128, 4 * NCQ])
        blast = sbC("blast", [128, 4 * NCQ])
        dec = sbC("dec", [128, 4 * NCQ])
        S32 = [sbC("S32_%d" % h, [128, 128]) for h in range(4)]
        Sbf = [[sbC("Sbf%d_%d" % (h, j), [128, 128], BF16) for j in range(2)] for h in range(4)]
        Am = [[sbC("Am%d_%d" % (h, j), [128, 128], BF16) for j in range(2)] for h in range(4)]
        for h in range(4):
            MEMSET(S32[h][:], 0.0, w=[("S32", h)])
            MEMSET(Sbf[h][0][:], 0.0, w=[("Sbf", h, 0)])
        sidx = [0, 0, 0, 0]

        def fl(t_):
            return t_[:].rearrange("p h t -> p (h t)")

        def v3(t_):
            return t_[:].rearrange("p h (c t) -> p (h c) t", t=64)

        for qt in range(4):
            t0 = qt * TQ
            qsl = slice(t0, t0 + TQ)
            hkeys = [("hT", 4 * qt + j) for j in range(4)]
            for (j, dst, func, dkey) in ((1, sg, AF.Sigmoid, "sg"), (0, qq, AF.Silu, "qq"), (3, gg, AF.Silu, "gg")):
                for h in range(4):
                    pp, ppk = next_pb()
                    for k in range(8):
                        MM(pp[:, :], wh[:, k, j, 128 * h:128 * (h + 1)], hT[:, k, qsl], k == 0, k == 7,
                           r=hkeys + [("wh", j)], w=[ppk])
                    ACT(dst[:, h, :], pp[:, :], func, r=[ppk], w=[(dkey, h // 2)])
            for i in range(NTQ):
                tsl = slice(t0 + 128 * i, t0 + 128 * (i + 1))
                pp, ppk = next_pb()
                for k in range(8):
                    MM(pp[:, :], hT[:, k, tsl], wh[:, k, 2, :], k == 0, k == 7, r=hkeys + [("wh", 2)], w=[ppk])
                ACT(v_tm[:, i, :], pp[:, :], AF.Copy, r=[ppk], w=["v_tm"])
            def e_chain(pr):
                hs = slice(2 * pr, 2 * pr + 2)
                HF = 2 * TQ
                cs_ = slice(2 * NCQ * pr, 2 * NCQ * (pr + 1))

                def fl2(t_):
                    return t_[:, hs, :].rearrange("p h t -> p (h t)")

                def v32(t_):
                    return t_[:, hs, :].rearrange("p h (c t) -> p (h c) t", t=64)

                K = lambda nm: (nm, pr)
                for h in range(2 * pr, 2 * pr + 2):
                    ACT(lf[:, h, :], sg[:, h, :], AF.Ln, bias=lb[:, h:h + 1], scale=oml[:, h:h + 1],
                        r=[K("sg"), "lb", "oml"], w=[K("lf")])
                    TS(kk[:, h, :], sg[:, h, :], noml[:, h:h + 1], oml[:, h:h + 1], ALU.mult, ALU.add,
                       r=[K("sg"), "noml", "oml"], w=[K("kk")])
                    yield
                S.add("dve", lambda e: e.tensor_tensor_scan(fl2(bb), rmask[:, 0:HF], fl2(lf), 0.0, ALU.mult, ALU.add),
                      reads=["rmask", K("lf")], writes=[K("bb")])
                yield
                ACT(fl2(t1), fl2(bb), AF.Exp, r=[K("bb")], w=[K("t1")])
                CP(bmid[:, cs_], fl2(bb)[:, 31:HF:64], r=[K("bb")], w=[K("bmid")])
                yield
                CP(blast[:, cs_], fl2(bb)[:, 63:HF:64], r=[K("bb")], w=[K("blast")])
                yield
                TT(fl2(qe), fl2(qq), fl2(t1), ALU.mult, r=[K("qq"), K("t1")], w=[K("qe")])
                ACT(dec[:, cs_], blast[:, cs_], AF.Exp, r=[K("blast")], w=[K("dec")])
                yield
                TT(v32(sg), v32(bb), bmid[:, cs_].unsqueeze(2).to_broadcast([128, 2 * NCQ, 64]), ALU.subtract,
                   r=[K("bb"), K("bmid"), K("sg")], w=[K("sg")])
                yield
                TS(fl2(lf), fl2(sg), -1.0, 75.0, ALU.mult, ALU.min, r=[K("sg"), K("lf")], w=[K("lf")])
                yield
                TS(fl2(sg), fl2(sg), 75.0, None, ALU.min, r=[K("sg")], w=[K("sg")])
                ACT(fl2(lf), fl2(lf), AF.Exp, r=[K("lf")], w=[K("lf")])
                yield
                ACT(fl2(sg), fl2(sg), AF.Exp, r=[K("sg")], w=[K("sg")])
                TT(fl2(kd), fl2(kk), fl2(lf), ALU.mult, r=[K("kk"), K("lf")], w=[K("kd")])
                yield
                TT(v32(t1), blast[:, cs_].unsqueeze(2).to_broadcast([128, 2 * NCQ, 64]), v32(bb), ALU.subtract,
                   r=[K("bb"), K("blast"), K("t1")], w=[K("t1")])
                ACT(fl2(kdz), fl2(kd), AF.Copy, r=[K("kd")], w=[K("kdz")])
                yield
                TT(fl2(qd), fl2(qq), fl2(sg), ALU.mult, r=[K("qq"), K("sg")], w=[K("qd")])
                ACT(fl2(t1), fl2(t1), AF.Exp, r=[K("t1")], w=[K("t1")])
                MEMSET(v32(kdz)[:, :, 32:64], 0.0, w=[K("kdz")])
                yield
                TT(fl2(keT), fl2(kk), fl2(t1), ALU.mult, r=[K("kk"), K("t1")], w=[K("keT")])
                yield
                for h in range(2 * pr, 2 * pr + 2):
                    pp, ppk = PB[4 + h], ("pb", 4 + h)
                    ppb = pp[:].bitcast(BF16)
                    for i in range(NTQ):
                        TR(ppb[:, 128 * i:128 * (i + 1)], keT[:, h, 128 * i:128 * (i + 1)], ident_b[:],
                           r=[K("keT"), "ident_b"], w=[ppk])
                    yield
                    CP(ke_tm[:, h, :, :], ppb[:, 0:512].rearrange("p (j k) -> p j k", j=4), r=[ppk], w=[K("ke_tm")])
                    yield

            live = [e_chain(0), e_chain(1)]
            while live:
                for g_ in list(live):
                    if next(g_, "done") == "done":
                        live.remove(g_)
            for i in range(NTQ):
                tl = slice(128 * i, 128 * (i + 1))
                for h in range(4):
                    pa, pak = PB[4 + h], ("pb", 4 + h)
                    pav = pa[:, 0:128].rearrange("p (c a t) -> p c a t", c=2, a=2)
                    qdv = qd[:, h, tl].rearrange("p (c a t) -> p c a t", c=2, a=2)
                    for cc_ in range(2):
                        MM(pav[:, cc_, 0, :], kdz[:, h, tl], qdv[:, cc_, 0, :], True, True, r=[("kdz", h // 2), ("qd", h // 2)], w=[pak])
                        MM(pav[:, cc_, 1, :], kd[:, h, tl], qdv[:, cc_, 1, :], True, True, r=[("kd", h // 2), ("qd", h // 2)], w=[pak])
                for h in range(4):
                    TT(Am[h][i % 2][:], PB[4 + h][:, 0:128], bmask[:], ALU.mult, r=[("pb", 4 + h), "bmask"],
                       w=[("Am", h, i % 2)])
                for h in range(4):
                    MM(PB[h][:, 0:128], v_tm[:, i, 128 * h:128 * (h + 1)], Am[h][i % 2][:], True, False,
                       r=["v_tm", ("Am", h, i % 2)], w=[("pb", h)])
                for cc in range(2):
                    c = 2 * i + cc
                    gc = NCQ * qt + c
                    rows = slice(64 * cc, 64 * (cc + 1))
                    for h in range(4):
                        MM(PB[h][:, 64 * cc:64 * (cc + 1)], Sbf[h][sidx[h] % 2][:], qe[:, h, 64 * c:64 * (c + 1)], False,
                           cc == 1, r=[("Sbf", h, sidx[h] % 2), ("qe", h // 2)], w=[("pb", h)])
                        if gc < 31:
                            MM(PB[4 + h][:, 0:128], ke_tm[rows, h, i, :], v_tm[rows, i, 128 * h:128 * (h + 1)], True, True,
                               r=[("ke_tm", h // 2), "v_tm"], w=[("pb", 4 + h)])
                    if gc < 31:
                        for h in range(4):
                            STT(S32[h][:], S32[h][:], dec[:, NCQ * h + c:NCQ * h + c + 1], PB[4 + h][:, 0:128], ALU.mult,
                                ALU.add, r=[("S32", h), ("dec", h // 2), ("pb", 4 + h)], w=[("S32", h)])
                        for h in range(4):
                            sidx[h] += 1
                            if h % 2 == 0:
                                ACT(Sbf[h][sidx[h] % 2][:], S32[h][:], AF.Copy, r=[("S32", h)], w=[("Sbf", h, sidx[h] % 2)])
                            else:
                                CP(Sbf[h][sidx[h] % 2][:], S32[h][:], r=[("S32", h)], w=[("Sbf", h, sidx[h] % 2)])
                for h in range(4):
                    ACT(oT32[:, h, tl], PB[h][:, 0:128], AF.Copy, r=[("pb", h)], w=["oT32"])
            def post_chain(pr):
                hs = slice(2 * pr, 2 * pr + 2)

                def fl2(t_):
                    return t_[:, hs, :].rearrange("p h t -> p (h t)")

                K = lambda nm: (nm, pr)
                ACT(fl2(sg), fl2(oT32), AF.Square, r=["oT32", K("sg")], w=[K("sg")])
                yield
                for h in range(2 * pr, 2 * pr + 2):
                    pn, pnk = PB[4 + h], ("pb", 4 + h)
                    MM(pn[:, :], ones_f[:], sg[:, h, :], True, True, r=["ones_f", K("sg")], w=[pnk])
                    yield
                    ACT(lf[:, h, :], pn[:, :], AF.Ln, bias=eps_c[:, 0:1], scale=1.0 / 128, r=[pnk, K("lf"), "eps_c"], w=[K("lf")])
                    yield
                ACT(fl2(lf), fl2(lf), AF.Exp, scale=-0.5, r=[K("lf")], w=[K("lf")])
                yield
                TT(fl2(lf), fl2(oT32), fl2(lf), ALU.mult, r=[K("lf"), "oT32"], w=[K("lf")])
                yield
                TT(mixT[:, hs, qsl], lf[:, hs, :], gg[:, hs, :], ALU.mult, r=[K("lf"), K("gg")], w=[("mixT", 0, qt, pr)])
                yield

            live = [post_chain(0), post_chain(1)]
            while live:
                for g_ in list(live):
                    if next(g_, "done") == "done":
                        live.remove(g_)
        if "ohgT" in dbg_d:
            DMA(dbg_d["ohgT"], mixT[:, 0:4, :], r=[("mixT", 0, q_, p_) for q_ in range(4) for p_ in range(2)])
    barrier()
    with ExitStack() as esD:
        def sbD(name, shape, dt=F32):
            return esD.enter_context(nc.sbuf_tensor("s_" + name, list(shape), dt))

        X1 = sbD("X1", [128, NT, D])
        gtbc = sbD("gtbc", [128, 2, D])
        ones_d = sbD("ones_d", [128, 128])
        MEMSET(ones_d[:], 1.0, w=["ones_d"])
        dg = [sbD("dg%d" % j, [128, 128]) for j in range(2)]
        gsbc2 = sbD("gsbc2", [128, 2, D])
        for gi, (col0, dstt, dkey_) in enumerate(((GT1, gtbc[:, 0, :], "gtbc"), (GT2, gtbc[:, 1, :], "gtbc"))):
            for hf in range(2):
                pp, ppk = next_pb()
                for cq in range(4):
                    c = 4 * hf + cq
                    dgb = dg[c % 2]
                    dgk = ("dg", c % 2)
                    TS(dgb[:], ident_f[:], modT[:, col0 + c:col0 + c + 1], None, ALU.mult, r=["ident_f", "modT"], w=[dgk])
                    MM(pp[:, 128 * cq:128 * (cq + 1)], ones_d[:], dgb[:], True, True, r=["ones_d", dgk], w=[ppk])
                CP(dstt[:, 512 * hf:512 * (hf + 1)], pp[:, :], r=[ppk], w=[dkey_])
        GROUPS = [(0, 4), (4, 8), (8, 12), (12, 16), (16, 19), (19, 22)]
        wu = [sbD("wu0", [128, 8, 2, 4, 128], BF16), None]
        wd = [sbD("wd0", [128, 4, D], BF16), None]
        wup_v = din("w_up", [D, 2 * DFF]).rearrange("(k p) n -> p k n", p=128)
        wdn_v = din("w_down", [DFF, D]).rearrange("(c p) n -> p c n", p=128)

        loaded = set()

        def grp(gi):
            c0, c1 = GROUPS[gi]
            return c0, c1 - c0, wu[gi % 2], ("wu", gi % 2), wd[gi % 2], ("wd", gi % 2)

        def load_wu(gi):
            if gi >= len(GROUPS) or ("wu", gi) in loaded:
                return
            loaded.add(("wu", gi))
            c0, ncg, wub, wuk, wdb, wdk = grp(gi)
            for s_ in range(2):
                DMA(wub[:, :, s_, 0:ncg, :],
                    wup_v[:, :, DFF * s_ + 128 * c0:DFF * s_ + 128 * (c0 + ncg)].rearrange("p k (c n) -> p k c n", n=128),
                    w=[wuk], eng="pool")

        def load_wd(gi):
            if gi >= len(GROUPS) or ("wd", gi) in loaded:
                return
            loaded.add(("wd", gi))
            c0, ncg, wub, wuk, wdb, wdk = grp(gi)
            DMA(wdb[:, 0:ncg, :], wdn_v[:, c0:c0 + ncg, :], w=[wdk], eng="pool")

        wo_holder = []
        ss2 = sbD("ss2", [128, NT])
        junk2 = sbD("junk2", [128, D], BF16)
        mix_all = [("mixT", 0, q_, p_) for q_ in range(4) for p_ in range(2)] + [("mixT", 1, i) for i in range(NT)]
        with ExitStack() as esD1:
            def sbD1(name, shape, dt=F32):
                return esD1.enter_context(nc.sbuf_tensor("s_" + name, list(shape), dt))

            wo = sbD1("wo", [128, 8, D], BF16)
            DMA(wo[:], din("w_out", [D, D]).rearrange("(c p) n -> p c n", p=128), w=["wo"], eng="pool")
            load_wu(0)
            load_wd(0)
            hgng = sbD1("hgng", [128, 1])
            DMA(hgng[:], din("hgng", [128, 1]), w=["hgng"])
            for c in range(8):
                if c < 4:
                    STT(wo[:, c, :], wo[:, c, :], hgng[:, 0:1], gtbc[:, 0, :], ALU.mult, ALU.mult,
                        r=["wo", "hgng", "gtbc"], w=["wo"])
                else:
                    TT(wo[:, c, :], wo[:, c, :], gtbc[:, 0, :], ALU.mult, r=["wo", "gtbc"], w=["wo"])
            xs2 = [sbD1("xs2_%d" % j, [128, D]) for j in range(2)]
            cnt = 0
            for i in range(NT):
                tsl = slice(128 * i, 128 * (i + 1))
                xb = xs2[i % 2]
                xk = ("xs2", i % 2)
                DMA(xb[:], x_d[tsl, :], w=[xk])
                for hf in range(2):
                    cs_ = slice(512 * hf, 512 * (hf + 1))
                    pp, ppk = next_pb()
                    for c in range(8):
                        MM(pp[:, :], mixT[:, c, tsl], wo[:, c, cs_], c == 0, c == 7, r=mix_all + ["wo"], w=[ppk])
                    TT(X1[:, i, cs_], pp[:, :], xb[:, cs_], ALU.add, r=[ppk, xk], w=[("X1", i)])
                ACT(junk2[:], X1[:, i, :], AF.Square, accum=ss2[:, i:i + 1], r=[("X1", i)], w=["junk2", ("ss2", i)])
        if "X1" in dbg_d:
            DMA(dbg_d["X1"].rearrange("(n p) d -> p n d", p=128), X1[:], r=[("X1", i) for i in range(NT)])
        barrier()
        xn2 = [sbD("xn2_%d" % j, [128, D], BF16) for j in range(2)]
        rstd2 = sbD("rstd2", [128, NT])
        wu[1] = sbD("wu1", [128, 8, 2, 4, 128], BF16)
        wd[1] = sbD("wd1", [128, 4, D], BF16)
        load_wu(1)
        load_wd(1)
        TS(rstd2[:], ss2[:], 1.0 / D, EPS, ALU.mult, ALU.add, r=[("ss2", i) for i in range(NT)], w=["rstd2"])
        ACT(rstd2[:], rstd2[:], AF.Sqrt, r=["rstd2"], w=["rstd2"])
        RECIP(rstd2[:], rstd2[:], r=["rstd2"], w=["rstd2"])
        for i in range(NT):
            xnb = xn2[i % 2]
            nk = ("xn2", i % 2)
            TS(xnb[:], X1[:, i, :], rstd2[:, i:i + 1], None, ALU.mult, r=[("X1", i), "rstd2"], w=[nk])
            pb, pk = next_pb()
            pbb = pb[:].bitcast(BF16)
            for c in range(8):
                TR(pbb[:, 128 * c:128 * (c + 1)], xnb[:, 128 * c:128 * (c + 1)], ident_b[:], r=[nk, "ident_b"], w=[pk])
            ACT(hT[:, :, 128 * i:128 * (i + 1)], pbb[:, :].rearrange("p (c t) -> p c t", c=8), AF.Copy,
                r=[pk], w=[("hT", i)])
        hT_keys2 = [("hT", i) for i in range(NT)]
        for c in range(8):
            TS(hT[:, c, :], hT[:, c, :], modT[:, GM2 + c:GM2 + c + 1], modT[:, SH2 + c:SH2 + c + 1], ALU.mult, ALU.add,
               r=hT_keys2 + ["modT"], w=hT_keys2)
        cw = sbD("cw", [128, 3, 44])
        cb = sbD("cb", [128, 44])
        DMA(cw[:], din("cwT", [128, 3, 44]), w=["cw"])
        DMA(cb[:], din("cbT", [128, 44]), w=["cb"])
        halo = sbD("halo", [128, 44, 2])
        MEMSET(halo[:], 0.0, w=[("halo", c) for c in range(44)])
        hid_t = gsbc2[:].rearrange("p a n -> p (a n)").bitcast(BF16).rearrange("p (j c t) -> p j c t", j=2, c=4)
        mixf = mixT[:].rearrange("p c t -> p (c t)").bitcast(F32)
        U = [[mixf[:, 516 * (3 * s_ + j):516 * (3 * s_ + j) + 514] for j in range(3)] for s_ in range(2)]
        UA = [[mixf[:, 3096 + 512 * (3 * s_ + j):3096 + 512 * (3 * s_ + j + 1)] for j in range(3)] for s_ in range(2)]
        SA = [mixf[:, 6168 + 512 * j:6168 + 512 * (j + 1)] for j in range(2)]
        pend_banks = set()
        ffn_ctr = [0]

        def ffn_pb(hold=False):
            for _ in range(8):
                j = ffn_ctr[0] % 8
                ffn_ctr[0] += 1
                if j not in pend_banks:
                    if hold:
                        pend_banks.add(j)
                    return PB[j], ("pb", j)
            raise RuntimeError("no free PSUM bank")

        jobs = []
        for gi, (c0, c1) in enumerate(GROUPS):
            for blk in range(4):
                for ci in range(c1 - c0):
                    jobs.append((gi, blk, ci))
        NJ = len(jobs)
        pps_of = {}
        wd_scaled = set()

        def st_up(k):
            gi, blk, ci = jobs[k]
            load_wu(gi)
            if blk == 0 and ci == 0:
                load_wu(gi + 1)
            c0, ncg, wub, wuk, wdb, wdk = grp(gi)
            bsl = slice(512 * blk, 512 * (blk + 1))
            res_ = []
            for s_ in range(2):
                pp, ppk = ffn_pb(hold=True)
                for kk_ in range(8):
                    MM(pp[:, :], wub[:, kk_, s_, ci, :], hT[:, kk_, bsl], kk_ == 0, kk_ == 7,
                       r=[("hT", 4 * blk + j) for j in range(4)] + [wuk], w=[ppk])
                res_.append((pp, ppk))
            pps_of[k] = res_

        def bufs(k):
            gi, blk, ci = jobs[k]
            c0 = GROUPS[gi][0]
            st = k % 3
            out = []
            for s_ in range(2):
                chidx = 22 * s_ + c0 + ci
                out.append((chidx, U[s_][st], ("U", s_, st), UA[s_][st], ("UA", s_, st), ("halo", chidx)))
            return out

        def st_A(k):
            info = bufs(k)
            pps = pps_of.pop(k)
            for (_pp, _ppk) in pps:
                pend_banks.discard(_ppk[1])
            for s_ in range(2):
                chidx, Ub, Uk, ua, uak, hkey = info[s_]
                CP(Ub[:, 0:2], halo[:, chidx, :], r=[hkey], w=[Uk], eng="pool")
            for s_ in range(2):
                chidx, Ub, Uk, ua, uak, hkey = info[s_]
                ACT(Ub[:, 2:514], pps[s_][0][:, :], AF.Copy, r=[pps[s_][1]], w=[Uk])
            for s_ in range(2):
                chidx, Ub, Uk, ua, uak, hkey = info[s_]
                ACT(ua, pps[s_][0][:, :], AF.Identity, bias=cb[:, chidx:chidx + 1],
                    scale=cw[:, 2, chidx:chidx + 1], r=[pps[s_][1], "cb", "cw"], w=[uak])
            for s_ in range(2):
                chidx, Ub, Uk, ua, uak, hkey = info[s_]
                CP(halo[:, chidx, :], Ub[:, 512:514], r=[Uk], w=[hkey], eng="pool")

        def st_B(k):
            info = bufs(k)
            for tap in (1, 0):
                for s_ in range(2):
                    chidx, Ub, Uk, ua, uak, hkey = info[s_]
                    STT(ua, Ub[:, tap:tap + 512], cw[:, tap, chidx:chidx + 1], ua, ALU.mult, ALU.add,
                        r=[Uk, uak, "cw"], w=[uak])

        def st_C(k):
            info = bufs(k)
            ACT(SA[k % 2], info[0][3], AF.Silu, r=[info[0][4]], w=[("SA", k % 2)])

        def blk_index(k):
            gi, blk, ci = jobs[k]
            return 4 * gi + blk

        def st_D(k):
            info = bufs(k)
            gi, blk, ci = jobs[k]
            bi = blk_index(k)
            TT(hid_t[:, bi % 2, ci, :], SA[k % 2], info[1][3], ALU.mult, r=[("SA", k % 2), info[1][4]],
               w=[("hid", bi % 2), "gsbc2"])

        def st_down(k_last):
            gi, blk, ci = jobs[k_last]
            c0, ncg, wub, wuk, wdb, wdk = grp(gi)
            bi = blk_index(k_last)
            if gi not in wd_scaled:
                wd_scaled.add(gi)
                for cj in range(ncg):
                    TT(wdb[:, cj, :], wdb[:, cj, :], gtbc[:, 1, :], ALU.mult, r=[wdk, "gtbc"], w=[wdk])
            hk = ("hid", bi % 2)
            for ii in range(4):
                i = 4 * blk + ii
                for hf in range(2):
                    cs_ = slice(512 * hf, 512 * (hf + 1))
                    pp, ppk = ffn_pb()
                    for cj in range(ncg):
                        MM(pp[:, :], hid_t[:, bi % 2, cj, 128 * ii:128 * (ii + 1)], wdb[:, cj, cs_], cj == 0, cj == ncg - 1,
                           r=[hk, wdk], w=[ppk])
                    TT(X1[:, i, cs_], pp[:, :], X1[:, i, cs_], ALU.add, r=[ppk, ("X1", i)], w=[("X1", i)])
            if blk == 3:
                load_wd(gi + 2)

        down_at = {}
        st_up(0)
        st_up(1)
        st_A(0)
        for k in range(NJ):
            if k + 2 < NJ:
                st_up(k + 2)
            st_B(k)
            if k + 1 < NJ:
                st_A(k + 1)
            st_C(k)
            if k >= 1:
                st_D(k - 1)
                last_of_blk = (blk_index(k - 1) != blk_index(k))
                if last_of_blk:
                    down_at[k + 1] = k - 1
            if k in down_at:
                st_down(down_at.pop(k))
        st_D(NJ - 1)
        for k_ in sorted(down_at):
            st_down(down_at[k_])
        st_down(NJ - 1)
        for i in range(NT):
            DMA(out_d[128 * i:128 * (i + 1), :], X1[:, i, :], r=[("X1", i)])

    S.emit()
    es.close()
    return nc


def _consts():
    f = np.float32
    c = {}
    a = np.arange(128)
    c["tri"] = (a[:, None] <= a[None, :]).astype(f)
    tt = np.arange(T)
    cur = tt // 64
    j = np.arange(32)
    am = np.zeros((T, 32), f)
    am[j[None, :] > cur[:, None]] = -10.0
    forced = (j[None] == 0) | (j[None] == cur[:, None]) | (j[None] == cur[:, None] - 1)
    am[forced] = 1e9
    c["am"] = am
    n_cmp = 127
    cst = np.arange(n_cmp) * 16
    sst = np.arange(32) * 64
    ovl = np.clip(np.minimum(cst[:, None] + 32, sst[None] + 64) - np.maximum(cst[:, None], sst[None]), 0, None) / 32
    c["Mov"] = ovl.astype(f)
    inv = (np.float32(500000.0) ** (-np.arange(8, dtype=f) * np.float32(2.0) / np.float32(16))).astype(f)
    c["invf"] = np.ascontiguousarray(np.broadcast_to(inv[None, :], (128, 8))).astype(f)
    nn = np.arange(128)
    c["cvalid"] = ((16 * nn[None, :] + 31 <= tt[:, None]) & (nn[None, :] < 127)).astype(f)
    c["bmask"] = ((a[:, None] <= a[None, :]) & (a[:, None] // 64 == a[None, :] // 64)).astype(f)
    c["Eblk"] = (np.arange(T)[None, :] // 64 == j[:, None]).astype(f)
    return c


CONSTS = _consts()


def make_inputs(inp, b):
    f = np.float32
    m = {}
    m["x"] = np.ascontiguousarray(inp["x"][b], dtype=f)
    m["cT"] = np.ascontiguousarray(np.asarray(inp["c"][b], dtype=f).reshape(8, 128).T)
    m["posT"] = np.ascontiguousarray(np.asarray(inp["positions"][b], dtype=np.int32).reshape(NT, 128).T)
    m["w_ada"] = np.ascontiguousarray(inp["w_ada"][0], dtype=f)
    m["b_ada"] = np.ascontiguousarray(inp["b_ada"], dtype=f).reshape(1, -1)
    m["norm1_g"] = np.ascontiguousarray(inp["norm1_g"], dtype=f).reshape(1, -1)
    m["w_in"] = np.ascontiguousarray(inp["w_in"][0], dtype=f)
    m["ident"] = np.eye(128, dtype=f)
    m.update(CONSTS)
    m["q_norm_g"] = np.ascontiguousarray(inp["q_norm_g"], dtype=f).reshape(1, 64)
    m["w_cmp1"] = np.ascontiguousarray(inp["w_cmp1"][0], dtype=f)
    m["w_cmp2"] = np.ascontiguousarray(inp["w_cmp2"][0], dtype=f)
    m["peT"] = np.ascontiguousarray(np.asarray(inp["pe_cmp"][0], dtype=f).transpose(2, 0, 1))
    m["lbT"] = np.ascontiguousarray(np.asarray(inp["lb_logits"], dtype=f).reshape(2, 4, 128).transpose(2, 0, 1))
    m["norm2_g"] = np.ascontiguousarray(inp["norm2_g"], dtype=f).reshape(1, -1)
    m["w_out"] = np.ascontiguousarray(inp["w_out"][0], dtype=f)
    m["hgng"] = np.ascontiguousarray(inp["hg_norm_g"], dtype=f).reshape(128, 1)
    m["w_up"] = np.ascontiguousarray(inp["w_up"][0], dtype=f)
    m["w_down"] = np.ascontiguousarray(inp["w_down"][0], dtype=f)
    m["cwT"] = np.ascontiguousarray(np.asarray(inp["conv_w"][0], dtype=f).reshape(3, 44, 128).transpose(2, 0, 1))
    m["cbT"] = np.ascontiguousarray(np.asarray(inp["conv_b"][0], dtype=f).reshape(44, 128).T)
    m["k_norm_g"] = np.ascontiguousarray(inp["k_norm_g"], dtype=f).reshape(1, 192)
    return m


def run(inp, dbg=(), cores=8):
    nc = build(dbg)
    in_maps = [make_inputs(inp, b) for b in range(cores)]
    res = run_bass_kernel_spmd(nc, in_maps, core_ids=list(range(cores)))
    return res


def kernel(**inputs):
    inp = {k: np.asarray(v) for k, v in inputs.items()}
    res = run(inp)
    return np.stack([r["out"] for r in res.results], axis=0).astype(np.float32)
```

```python
import numpy as np
from contextlib import ExitStack
import concourse.bass as bass
import concourse.mybir as mybir
from concourse.bass_utils import run_bass_kernel_spmd

F32, BF16, I32 = mybir.dt.float32, mybir.dt.bfloat16, mybir.dt.int32
AF = mybir.ActivationFunctionType
ALU = mybir.AluOpType
AX = mybir.AxisListType

T = 2048
D = 1024
NT = 16
IN_COLS = 3352
DFF = 2816
EPS = 1e-6

ENGINES = ("pe", "act", "dve", "pool", "sp")
SAME_ENGINE_SYNC = {"pe": False, "act": True, "dve": True, "pool": True, "sp": False}
N_DMA_SEMS = 32


class Op:
    __slots__ = ("eng", "fn", "deps", "idx", "needs_inc", "val", "is_dma", "dsem", "dval")

    def __init__(self, eng, fn, is_dma):
        self.eng = eng
        self.fn = fn
        self.deps = []
        self.idx = None
        self.needs_inc = False
        self.val = None
        self.is_dma = is_dma
        self.dsem = None
        self.dval = None


class Sched:
    def __init__(self, nc):
        self.nc = nc
        self.q = {e: [] for e in ENGINES}
        self.last_w = {}
        self.readers = {}
        self.known = {e: {f: 0 for f in ENGINES} for e in ENGINES}
        self.known_dma = {e: set() for e in ENGINES}
        self.n_dma = 0
        self.dma_ops = []
        self.dma_by_q = {}

    def add(self, eng, fn, reads=(), writes=(), dma=False):
        op = Op(eng, fn, dma)
        op.idx = len(self.q[eng]) + 1
        deps = []
        for r in reads:
            w = self.last_w.get(r)
            if w is not None:
                deps.append(w)
        for w_ in writes:
            w = self.last_w.get(w_)
            if w is not None:
                deps.append(w)
            deps.extend(self.readers.get(w_, ()))
        kn = self.known[eng]
        kd = self.known_dma[eng]
        seen = set()
        for d in deps:
            if d is op or id(d) in seen:
                continue
            seen.add(id(d))
            if d.is_dma:
                if id(d) in kd:
                    continue
                kd.add(id(d))
                op.deps.append(d)
            else:
                if d.eng == eng and not SAME_ENGINE_SYNC[eng]:
                    continue
                if kn[d.eng] >= d.idx:
                    continue
                kn[d.eng] = d.idx
                d.needs_inc = True
                op.deps.append(d)
        if dma:
            lst = self.dma_by_q.setdefault(eng, [])
            n = len(lst)
            base = 0 if eng == "sp" else N_DMA_SEMS // 2
            half = N_DMA_SEMS // 2
            op.dsem = base + n % half
            op.dval = 16 * (n // half + 1)
            if n >= half:
                prev = lst[n - half]
                if id(prev) not in kd:
                    kd.add(id(prev))
                    op.deps.append(prev)
            lst.append(op)
            self.n_dma += 1
            self.dma_ops.append(op)
        self.q[eng].append(op)
        for r in reads:
            self.readers.setdefault(r, []).append(op)
        for w_ in writes:
            self.last_w[w_] = op
            self.readers[w_] = []
        return op

    def emit(self):
        nc = self.nc
        with ExitStack() as es:
            esem = {e: es.enter_context(nc.semaphore("s_" + e)) for e in ENGINES}
            dsem = [es.enter_context(nc.semaphore("d_%d" % i)) for i in range(N_DMA_SEMS)]
            for e in ENGINES:
                c = 0
                for op in self.q[e]:
                    if (not op.is_dma) and op.needs_inc:
                        c += 1
                        op.val = c
            block = es.enter_context(nc.Block())

            def replay(e):
                def run(eng):
                    for op in self.q[e]:
                        for d in op.deps:
                            if d.is_dma:
                                eng.wait_ge(dsem[d.dsem], d.dval)
                            else:
                                eng.wait_ge(esem[d.eng], d.val)
                        ins = op.fn(eng)
                        if op.is_dma:
                            ins.then_inc(dsem[op.dsem], 16)
                        elif op.needs_inc:
                            ins.then_inc(esem[e], 1)
                    if e == "sp":
                        for lst in self.dma_by_q.values():
                            for d in lst[-min(len(lst), N_DMA_SEMS // 2):]:
                                eng.wait_ge(dsem[d.dsem], d.dval)
                return run

            block.tensor(replay("pe"))
            block.scalar(replay("act"))
            block.vector(replay("dve"))
            block.gpsimd(replay("pool"))
            block.sync(replay("sp"))


def build(dbg=()):
    nc = bass.Bass("TRN2", target_bir_lowering=False)
    S = Sched(nc)
    dram_in = {}

    def din(name, shape, dt=F32):
        dram_in[name] = nc.dram_tensor(name, list(shape), dt, kind="ExternalInput").ap()
        return dram_in[name]

    x_d = din("x", [T, D])
    cT_d = din("cT", [128, 8])
    posT_d = din("posT", [128, NT], I32)
    wada_d = din("w_ada", [D, 6 * D])
    bada_d = din("b_ada", [1, 6 * D])
    n1g_d = din("norm1_g", [1, D])
    win_d = din("w_in", [D, IN_COLS])
    ident_d = din("ident", [128, 128])
    out_d = nc.dram_tensor("out", [T, D], F32, kind="ExternalOutput").ap()
    dbg_d = {}
    for spec in dbg:
        name, shape = spec[0], spec[1]
        ddt = BF16 if (len(spec) > 2 and spec[2] == "bf16") else F32
        dbg_d[name] = nc.dram_tensor("dbg_" + name, list(shape), ddt, kind="ExternalOutput").ap()

    es = ExitStack()

    def sb(name, shape, dt=F32):
        return es.enter_context(nc.sbuf_tensor("s_" + name, list(shape), dt))

    def ps(name, shape, dt=F32):
        return es.enter_context(nc.psum_tensor(name, list(shape), dt))

    def DMA(out, in_, r=(), w=(), eng="sp"):
        return S.add(eng, lambda e: e.dma_start(out=out, in_=in_), reads=r, writes=w, dma=True)

    def ACT(out, in_, func, bias=0.0, scale=1.0, accum=None, r=(), w=()):
        if accum is None:
            return S.add("act", lambda e: e.activation(out, in_, func, bias=bias, scale=scale), reads=r, writes=w)
        return S.add("act", lambda e: e.activation(out, in_, func, bias=bias, scale=scale, accum_out=accum),
                     reads=r, writes=w)

    def TS(out, in0, s1, s2, op0, op1=None, r=(), w=(), eng="dve"):
        if op1 is None:
            return S.add(eng, lambda e: e.tensor_scalar(out, in0, s1, None, op0), reads=r, writes=w)
        return S.add(eng, lambda e: e.tensor_scalar(out, in0, s1, s2, op0, op1), reads=r, writes=w)

    def TT(out, in0, in1, op, r=(), w=(), eng="dve"):
        return S.add(eng, lambda e: e.tensor_tensor(out, in0, in1, op), reads=r, writes=w)

    def STT(out, in0, scalar, in1, op0, op1, r=(), w=()):
        return S.add("dve", lambda e: e.scalar_tensor_tensor(out, in0, scalar, in1, op0, op1), reads=r, writes=w)

    def CP(out, in_, r=(), w=(), eng="dve"):
        return S.add(eng, lambda e: e.tensor_copy(out, in_), reads=r, writes=w)

    def RECIP(out, in_, r=(), w=()):
        return S.add("dve", lambda e: e.reciprocal(out, in_), reads=r, writes=w)

    def MEMSET(ap, val, w=(), eng="pool"):
        return S.add(eng, lambda e: e.memset(ap, val), writes=w)

    def MM(out, lhsT, rhs, start, stop, r=(), w=()):
        return S.add("pe", lambda e: e.matmul(out, lhsT, rhs, start=start, stop=stop), reads=r, writes=w)

    def TR(out, in_, ident, r=(), w=()):
        return S.add("pe", lambda e: e.transpose(out, in_, ident), reads=r, writes=w)

    PB = [ps("pb%d" % i, [128, 512], F32) for i in range(8)]
    pb_ctr = [0]

    def next_pb():
        i = pb_ctr[0] % 8
        pb_ctr[0] += 1
        return PB[i], ("pb", i)

    ident_f = sb("ident_f", [128, 128])
    ident_b = sb("ident_b", [128, 128], BF16)
    DMA(ident_f[:], ident_d, w=["ident_f"])
    CP(ident_b[:], ident_f[:], r=["ident_f"], w=["ident_b"])
    ones_row = sb("ones_row", [1, 128])
    MEMSET(ones_row[:], 1.0, w=["ones_row"])

    bar_scr = sb("bar_scr", [1, 64])
    hT = sb("hT", [128, 8, T], BF16)
    modT = sb("modT", [128, 48])
    SH1, GM1, GT1, SH2, GM2, GT2 = 0, 8, 16, 24, 32, 40

    with ExitStack() as esA:
        def sbA(name, shape, dt=F32):
            return esA.enter_context(nc.sbuf_tensor("s_" + name, list(shape), dt))

        modrow = sbA("modrow", [1, 6 * D])
        cT = sbA("cT", [128, 8])
        cs = sbA("cs", [128, 8])
        DMA(cT[:], cT_d, w=["cT"])
        ACT(cs[:], cT[:], AF.Silu, r=["cT"], w=["cs"])
        bada = sbA("bada", [1, 6 * D])
        DMA(bada[:], bada_d, w=["bada"])
        n1g = sbA("n1g", [1, D])
        DMA(n1g[:], n1g_d, w=["n1g"])
        n2g = sbA("n2g", [1, D])
        DMA(n2g[:], din("norm2_g", [1, D]), w=["n2g"])
        wa = [sbA("wa%d" % i, [128, 8, 512]) for i in range(2)]
        wada_v = wada_d.rearrange("(k p) n -> p k n", p=128)
        xs = [sbA("xs%d" % i, [128, D]) for i in range(3)]
        junk = sbA("junk", [128, D], BF16)
        xn = [sbA("xn%d" % i, [128, D], BF16) for i in range(2)]
        ss = sbA("ss", [128, NT])
        rstd = sbA("rstd", [128, NT])

        def x_tile(i):
            xb = xs[i % 3]
            xk = ("xs", i % 3)
            DMA(xb[:], x_d[128 * i:128 * (i + 1), :], w=[xk], eng="pool")
            ACT(junk[:], xb[:], AF.Square, accum=ss[:, i:i + 1], r=[xk], w=["junk", ("ss", i)])
            TS(rstd[:, i:i + 1], ss[:, i:i + 1], 1.0 / D, EPS, ALU.mult, ALU.add, r=[("ss", i)], w=[("rstd", i)])
            ACT(rstd[:, i:i + 1], rstd[:, i:i + 1], AF.Sqrt, r=[("rstd", i)], w=[("rstd", i)])
            RECIP(rstd[:, i:i + 1], rstd[:, i:i + 1], r=[("rstd", i)], w=[("rstd", i)])
            xnb = xn[i % 2]
            nk = ("xn", i % 2)
            TS(xnb[:], xb[:], rstd[:, i:i + 1], None, ALU.mult, r=[xk, ("rstd", i)], w=[nk])
            pb, pk = next_pb()
            pbb = pb[:].bitcast(BF16)
            for c in range(8):
                TR(pbb[:, 128 * c:128 * (c + 1)], xnb[:, 128 * c:128 * (c + 1)], ident_b[:],
                   r=[nk, "ident_b"], w=[pk])
            ACT(hT[:, :, 128 * i:128 * (i + 1)], pbb[:, :].rearrange("p (c t) -> p c t", c=8), AF.Copy,
                r=[pk], w=[("hT", i)])

        xt = 0
        for blk in range(12):
            wb = wa[blk % 2]
            key = ("wa", blk % 2)
            DMA(wb[:], wada_v[:, :, 512 * blk:512 * (blk + 1)], w=[key])
            for _ in range(2 if blk % 3 == 0 else 1):
                if xt < NT:
                    x_tile(xt)
                    xt += 1
            pb, pk = next_pb()
            for k in range(8):
                MM(pb[0:1, :], cs[:, k:k + 1], wb[:, k, :], k == 0, k == 7, r=[key, "cs"], w=[pk])
            TT(modrow[0:1, 512 * blk:512 * (blk + 1)], pb[0:1, :], bada[0:1, 512 * blk:512 * (blk + 1)],
               ALU.add, r=[pk, "bada"], w=[("modrow", blk)])
        while xt < NT:
            x_tile(xt)
            xt += 1
        mr_all = [("modrow", b) for b in range(12)]
        for (off, gsrc, gkey) in ((D, n1g, "n1g"), (4 * D, n2g, "n2g")):
            STT(modrow[0:1, off:off + D], modrow[0:1, off:off + D], 1.0, gsrc[0:1, :], ALU.add, ALU.mult,
                r=mr_all + [gkey], w=mr_all)
        one11 = ones_row[0:1, 0:1]
        pb, pk = next_pb()
        for j in range(48):
            MM(pb[:, j:j + 1], modrow[0:1, 128 * j:128 * (j + 1)], one11, True, True,
               r=mr_all + ["ones_row"], w=[pk])
        CP(modT[:], pb[:, 0:48], r=[pk], w=["modT"])
        hT_keys = [("hT", i) for i in range(NT)]
        for c in range(8):
            TS(hT[:, c, :], hT[:, c, :], modT[:, GM1 + c:GM1 + c + 1], modT[:, SH1 + c:SH1 + c + 1], ALU.mult, ALU.add,
               r=hT_keys + ["modT"], w=hT_keys)
        if "modrow" in dbg_d:
            DMA(dbg_d["modrow"], modrow[:], r=mr_all)
        if "hT" in dbg_d:
            hTf = sbA("hTf", [128, 8, T])
            CP(hTf[:], hT[:], r=[("hT", i) for i in range(NT)], w=["hTf"])
            DMA(dbg_d["hT"].rearrange("(c p) t -> p c t", p=128), hTf[:], r=["hTf"])

    bar_n = [0]

    def barrier():
        n = bar_n[0]
        bar_n[0] += 1
        o1 = ones_row[0:1, 0:1]
        pb, pk = next_pb()
        MM(pb[0:1, 0:1], o1, o1, True, True, r=["ones_row"], w=[pk, ("bar", "pe", n)])
        ACT(bar_scr[0:1, 0:1], o1, AF.Copy, r=["ones_row"], w=[("bar", "act", n)])
        CP(bar_scr[0:1, 8:9], o1, r=["ones_row"], w=[("bar", "dve", n)])
        CP(bar_scr[0:1, 16:17], o1, r=["ones_row"], w=[("bar", "pool", n)], eng="pool")
        allb = [("bar", e, n) for e in ("pe", "act", "dve", "pool")]
        pb, pk = next_pb()
        MM(pb[0:1, 0:1], o1, o1, True, True, r=allb, w=[pk])
        ACT(bar_scr[0:1, 1:2], o1, AF.Copy, r=allb, w=[("bar2", "act", n)])
        CP(bar_scr[0:1, 9:10], o1, r=allb, w=[("bar2", "dve", n)])
        CP(bar_scr[0:1, 17:18], o1, r=allb, w=[("bar2", "pool", n)], eng="pool")
        DMA(bar_scr[0:1, 32:40], ones_row[0:1, 0:8], r=allb, w=["bar_sp"])

    mixT = sb("mixT", [128, 8, T], BF16)
    hT_all = [("hT", i) for i in range(NT)]
    TWO_PI = 6.283185307179586
    HALF_PI = 1.5707963267948966
    win_v = win_d.rearrange("(k p) n -> p k n", p=128)

    barrier()
    with ExitStack() as esB:
        def sbB(name, shape, dt=F32):
            return esB.enter_context(nc.sbuf_tensor("s_" + name, list(shape), dt))

        tri_f = sbB("tri_f", [128, 128])
        tri_b = sbB("tri_b", [128, 128], BF16)
        anti_b = sbB("anti_b", [128, 128], BF16)
        am = sbB("am", [128, NT, 32])
        Mov = sbB("Mov", [128, 32])
        Qaug = [sbB("Qaug%d" % g, [128, NT, 4, 128], BF16) for g in range(2)]
        Ksaug = [sbB("Ksaug%d" % g, [128, T], BF16) for g in range(2)]
        KwT = [sbB("KwT%d" % g, [128, T], BF16) for g in range(2)]
        Vs = sbB("Vs", [128, NT, 2, 65], BF16)
        Vw = sbB("Vw", [128, NT, 2, 65], BF16)
        gsig = sbB("gsig", [128, NT, 24])
        kcmpT = [sbB("kcmpT%d" % g, [64, 128], BF16) for g in range(2)]
        vcmp = [sbB("vcmp%d" % g, [128, 64]) for g in range(2)]
        esB1 = ExitStack()

        def sbB1(name, shape, dt=F32):
            return esB1.enter_context(nc.sbuf_tensor("s_" + name, list(shape), dt))

        KcT = [sbB1("KcT%d" % g, [64, T], BF16) for g in range(2)]
        VcT = [sbB1("VcT%d" % g, [64, T], BF16) for g in range(2)]
        esB1a = ExitStack()

        def sbB1a(name, shape, dt=F32):
            return esB1a.enter_context(nc.sbuf_tensor("s_" + name, list(shape), dt))

        DMA(tri_f[:], din("tri", [128, 128]), w=["tri_f"])
        CP(tri_b[:], tri_f[:], r=["tri_f"], w=["tri_b"])
        TS(anti_b[:], tri_f[:], -1.0, 1.0, ALU.mult, ALU.add, r=["tri_f"], w=["anti_b"])
        DMA(am[:], din("am", [T, 32]).rearrange("(n p) j -> p n j", p=128), w=["am"])
        DMA(Mov[0:127, :], din("Mov", [127, 32]), w=["Mov"])
        invf = sbB1a("invf", [128, 8])
        DMA(invf[:], din("invf", [128, 8]), w=["invf"])
        qg8 = sbB1a("qg8", [128, 64])
        DMA(qg8[:], din("q_norm_g", [1, 64]).partition_broadcast(128), w=["qg8"])
        TS(qg8[:], qg8[:], 0.125, None, ALU.mult, r=["qg8"], w=["qg8"])
        kg = sbB1a("kg", [128, 3, 64])
        DMA(kg[:].rearrange("p a d -> p (a d)"), din("k_norm_g", [1, 192]).partition_broadcast(128), w=["kg"])

        posi = sbB1a("posi", [128, NT], I32)
        posf = sbB1a("posf", [128, NT])
        DMA(posi[:], posT_d, w=["posi"])
        CP(posf[:], posi[:], r=["posi"], w=["posf"])
        ang = sbB1a("ang", [128, NT, 8])
        TT(ang[:], posf[:].unsqueeze(2).to_broadcast([128, NT, 8]), invf[:].unsqueeze(1).to_broadcast([128, NT, 8]),
           ALU.mult, r=["posf", "invf"], w=["ang"])
        sinT = sbB1a("sinT", [128, NT, 8])
        cosT = sbB1a("cosT", [128, NT, 8])
        rr_t = sbB1a("rr_t", [128, NT, 8])
        rr_i = sbB1a("rr_i", [128, NT, 8], I32)
        rr_k = sbB1a("rr_k", [128, NT, 8])
        for (dst, shift, key) in ((sinT, 0.0, "sinT"), (cosT, HALF_PI, "cosT")):
            TS(rr_t[:], ang[:], shift, 1.0 / TWO_PI, ALU.add, ALU.mult, r=["ang"], w=["rr_t"])
            CP(rr_i[:], rr_t[:], r=["rr_t"], w=["rr_i"])
            CP(rr_k[:], rr_i[:], r=["rr_i"], w=["rr_k"])
            TS(rr_t[:], ang[:], shift, None, ALU.add, r=["ang"], w=["rr_t"])
            STT(rr_t[:], rr_k[:], -TWO_PI, rr_t[:], ALU.mult, ALU.add, r=["rr_k", "rr_t"], w=["rr_t"])
            TS(rr_k[:], rr_t[:], 3.141592653589793, None, ALU.is_gt, r=["rr_t"], w=["rr_k"])
            STT(rr_t[:], rr_k[:], -TWO_PI, rr_t[:], ALU.mult, ALU.add, r=["rr_k", "rr_t"], w=["rr_t"])
            TS(rr_k[:], rr_t[:], -3.141592653589793, None, ALU.is_lt, r=["rr_t"], w=["rr_k"])
            STT(rr_t[:], rr_k[:], TWO_PI, rr_t[:], ALU.mult, ALU.add, r=["rr_k", "rr_t"], w=["rr_t"])
            TS(rr_t[:], rr_t[:], -3.1415925, 3.1415925, ALU.max, ALU.min, r=["rr_t"], w=["rr_t"])
            ACT(dst[:], rr_t[:], AF.Sin, r=["rr_t"], w=[key])

        wq = sbB1a("wq", [128, 8, 512], BF16)
        w1b = sbB1a("w1b", [128, 8, 512], BF16)
        w2b = sbB1a("w2b", [128, 8, 280], BF16)
        DMA(wq[:], win_v[:, :, 2048:2560], w=["wq"], eng="pool")
        DMA(w1b[:], win_v[:, :, 2560:3072], w=["w1b"], eng="pool")
        DMA(w2b[:], win_v[:, :, 3072:3352], w=["w2b"], eng="pool")

        E_d = din("Eblk", [32, T])
        for g in range(2):
            MEMSET(Qaug[g][64:128, :, :, :], 0.0, w=[("Qm", g, i) for i in range(NT)])
            MEMSET(Ksaug[g][64:128, :], 0.0, w=[("KsZ", g), ("KsE", g)])
            MEMSET(KwT[g][64:128, :], 0.0, w=[("KwZ", g)])
            DMA(Ksaug[g][64:96, :], E_d, w=[("KsE", g)], eng="pool")
        MEMSET(Vs[:, :, :, 64:65], 1.0, w=["Vs1"])
        MEMSET(Vw[:, :, :, 64:65], 1.0, w=["Vw1"])

        qf = [sbB1a("qf%d" % i, [128, 8, 64]) for i in range(2)]
        kf = [sbB1a("kf%d" % i, [128, 6, 64]) for i in range(2)]
        sqt2 = [sbB1a("sqt%d" % j, [128, 8, 64]) for j in range(4)]
        ssq2 = [sbB1a("ssq%d" % j, [128, 8]) for j in range(4)]
        rt2 = [[sbB1a("rt%d_%d" % (j, i), [128, 8, 8]) for i in range(4)] for j in range(4)]
        qb = [sbB1a("qb%d" % i, [128, 8, 64], BF16) for i in range(2)]
        kb = [sbB1a("kb%d" % i, [128, 6, 64], BF16) for i in range(2)]
        epsb = sbB1a("epsb", [128, 1])
        MEMSET(epsb[:], EPS, w=["epsb"])

        def normrope(src, skey, nh, gains_bc, gkey, dst, dkey, i, par):
            sqt, ssq, rt = sqt2[par], ssq2[par], rt2[par]
            ksq, kss = ("sqt", par), ("ssq", par)
            krt = [("rt", par, j) for j in range(4)]
            ACT(sqt[:, 0:nh, :], src, AF.Square, r=[skey], w=[ksq])
            yield
            S.add("dve", lambda e: e.tensor_reduce(ssq[:, 0:nh], sqt[:, 0:nh, :], AX.X, ALU.add),
                  reads=[ksq], writes=[kss])
            yield
            ACT(ssq[:, 0:nh], ssq[:, 0:nh], AF.Ln, bias=epsb[:, 0:1], scale=1.0 / 64, r=[kss, "epsb"], w=[kss])
            ACT(ssq[:, 0:nh], ssq[:, 0:nh], AF.Exp, scale=-0.5, r=[kss], w=[kss])
            yield
            TT(src, src, ssq[:, 0:nh].unsqueeze(2).to_broadcast([128, nh, 64]), ALU.mult, r=[skey, kss], w=[skey])
            yield
            TT(src, src, gains_bc, ALU.mult, r=[skey, gkey], w=[skey])
            yield
            cb = cosT[:, i, :].unsqueeze(1).to_broadcast([128, nh, 8])
            sbb = sinT[:, i, :].unsqueeze(1).to_broadcast([128, nh, 8])
            x1 = src[:, :, 0:8]
            x2 = src[:, :, 8:16]
            ACT(dst, src, AF.Copy, r=[skey], w=[dkey])
            TT(rt[0][:, 0:nh, :], x1, cb, ALU.mult, r=[skey, "cosT"], w=[krt[0]])
            yield
            TT(rt[1][:, 0:nh, :], x2, sbb, ALU.mult, r=[skey, "sinT"], w=[krt[1]])
            yield
            TT(rt[2][:, 0:nh, :], x2, cb, ALU.mult, r=[skey, "cosT"], w=[krt[2]])
            yield
            TT(rt[3][:, 0:nh, :], x1, sbb, ALU.mult, r=[skey, "sinT"], w=[krt[3]])
            yield
            TT(dst[:, :, 0:8], rt[0][:, 0:nh, :], rt[1][:, 0:nh, :], ALU.subtract, r=[krt[0], krt[1], dkey], w=[dkey])
            yield
            TT(dst[:, :, 8:16], rt[2][:, 0:nh, :], rt[3][:, 0:nh, :], ALU.add, r=[krt[2], krt[3], dkey], w=[dkey])
            yield

        qg_bc = qg8[:].unsqueeze(1).to_broadcast([128, 8, 64])
        kg6 = sbB1a("kg6", [128, 3, 2, 64])
        CP(kg6[:, :, 0, :], kg[:], r=["kg"], w=["kg6"])
        CP(kg6[:, :, 1, :], kg[:], r=["kg"], w=["kg6"])

        def tile_gen(i):
            par = i % 2
            tsl = slice(128 * i, 128 * (i + 1))
            pq, pqk = PB[4 * par + 0], ("pb", 4 * par + 0)
            p1, p1k = PB[4 * par + 1], ("pb", 4 * par + 1)
            p2, p2k = PB[4 * par + 2], ("pb", 4 * par + 2)
            for k in range(8):
                MM(pq[:, :], hT[:, k, tsl], wq[:, k, :], k == 0, k == 7, r=[("hT", i), "wq"], w=[pqk])
            for k in range(8):
                MM(p1[:, :], hT[:, k, tsl], w1b[:, k, :], k == 0, k == 7, r=[("hT", i), "w1b"], w=[p1k])
            for k in range(8):
                MM(p2[:, 0:280], hT[:, k, tsl], w2b[:, k, :], k == 0, k == 7, r=[("hT", i), "w2b"], w=[p2k])
            yield
            qfb, qfk = qf[par], ("qf", par)
            kfb, kfk = kf[par], ("kf", par)
            ACT(qfb[:].rearrange("p h d -> p (h d)"), pq[:, :], AF.Copy, r=[pqk], w=[qfk])
            ACT(kfb[:, 0:2, :].rearrange("p h d -> p (h d)"), p1[:, 0:128], AF.Copy, r=[p1k], w=[kfk])
            ACT(kfb[:, 2:4, :].rearrange("p h d -> p (h d)"), p1[:, 256:384], AF.Copy, r=[p1k], w=[kfk])
            ACT(kfb[:, 4:6, :].rearrange("p h d -> p (h d)"), p2[:, 0:128], AF.Copy, r=[p2k], w=[kfk])
            yield
            ACT(Vs[:, i, :, 0:64], p1[:, 384:512].rearrange("p (g d) -> p g d", g=2), AF.Copy, r=[p1k], w=[("Vs", i)])
            ACT(Vw[:, i, :, 0:64], p2[:, 128:256].rearrange("p (g d) -> p g d", g=2), AF.Copy, r=[p2k], w=[("Vw", i)])
            ACT(gsig[:, i, :], p2[:, 256:280], AF.Exp, scale=-1.0, r=[p2k], w=[("gsig", i)])
            yield
            TS(gsig[:, i, :], gsig[:, i, :], 1.0, None, ALU.add, r=[("gsig", i)], w=[("gsig", i)])
            RECIP(gsig[:, i, :], gsig[:, i, :], r=[("gsig", i)], w=[("gsig", i)])
            yield
            qbb, qbk = qb[par], ("qb", par)
            kbb, kbk = kb[par], ("kb", par)
            g1 = normrope(qfb[:], qfk, 8, qg_bc, "qg8", qbb[:], qbk, i, 2 * par)
            g2 = normrope(kfb[:], kfk, 6, kg6[:].rearrange("p a g d -> p (a g) d"), "kg6", kbb[:], kbk, i, 2 * par + 1)
            live = [g1, g2]
            while live:
                for g_ in list(live):
                    if next(g_, "done") == "done":
                        live.remove(g_)
                yield
            ptb = pq[:].bitcast(BF16)
            for h in range(8):
                TR(ptb[0:64, 128 * h:128 * (h + 1)], qbb[:, h, :], ident_b[:], r=[qbk, "ident_b"], w=[pqk])
            pt2b = p1[:].bitcast(BF16)
            for s_ in range(6):
                TR(pt2b[0:64, 128 * s_:128 * (s_ + 1)], kbb[:, s_, :], ident_b[:], r=[kbk, "ident_b"], w=[p1k])
            yield
            for g in range(2):
                ACT(Qaug[g][0:64, i, :, :], ptb[0:64, 512 * g:512 * (g + 1)].rearrange("p (h t) -> p h t", h=4),
                    AF.Copy, r=[pqk], w=[("Q", g, i)])
            yield
            for g in range(2):
                CP(KcT[g][:, tsl], pt2b[0:64, 128 * g:128 * (g + 1)], r=[p1k], w=[("KcT", g, i)])
                CP(Ksaug[g][0:64, tsl], pt2b[0:64, 128 * (2 + g):128 * (3 + g)], r=[p1k], w=[("Ks", g, i)])
                CP(KwT[g][0:64, tsl], pt2b[0:64, 128 * (4 + g):128 * (5 + g)], r=[p1k], w=[("Kw", g, i)])
            yield

        nxt = 0
        active = []
        while nxt < NT or active:
            while len(active) < 2 and nxt < NT:
                active.append(tile_gen(nxt))
                nxt += 1
            for g_ in list(active):
                if next(g_, "done") == "done":
                    active.remove(g_)
        for blk in range(4):
            bsl = slice(512 * blk, 512 * (blk + 1))
            for g in range(2):
                pv, pvk = next_pb()
                for k in range(8):
                    MM(pv[0:64, :], w1b[:, k, 128 + 64 * g:128 + 64 * (g + 1)], hT[:, k, bsl], k == 0, k == 7,
                       r=hT_all + ["w1b"], w=[pvk])
                ACT(VcT[g][:, bsl], pv[0:64, :], AF.Copy, r=[pvk], w=[("VcT", g)])

        def dump(name, ap, keys):
            if name in dbg_d:
                DMA(dbg_d[name], ap, r=keys)

        dump("Q0", Qaug[0][0:64, :, :, :], [("Q", 0, i) for i in range(NT)])
        dump("Ks1", Ksaug[1][0:96, :], [("Ks", 1, i) for i in range(NT)] + [("KsE", 1)])
        dump("Kw0", KwT[0][0:64, :], [("Kw", 0, i) for i in range(NT)])
        dump("Vc1", VcT[1][:], [("VcT", 1)])
        dump("Vs", Vs[:], [("Vs", i) for i in range(NT)] + ["Vs1"])
        dump("gsig", gsig[:], [("gsig", i) for i in range(NT)])
        esB1a.close()
        barrier()
        w1c = [sbB1("w1c%d" % kv, [64, 32, 256], BF16) for kv in range(2)]
        w2c = [sbB1("w2c%d" % kv, [128, 2, 64], BF16) for kv in range(2)]
        wc1_d = din("w_cmp1", [2, 2048, 256])
        wc2_d = din("w_cmp2", [2, 256, 64])
        for kv in range(2):
            DMA(w1c[kv][:], wc1_d[kv].rearrange("(l d) c -> d l c", d=64), w=[("w1c", kv)], eng="pool")
            DMA(w2c[kv][:], wc2_d[kv].rearrange("(cc c) d -> c cc d", c=128), w=[("w2c", kv)], eng="pool")
        peTf = sbB1("peTf", [64, 2, 32])
        peTb = sbB1("peTb", [64, 2, 32], BF16)
        DMA(peTf[:], din("peT", [64, 2, 32]), w=["peTf"])
        CP(peTb[:], peTf[:], r=["peTf"], w=["peTb"])
        onesb = sbB1("onesb", [1, 128], BF16)
        MEMSET(onesb[:], 1.0, w=["onesb"])
        brow = sbB1("brow", [1, 512], BF16)
        pbi, pbik = next_pb()
        for kv in range(2):
            for l in range(32):
                MM(pbi[0:1, 256 * kv:256 * (kv + 1)], peTb[:, kv, l:l + 1], w1c[kv][:, l, :], l == 0, l == 31,
                   r=[("w1c", kv), "peTb"], w=[pbik])
        CP(brow[:], pbi[0:1, 0:512], r=[pbik], w=["brow"])
        hc = [sbB1("hc%d" % j, [128, 256], BF16) for j in range(2)]
        hcT = [sbB1("hcT%d" % j, [128, 2, 128], BF16) for j in range(2)]
        KcT_keys = lambda g: [("KcT", g, i) for i in range(NT)]
        cnt = 0
        for g in range(2):
            for kv in range(2):
                src = KcT[g] if kv == 0 else VcT[g]
                skeys = KcT_keys(g) if kv == 0 else [("VcT", g)]
                hcb, hck = hc[cnt % 2], ("hc", cnt % 2)
                htb, htk = hcT[cnt % 2], ("hcT", cnt % 2)
                cnt += 1
                ph, phk = next_pb()
                for l in range(32):
                    MM(ph[0:127, 0:256], src[:, l:l + 16 * 126 + 1:16], w1c[kv][:, l, :], l == 0, False,
                       r=skeys + [("w1c", kv)], w=[phk])
                MM(ph[0:127, 0:256], onesb[0:1, 0:127], brow[0:1, 256 * kv:256 * (kv + 1)], False, True,
                   r=["onesb", "brow"], w=[phk])
                ACT(hcb[0:127, :], ph[0:127, 0:256], AF.Silu, r=[phk], w=[hck])
                pt, ptk = next_pb()
                ptb = pt[:].bitcast(BF16)
                for cc in range(2):
                    TR(ptb[:, 128 * cc:128 * cc + 127], hcb[0:127, 128 * cc:128 * (cc + 1)], ident_b[0:127, 0:127],
                       r=[hck, "ident_b"], w=[ptk])
                CP(htb[:, :, 0:127], ptb[:, 0:256].rearrange("p (c n) -> p c n", c=2)[:, :, 0:127], r=[ptk], w=[htk])
                po, pok = next_pb()
                if kv == 0:
                    for cc in range(2):
                        MM(po[0:64, 0:127], w2c[0][:, cc, :], htb[:, cc, 0:127], cc == 0, cc == 1,
                           r=[htk, ("w2c", 0)], w=[pok])
                    CP(kcmpT[g][:, 0:127], po[0:64, 0:127], r=[pok], w=[("kcmpT", g)])
                else:
                    for cc in range(2):
                        MM(po[0:127, 0:64], htb[:, cc, 0:127], w2c[1][:, cc, :], cc == 0, cc == 1,
                           r=[htk, ("w2c", 1)], w=[pok])
                    CP(vcmp[g][0:127, :], po[0:127, 0:64], r=[pok], w=[("vcmp", g)])
        dump("kcmpT1", kcmpT[1][:, 0:127], [("kcmpT", 1)])
        dump("vcmp0", vcmp[0][0:127, :], [("vcmp", 0)])
        esB1.close()
        barrier()
        esB3 = ExitStack()

        def sbB3(name, shape, dt=F32):
            return esB3.enter_context(nc.sbuf_tensor("s_" + name, list(shape), dt))

        cv_d = din("cvalid", [T, 128])
        cvt = [sbB3("cvt%d" % j, [128, 128]) for j in range(3)]
        esb2 = [sbB3("esb%d" % j, [128, 4, 128]) for j in range(2)]
        mx2 = [sbB3("mx%d" % j, [128, 4]) for j in range(2)]
        sm2 = [sbB3("sm%d" % j, [128, 4]) for j in range(2)]
        pTs2 = [sbB3("pTs%d" % j, [128, 4, 128], BF16) for j in range(2)]
        pbf2 = [sbB3("pbf%d" % j, [128, 4, 128], BF16) for j in range(2)]
        psm2 = [sbB3("psm%d" % j, [128, 128]) for j in range(2)]
        psT2 = [sbB3("psT%d" % j, [128, 128]) for j in range(2)]
        vcmpb = [sbB3("vcmpb%d" % g, [128, 64], BF16) for g in range(2)]
        for g in range(2):
            CP(vcmpb[g][0:127, :], vcmp[g][0:127, :], r=[("vcmp", g)], w=[("vcmpb", g)])
        impv2 = [sbB3("impv%d" % j, [128, 32]) for j in range(2)]
        cmp32 = [sbB3("cmp3_%d" % j, [128, 32, 32]) for j in range(2)]
        rank2 = [sbB3("rank%d" % j, [128, 32]) for j in range(2)]
        negm2 = [sbB3("negm%d" % j, [128, 96], BF16) for j in range(2)]
        for j in range(2):
            MEMSET(negm2[j][:], 0.0, w=[("negm", j)])
        ptS = [sbB3("ptS%d" % j, [128, 4, 128], BF16) for j in range(6)]
        ocs = [sbB3("ocs%d" % j, [128, 4, 64]) for j in range(3)]
        acc = sbB3("acc", [128, 4, 64])
        tmpa = sbB3("tmpa", [128, 4, 64])
        tmpb = sbB3("tmpb", [128, 4, 64])
        rd = sbB3("rd", [128, 2, 4])
        Ocp = [[sbB3("Ocp%d_%d" % (a_, b_), [128, 260]) for b_ in range(2)] for a_ in range(2)]
        onsa = [sbB3("onsa%d" % j, [128, 8, 64], BF16) for j in range(2)]
        st_ctr = [0]
        cm_ctr = [0]

        def st_pb():
            j = 2 + st_ctr[0] % 4
            st_ctr[0] += 1
            return PB[j], ("pb", j)

        def cm_pb():
            j = 6 + cm_ctr[0] % 2
            cm_ctr[0] += 1
            return PB[j], ("pb", j)

        torder = list(range(NT))
        tpos = {t_: p_ for p_, t_ in enumerate(torder)}
        steps = [(i, g) for i in torder for g in range(2)]

        def cmp_gen(i, g, step):
            sp_ = step % 2
            esb, mx, sm, pTs = esb2[sp_], mx2[sp_], sm2[sp_], pTs2[sp_]
            pbf, psm, psT = pbf2[sp_], psm2[sp_], psT2[sp_]
            kB, kPS, kPT = ("pbf", sp_), ("psm", sp_), ("psT", sp_)
            impv, cmp3, rank, negm = impv2[sp_], cmp32[sp_], rank2[sp_], negm2[sp_]
            kE, kM, kS, kP = ("esb", sp_), ("mx", sp_), ("sm", sp_), ("pTs", sp_)
            kI, kC, kR, kN = ("impv", sp_), ("cmp3", sp_), ("rank", sp_), ("negm", sp_)
            cvb = cvt[tpos[i] % 3]
            cvk = ("cvt", tpos[i] % 3)
            if g == 0:
                DMA(cvb[:], cv_d[128 * i:128 * (i + 1), :], w=[cvk])
            ocb = ocs[step % 3]
            ock = ("ocs", step % 3)
            qkeys = [("Q", g, i)]
            bank, bk = PB[6 + sp_], ("pb", 6 + sp_)
            for hh in range(4):
                MM(bank[:, 128 * hh:128 * hh + 127], Qaug[g][0:64, i, hh, :], kcmpT[g][:, 0:127], True, True,
                   r=qkeys + [("kcmpT", g)], w=[bk])
            yield
            pscv = bank[:, :].rearrange("p (h n) -> p h n", h=4)[:, :, 0:127]
            S.add("dve", lambda e, a=pscv: e.tensor_reduce(mx[:], a, AX.X, ALU.max), reads=[bk], writes=[kM])
            TS(mx[:], mx[:], -1.0, None, ALU.mult, r=[kM], w=[kM])
            yield
            for hh in range(4):
                ACT(esb[:, hh, 0:127], bank[:, 128 * hh:128 * hh + 127], AF.Exp, bias=mx[:, hh:hh + 1],
                    r=[bk, kM], w=[kE])
            yield
            ev = esb[:, :, 0:127]
            TT(ev, ev, cvb[:, 0:127].unsqueeze(1).to_broadcast([128, 4, 127]), ALU.mult, r=[kE, cvk], w=[kE])
            S.add("dve", lambda e, a=ev: e.tensor_reduce(sm[:], a, AX.X, ALU.add), reads=[kE], writes=[kS])
            yield
            TS(sm[:], sm[:], 1e-30, None, ALU.max, r=[kS], w=[kS])
            RECIP(sm[:], sm[:], r=[kS], w=[kS])
            TT(ev, ev, sm[:].unsqueeze(2).to_broadcast([128, 4, 127]), ALU.mult, r=[kE, kS], w=[kE])
            yield
            yield
            ACT(pbf[:, :, 0:127], ev, AF.Copy, r=[kE], w=[kB])
            if i >= 8:
                S.add("dve", lambda e, a=ev: e.tensor_reduce(psm[:, 0:127], a.rearrange("p h n -> p n h"), AX.X, ALU.add),
                      reads=[kE], writes=[kPS])
            yield
            yield
            yield
            bankb = bank[:].bitcast(BF16)
            for hh in range(4):
                TR(bankb[0:127, 128 * hh:128 * (hh + 1)], pbf[:, hh, 0:127], ident_b[:], r=[kB, "ident_b"], w=[bk])
            yield
            CP(pTs[0:127, :, :], bankb[0:127, 0:512].rearrange("p (h t) -> p h t", h=4), r=[bk], w=[kP])
            yield
            for hh in range(4):
                MM(bank[:, 64 * hh:64 * (hh + 1)], pTs[0:127, hh, :], vcmpb[g][0:127, :], True, True,
                   r=[kP, ("vcmpb", g)], w=[bk])
            yield
            ACT(ocb[:].rearrange("p h d -> p (h d)"), bank[:, 0:256], AF.Copy, r=[bk], w=[ock])
            if i >= 8:
                yield
                TR(bank[0:127, 0:128], psm[:, 0:127], ident_f[:], r=[kPS, "ident_f"], w=[bk])
                yield
                CP(psT[0:127, :], bank[0:127, 0:128], r=[bk], w=[kPT])
                yield
                MM(bank[:, 0:32], psT[0:127, :], Mov[0:127, :], True, True, r=[kPT, "Mov"], w=[bk])
                yield
                TT(impv[:], bank[:, 0:32], am[:, i, :], ALU.add, r=[bk, "am"], w=[kI])
                TT(cmp3[:], impv[:].unsqueeze(1).to_broadcast([128, 32, 32]),
                   impv[:].unsqueeze(2).to_broadcast([128, 32, 32]), ALU.is_gt, r=[kI], w=[kC])
                yield
                S.add("dve", lambda e: e.tensor_reduce(rank[:], cmp3[:], AX.X, ALU.add), reads=[kC], writes=[kR])
                TS(negm[:, 64:96], rank[:], 16.0, -30000.0, ALU.is_ge, ALU.mult, r=[kR], w=[kN])
                yield
                yield
                yield
                yield
                pmb = bank[:].bitcast(BF16)
                TR(pmb[0:96, 0:128], negm[:, 0:96], ident_b[:], r=[kN, "ident_b"], w=[bk])
                yield
                CP(Qaug[g][64:96, i, :, :], pmb[64:96, 0:128].unsqueeze(1).to_broadcast([32, 4, 128]),
                   r=[bk], w=[("Qm", g, i)])

        pt_ctr = [0]

        def attn_S(it):
            (n, i, g, br, kj, kjs) = it
            ksl = slice(128 * kj, 128 * (kj + 1))
            qkeys = [("Q", g, i)]
            pS, pSk = st_pb()
            if br == "win":
                MM(pS[:, :], KwT[g][0:128, ksl], Qaug[g][0:128, i, :, :].rearrange("p h t -> p (h t)"),
                   True, True, r=qkeys + [("Qm", g, i), ("Kw", g, kj), ("KwZ", g)], w=[pSk])
            else:
                MM(pS[:, :], Ksaug[g][0:128, ksl], Qaug[g][0:128, i, :, :].rearrange("p h t -> p (h t)"),
                   True, True, r=qkeys + [("Qm", g, i), ("Ks", g, kj), ("KsE", g), ("KsZ", g)], w=[pSk])
            pb_ = ptS[pt_ctr[0] % 6]
            pbk_ = ("ptS", pt_ctr[0] % 6)
            pt_ctr[0] += 1
            ACT(pb_[:].rearrange("p h t -> p (h t)"), pS[:, :], AF.Exp, r=[pSk], w=[pbk_])
            if kj == i:
                TT(pb_[:], pb_[:], tri_b[:].unsqueeze(1).to_broadcast([128, 4, 128]), ALU.mult,
                   r=[pbk_, "tri_b"], w=[pbk_])
            elif br == "win" and kj == i - 4:
                TT(pb_[:], pb_[:], anti_b[:].unsqueeze(1).to_broadcast([128, 4, 128]), ALU.mult,
                   r=[pbk_, "anti_b"], w=[pbk_])
            return (pb_, pbk_)

        def attn_PV(it, pbp):
            (n, i, g, br, kj, kjs) = it
            pb_, pbk_ = pbp
            Os, Osk = PB[0], ("pb", 0)
            Ow, Owk = PB[1], ("pb", 1)
            Osv = Os[:, 0:260].rearrange("p (h e) -> p h e", h=4)
            Owv = Ow[:, 0:260].rearrange("p (h e) -> p h e", h=4)
            if br == "win":
                O_, Ok_, V_, Vk_, vone = Owv, Owk, Vw, ("Vw", kj), "Vw1"
            else:
                O_, Ok_, V_, Vk_, vone = Osv, Osk, Vs, ("Vs", kj), "Vs1"
            for hh in range(4):
                MM(O_[:, hh, :], pb_[:, hh, :], V_[:, kj, g, :], (kj == kjs[0] and hh == 0), (kj == kjs[-1] and hh == 3),
                   r=[pbk_, Vk_, vone], w=[Ok_])

        items = []
        for n, (i, g) in enumerate(steps):
            for br in ("win", "slc"):
                kjs = list(range(max(0, i - 4), i + 1)) if br == "win" else list(range(0, i + 1))
                for kj in kjs:
                    items.append((n, i, g, br, kj, kjs))
        pend_tr = []

        def flush_tr():
            while pend_tr:
                (onb_, onk_, tsl_, i_) = pend_tr.pop(0)
                pto, ptok = st_pb()
                ptob = pto[:].bitcast(BF16)
                onf = onb_[:].rearrange("p h d -> p (h d)")
                for c in range(4):
                    TR(ptob[:, 128 * c:128 * (c + 1)], onf[:, 128 * c:128 * (c + 1)], ident_b[:], r=[onk_, "ident_b"],
                       w=[ptok])
                ACT(mixT[:, 4:8, tsl_], ptob[:, 0:512].rearrange("p (c t) -> p c t", c=4), AF.Copy, r=[ptok],
                    w=[("mixT", 1, i_)])

        def mk_gen(n_):
            if n_ < len(steps):
                return cmp_gen(steps[n_][0], steps[n_][1], n_)
            return None

        for _ in cmp_gen(steps[0][0], steps[0][1], 0):
            pass
        gen_a = mk_gen(1)
        gen_b = mk_gen(2)
        LOOK = 3
        pendq = [attn_S(items[j_]) for j_ in range(min(LOOK, len(items)))]
        idx = 0
        tick = 0
        for n, (i, g) in enumerate(steps):
            tsl = slice(128 * i, 128 * (i + 1))
            onb = onsa[tpos[i] % 2]
            onk = ("onsa", tpos[i] % 2)
            n_in_step = 0
            while idx < len(items) and items[idx][0] == n:
                cur = items[idx]
                curp = pendq.pop(0)
                last_of_step = (idx + 1 >= len(items)) or (items[idx + 1][0] != n)
                if last_of_step and gen_a is not None:
                    for _ in gen_a:
                        pass
                    gen_a = None
                if idx + LOOK < len(items):
                    nxt_it = items[idx + LOOK]
                    if nxt_it[0] != n and nxt_it[3] == "slc" and gen_a is not None and nxt_it[0] == n + 1:
                        for _ in gen_a:
                            pass
                        gen_a = None
                    pendq.append(attn_S(nxt_it))
                attn_PV(cur, curp)
                tick += 1
                n_in_step += 1
                if n_in_step == 6:
                    flush_tr()
                if gen_a is not None:
                    if next(gen_a, "done") == "done":
                        gen_a = None
                if gen_b is not None:
                    if next(gen_b, "done") == "done":
                        gen_b = None
                idx += 1
            if gen_a is not None:
                for _ in gen_a:
                    pass
            gen_a = gen_b
            gen_b = mk_gen(n + 3)
            ocb = ocs[n % 3]
            ock = ("ocs", n % 3)
            gs0 = gsig[:, i, 12 * g + 0:12 * g + 12:3]
            gs1 = gsig[:, i, 12 * g + 1:12 * g + 12:3]
            gs2 = gsig[:, i, 12 * g + 2:12 * g + 12:3]
            Osb, Owb = Ocp[n % 2][0], Ocp[n % 2][1]
            Osk, Owk = ("Ocp", n % 2, 0), ("Ocp", n % 2, 1)
            ACT(Osb[:], PB[0][:, 0:260], AF.Copy, r=[("pb", 0)], w=[Osk])
            ACT(Owb[:], PB[1][:, 0:260], AF.Copy, r=[("pb", 1)], w=[Owk])
            Osv = Osb[:].rearrange("p (h e) -> p h e", h=4)
            Owv = Owb[:].rearrange("p (h e) -> p h e", h=4)
            RECIP(rd[:, 0, :], Osv[:, :, 64], r=[Osk], w=["rd"])
            RECIP(rd[:, 1, :], Owv[:, :, 64], r=[Owk], w=["rd"])
            TT(rd[:, 0, :], rd[:, 0, :], gs1, ALU.mult, r=["rd", ("gsig", i)], w=["rd"])
            TT(rd[:, 1, :], rd[:, 1, :], gs2, ALU.mult, r=["rd", ("gsig", i)], w=["rd"])
            TT(acc[:], ocb[:], gs0.unsqueeze(2).to_broadcast([128, 4, 64]), ALU.mult, r=[ock, ("gsig", i)], w=["acc"],
               eng="pool")
            TT(tmpa[:], Osv[:, :, 0:64], rd[:, 0, :].unsqueeze(2).to_broadcast([128, 4, 64]), ALU.mult,
               r=[Osk, "rd"], w=["tmpa"])
            TT(tmpb[:], Owv[:, :, 0:64], rd[:, 1, :].unsqueeze(2).to_broadcast([128, 4, 64]), ALU.mult,
               r=[Owk, "rd"], w=["tmpb"])
            TT(acc[:], acc[:], tmpa[:], ALU.add, r=["acc", "tmpa"], w=["acc"], eng="pool")
            TT(onb[:, 4 * g:4 * g + 4, :], acc[:], tmpb[:], ALU.add, r=["acc", "tmpb"], w=[onk], eng="pool")
            if g == 1:
                pend_tr.append((onb, onk, tsl, i))
        flush_tr()
        dump("onsaT", mixT[:, 4:8, :], [("mixT", 1, i) for i in range(NT)])
        esB3.close()
    barrier()
    with ExitStack() as esC:
        def sbC(name, shape, dt=F32):
            return esC.enter_context(nc.sbuf_tensor("s_" + name, list(shape), dt))

        TQ = 512
        NTQ = 4
        NCQ = 8
        FQ = 4 * TQ
        lbl = sbC("lbl", [128, 2, 4])
        DMA(lbl[:], din("lbT", [128, 2, 4]), w=["lbl"])
        lb = sbC("lb", [128, 4])
        oml = sbC("oml", [128, 4])
        noml = sbC("noml", [128, 4])
        TT(lb[:], lbl[:, 0, :], lbl[:, 1, :], ALU.subtract, r=["lbl"], w=["lb"])
        ACT(lb[:], lb[:], AF.Sigmoid, r=["lb"], w=["lb"])
        TS(oml[:], lb[:], -1.0, 1.0, ALU.mult, ALU.add, r=["lb"], w=["oml"])
        TS(noml[:], oml[:], -1.0, None, ALU.mult, r=["oml"], w=["noml"])
        bmask = sbC("bmask", [128, 128], BF16)
        DMA(bmask[:], din("bmask", [128, 128]), w=["bmask"], eng="pool")
        ones_f = sbC("ones_f", [128, 128])
        MEMSET(ones_f[:], 1.0, w=["ones_f"])
        eps_c = sbC("eps_c", [128, 1])
        MEMSET(eps_c[:], EPS, w=["eps_c"])
        rmask = sbC("rmask", [128, FQ], BF16)
        MEMSET(rmask[:], 1.0, w=["rmask"])
        MEMSET(rmask[:, 0:FQ:64], 0.0, w=["rmask"])
        wh = sbC("wh", [128, 8, 4, 512], BF16)
        for j in range(4):
            DMA(wh[:, :, j, :], win_v[:, :, 512 * j:512 * (j + 1)], w=[("wh", j)], eng="pool")
        sg = sbC("sg", [128, 4, TQ])
        lf = sbC("lf", [128, 4, TQ])
        kk = sbC("kk", [128, 4, TQ])
        bb = sbC("bb", [128, 4, TQ])
        qq = sbC("qq", [128, 4, TQ])
        t1 = sbC("t1", [128, 4, TQ])
        qe = sbC("qe", [128, 4, TQ], BF16)
        qd = sbC("qd", [128, 4, TQ], BF16)
        kd = sbC("kd", [128, 4, TQ], BF16)
        kdz = sbC("kdz", [128, 4, TQ], BF16)
        gg = sbC("gg", [128, 4, TQ], BF16)
        keT = sbC("keT", [128, 4, TQ], BF16)
        v_tm = sbC("v_tm", [128, NTQ, 512], BF16)
        ke_tm = sbC("ke_tm", [128, 4, NTQ, 128], BF16)
        oT32 = sbC("oT32", [128, 4, TQ])
        bmid = sbC("bmid", [128, 4 * NCQ])
        blast = sbC("blast", [128, 4 * NCQ])
        dec = sbC("dec", [128, 4 * NCQ])
        S32 = [sbC("S32_%d" % h, [128, 128]) for h in range(4)]
        Sbf = [[sbC("Sbf%d_%d" % (h, j), [128, 128], BF16) for j in range(2)] for h in range(4)]
        Am = [[sbC("Am%d_%d" % (h, j), [128, 128], BF16) for j in range(2)] for h in range(4)]
        for h in range(4):
            MEMSET(S32[h][:], 0.0, w=[("S32", h)])
            MEMSET(Sbf[h][0][:], 0.0, w=[("Sbf", h, 0)])
        sidx = [0, 0, 0, 0]

        def fl(t_):
            return t_[:].rearrange("p h t -> p (h t)")

        def v3(t_):
            return t_[:].rearrange("p h (c t) -> p (h c) t", t=64)

        for qt in range(4):
            t0 = qt * TQ
            qsl = slice(t0, t0 + TQ)
            hkeys = [("hT", 4 * qt + j) for j in range(4)]
            for (j, dst, func, dkey) in ((1, sg, AF.Sigmoid, "sg"), (0, qq, AF.Silu, "qq"), (3, gg, AF.Silu, "gg")):
                for h in range(4):
                    pp, ppk = next_pb()
                    for k in range(8):
                        MM(pp[:, :], wh[:, k, j, 128 * h:128 * (h + 1)], hT[:, k, qsl], k == 0, k == 7,
                           r=hkeys + [("wh", j)], w=[ppk])
                    ACT(dst[:, h, :], pp[:, :], func, r=[ppk], w=[(dkey, h // 2)])
            for i in range(NTQ):
                tsl = slice(t0 + 128 * i, t0 + 128 * (i + 1))
                pp, ppk = next_pb()
                for k in range(8):
                    MM(pp[:, :], hT[:, k, tsl], wh[:, k, 2, :], k == 0, k == 7, r=hkeys + [("wh", 2)], w=[ppk])
                ACT(v_tm[:, i, :], pp[:, :], AF.Copy, r=[ppk], w=["v_tm"])
            def e_chain(pr):
                hs = slice(2 * pr, 2 * pr + 2)
                HF = 2 * TQ
                cs_ = slice(2 * NCQ * pr, 2 * NCQ * (pr + 1))

                def fl2(t_):
                    return t_[:, hs, :].rearrange("p h t -> p (h t)")

                def v32(t_):
                    return t_[:, hs, :].rearrange("p h (c t) -> p (h c) t", t=64)

                K = lambda nm: (nm, pr)
                for h in range(2 * pr, 2 * pr + 2):
                    ACT(lf[:, h, :], sg[:, h, :], AF.Ln, bias=lb[:, h:h + 1], scale=oml[:, h:h + 1],
                        r=[K("sg"), "lb", "oml"], w=[K("lf")])
                    TS(kk[:, h, :], sg[:, h, :], noml[:, h:h + 1], oml[:, h:h + 1], ALU.mult, ALU.add,
                       r=[K("sg"), "noml", "oml"], w=[K("kk")])
                    yield
                S.add("dve", lambda e: e.tensor_tensor_scan(fl2(bb), rmask[:, 0:HF], fl2(lf), 0.0, ALU.mult, ALU.add),
                      reads=["rmask", K("lf")], writes=[K("bb")])
                yield
                ACT(fl2(t1), fl2(bb), AF.Exp, r=[K("bb")], w=[K("t1")])
                CP(bmid[:, cs_], fl2(bb)[:, 31:HF:64], r=[K("bb")], w=[K("bmid")])
                yield
                CP(blast[:, cs_], fl2(bb)[:, 63:HF:64], r=[K("bb")], w=[K("blast")])
                yield
                TT(fl2(qe), fl2(qq), fl2(t1), ALU.mult, r=[K("qq"), K("t1")], w=[K("qe")])
                ACT(dec[:, cs_], blast[:, cs_], AF.Exp, r=[K("blast")], w=[K("dec")])
                yield
                TT(v32(sg), v32(bb), bmid[:, cs_].unsqueeze(2).to_broadcast([128, 2 * NCQ, 64]), ALU.subtract,
                   r=[K("bb"), K("bmid"), K("sg")], w=[K("sg")])
                yield
                TS(fl2(lf), fl2(sg), -1.0, 75.0, ALU.mult, ALU.min, r=[K("sg"), K("lf")], w=[K("lf")])
                yield
                TS(fl2(sg), fl2(sg), 75.0, None, ALU.min, r=[K("sg")], w=[K("sg")])
                ACT(fl2(lf), fl2(lf), AF.Exp, r=[K("lf")], w=[K("lf")])
                yield
                ACT(fl2(sg), fl2(sg), AF.Exp, r=[K("sg")], w=[K("sg")])
                TT(fl2(kd), fl2(kk), fl2(lf), ALU.mult, r=[K("kk"), K("lf")], w=[K("kd")])
                yield
                TT(v32(t1), blast[:, cs_].unsqueeze(2).to_broadcast([128, 2 * NCQ, 64]), v32(bb), ALU.subtract,
                   r=[K("bb"), K("blast"), K("t1")], w=[K("t1")])
                ACT(fl2(kdz), fl2(kd), AF.Copy, r=[K("kd")], w=[K("kdz")])
                yield
                TT(fl2(qd), fl2(qq), fl2(sg), ALU.mult, r=[K("qq"), K("sg")], w=[K("qd")])
                ACT(fl2(t1), fl2(t1), AF.Exp, r=[K("t1")], w=[K("t1")])
                MEMSET(v32(kdz)[:, :, 32:64], 0.0, w=[K("kdz")])
                yield
                TT(fl2(keT), fl2(kk), fl2(t1), ALU.mult, r=[K("kk"), K("t1")], w=[K("keT")])
                yield
                for h in range(2 * pr, 2 * pr + 2):
                    pp, ppk = PB[4 + h], ("pb", 4 + h)
                    ppb = pp[:].bitcast(BF16)
                    for i in range(NTQ):
                        TR(ppb[:, 128 * i:128 * (i + 1)], keT[:, h, 128 * i:128 * (i + 1)], ident_b[:],
                           r=[K("keT"), "ident_b"], w=[ppk])
                    yield
                    CP(ke_tm[:, h, :, :], ppb[:, 0:512].rearrange("p (j k) -> p j k", j=4), r=[ppk], w=[K("ke_tm")])
                    yield

            live = [e_chain(0), e_chain(1)]
            while live:
                for g_ in list(live):
                    if next(g_, "done") == "done":
                        live.remove(g_)
            for i in range(NTQ):
                tl = slice(128 * i, 128 * (i + 1))
                for h in range(4):
                    pa, pak = PB[4 + h], ("pb", 4 + h)
                    pav = pa[:, 0:128].rearrange("p (c a t) -> p c a t", c=2, a=2)
                    qdv = qd[:, h, tl].rearrange("p (c a t) -> p c a t", c=2, a=2)
                    for cc_ in range(2):
                        MM(pav[:, cc_, 0, :], kdz[:, h, tl], qdv[:, cc_, 0, :], True, True, r=[("kdz", h // 2), ("qd", h // 2)], w=[pak])
                        MM(pav[:, cc_, 1, :], kd[:, h, tl], qdv[:, cc_, 1, :], True, True, r=[("kd", h // 2), ("qd", h // 2)], w=[pak])
                for h in range(4):
                    TT(Am[h][i % 2][:], PB[4 + h][:, 0:128], bmask[:], ALU.mult, r=[("pb", 4 + h), "bmask"],
                       w=[("Am", h, i % 2)])
                for h in range(4):
                    MM(PB[h][:, 0:128], v_tm[:, i, 128 * h:128 * (h + 1)], Am[h][i % 2][:], True, False,
                       r=["v_tm", ("Am", h, i % 2)], w=[("pb", h)])
                for cc in range(2):
                    c = 2 * i + cc
                    gc = NCQ * qt + c
                    rows = slice(64 * cc, 64 * (cc + 1))
                    for h in range(4):
                        MM(PB[h][:, 64 * cc:64 * (cc + 1)], Sbf[h][sidx[h] % 2][:], qe[:, h, 64 * c:64 * (c + 1)], False,
                           cc == 1, r=[("Sbf", h, sidx[h] % 2), ("qe", h // 2)], w=[("pb", h)])
                        if gc < 31:
                            MM(PB[4 + h][:, 0:128], ke_tm[rows, h, i, :], v_tm[rows, i, 128 * h:128 * (h + 1)], True, True,
                               r=[("ke_tm", h // 2), "v_tm"], w=[("pb", 4 + h)])
                    if gc < 31:
                        for h in range(4):
                            STT(S32[h][:], S32[h][:], dec[:, NCQ * h + c:NCQ * h + c + 1], PB[4 + h][:, 0:128], ALU.mult,
                                ALU.add, r=[("S32", h), ("dec", h // 2), ("pb", 4 + h)], w=[("S32", h)])
                        for h in range(4):
                            sidx[h] += 1
                            if h % 2 == 0:
                                ACT(Sbf[h][sidx[h] % 2][:], S32[h][:], AF.Copy, r=[("S32", h)], w=[("Sbf", h, sidx[h] % 2)])
                            else:
                                CP(Sbf[h][sidx[h] % 2][:], S32[h][:], r=[("S32", h)], w=[("Sbf", h, sidx[h] % 2)])
                for h in range(4):
                    ACT(oT32[:, h, tl], PB[h][:, 0:128], AF.Copy, r=[("pb", h)], w=["oT32"])
            def post_chain(pr):
                hs = slice(2 * pr, 2 * pr + 2)

                def fl2(t_):
                    return t_[:, hs, :].rearrange("p h t -> p (h t)")

                K = lambda nm: (nm, pr)
                ACT(fl2(sg), fl2(oT32), AF.Square, r=["oT32", K("sg")], w=[K("sg")])
                yield
                for h in range(2 * pr, 2 * pr + 2):
                    pn, pnk = PB[4 + h], ("pb", 4 + h)
                    MM(pn[:, :], ones_f[:], sg[:, h, :], True, True, r=["ones_f", K("sg")], w=[pnk])
                    yield
                    ACT(lf[:, h, :], pn[:, :], AF.Ln, bias=eps_c[:, 0:1], scale=1.0 / 128, r=[pnk, K("lf"), "eps_c"], w=[K("lf")])
                    yield
                ACT(fl2(lf), fl2(lf), AF.Exp, scale=-0.5, r=[K("lf")], w=[K("lf")])
                yield
                TT(fl2(lf), fl2(oT32), fl2(lf), ALU.mult, r=[K("lf"), "oT32"], w=[K("lf")])
                yield
                TT(mixT[:, hs, qsl], lf[:, hs, :], gg[:, hs, :], ALU.mult, r=[K("lf"), K("gg")], w=[("mixT", 0, qt, pr)])
                yield

            live = [post_chain(0), post_chain(1)]
            while live:
                for g_ in list(live):
                    if next(g_, "done") == "done":
                        live.remove(g_)
        if "ohgT" in dbg_d:
            DMA(dbg_d["ohgT"], mixT[:, 0:4, :], r=[("mixT", 0, q_, p_) for q_ in range(4) for p_ in range(2)])
    barrier()
    with ExitStack() as esD:
        def sbD(name, shape, dt=F32):
            return esD.enter_context(nc.sbuf_tensor("s_" + name, list(shape), dt))

        X1 = sbD("X1", [128, NT, D])
        gtbc = sbD("gtbc", [128, 2, D])
        ones_d = sbD("ones_d", [128, 128])
        MEMSET(ones_d[:], 1.0, w=["ones_d"])
        dg = [sbD("dg%d" % j, [128, 128]) for j in range(2)]
        gsbc2 = sbD("gsbc2", [128, 2, D])
        for gi, (col0, dstt, dkey_) in enumerate(((GT1, gtbc[:, 0, :], "gtbc"), (GT2, gtbc[:, 1, :], "gtbc"))):
            for hf in range(2):
                pp, ppk = next_pb()
                for cq in range(4):
                    c = 4 * hf + cq
                    dgb = dg[c % 2]
                    dgk = ("dg", c % 2)
                    TS(dgb[:], ident_f[:], modT[:, col0 + c:col0 + c + 1], None, ALU.mult, r=["ident_f", "modT"], w=[dgk])
                    MM(pp[:, 128 * cq:128 * (cq + 1)], ones_d[:], dgb[:], True, True, r=["ones_d", dgk], w=[ppk])
                CP(dstt[:, 512 * hf:512 * (hf + 1)], pp[:, :], r=[ppk], w=[dkey_])
        GROUPS = [(0, 4), (4, 8), (8, 12), (12, 16), (16, 19), (19, 22)]
        wu = [sbD("wu0", [128, 8, 2, 4, 128], BF16), None]
        wd = [sbD("wd0", [128, 4, D], BF16), None]
        wup_v = din("w_up", [D, 2 * DFF]).rearrange("(k p) n -> p k n", p=128)
        wdn_v = din("w_down", [DFF, D]).rearrange("(c p) n -> p c n", p=128)

        loaded = set()

        def grp(gi):
            c0, c1 = GROUPS[gi]
            return c0, c1 - c0, wu[gi % 2], ("wu", gi % 2), wd[gi % 2], ("wd", gi % 2)

        def load_wu(gi):
            if gi >= len(GROUPS) or ("wu", gi) in loaded:
                return
            loaded.add(("wu", gi))
            c0, ncg, wub, wuk, wdb, wdk = grp(gi)
            for s_ in range(2):
                DMA(wub[:, :, s_, 0:ncg, :],
                    wup_v[:, :, DFF * s_ + 128 * c0:DFF * s_ + 128 * (c0 + ncg)].rearrange("p k (c n) -> p k c n", n=128),
                    w=[wuk], eng="pool")

        def load_wd(gi):
            if gi >= len(GROUPS) or ("wd", gi) in loaded:
                return
            loaded.add(("wd", gi))
            c0, ncg, wub, wuk, wdb, wdk = grp(gi)
            DMA(wdb[:, 0:ncg, :], wdn_v[:, c0:c0 + ncg, :], w=[wdk], eng="pool")

        wo_holder = []
        ss2 = sbD("ss2", [128, NT])
        junk2 = sbD("junk2", [128, D], BF16)
        mix_all = [("mixT", 0, q_, p_) for q_ in range(4) for p_ in range(2)] + [("mixT", 1, i) for i in range(NT)]
        with ExitStack() as esD1:
            def sbD1(name, shape, dt=F32):
                return esD1.enter_context(nc.sbuf_tensor("s_" + name, list(shape), dt))

            wo = sbD1("wo", [128, 8, D], BF16)
            DMA(wo[:], din("w_out", [D, D]).rearrange("(c p) n -> p c n", p=128), w=["wo"], eng="pool")
            load_wu(0)
            load_wd(0)
            hgng = sbD1("hgng", [128, 1])
            DMA(hgng[:], din("hgng", [128, 1]), w=["hgng"])
            for c in range(8):
                if c < 4:
                    STT(wo[:, c, :], wo[:, c, :], hgng[:, 0:1], gtbc[:, 0, :], ALU.mult, ALU.mult,
                        r=["wo", "hgng", "gtbc"], w=["wo"])
                else:
                    TT(wo[:, c, :], wo[:, c, :], gtbc[:, 0, :], ALU.mult, r=["wo", "gtbc"], w=["wo"])
            xs2 = [sbD1("xs2_%d" % j, [128, D]) for j in range(2)]
            cnt = 0
            for i in range(NT):
                tsl = slice(128 * i, 128 * (i + 1))
                xb = xs2[i % 2]
                xk = ("xs2", i % 2)
                DMA(xb[:], x_d[tsl, :], w=[xk])
                for hf in range(2):
                    cs_ = slice(512 * hf, 512 * (hf + 1))
                    pp, ppk = next_pb()
                    for c in range(8):
                        MM(pp[:, :], mixT[:, c, tsl], wo[:, c, cs_], c == 0, c == 7, r=mix_all + ["wo"], w=[ppk])
                    TT(X1[:, i, cs_], pp[:, :], xb[:, cs_], ALU.add, r=[ppk, xk], w=[("X1", i)])
                ACT(junk2[:], X1[:, i, :], AF.Square, accum=ss2[:, i:i + 1], r=[("X1", i)], w=["junk2", ("ss2", i)])
        if "X1" in dbg_d:
            DMA(dbg_d["X1"].rearrange("(n p) d -> p n d", p=128), X1[:], r=[("X1", i) for i in range(NT)])
        barrier()
        xn2 = [sbD("xn2_%d" % j, [128, D], BF16) for j in range(2)]
        rstd2 = sbD("rstd2", [128, NT])
        wu[1] = sbD("wu1", [128, 8, 2, 4, 128], BF16)
        wd[1] = sbD("wd1", [128, 4, D], BF16)
        load_wu(1)
        load_wd(1)
        TS(rstd2[:], ss2[:], 1.0 / D, EPS, ALU.mult, ALU.add, r=[("ss2", i) for i in range(NT)], w=["rstd2"])
        ACT(rstd2[:], rstd2[:], AF.Sqrt, r=["rstd2"], w=["rstd2"])
        RECIP(rstd2[:], rstd2[:], r=["rstd2"], w=["rstd2"])
        for i in range(NT):
            xnb = xn2[i % 2]
            nk = ("xn2", i % 2)
            TS(xnb[:], X1[:, i, :], rstd2[:, i:i + 1], None, ALU.mult, r=[("X1", i), "rstd2"], w=[nk])
            pb, pk = next_pb()
            pbb = pb[:].bitcast(BF16)
            for c in range(8):
                TR(pbb[:, 128 * c:128 * (c + 1)], xnb[:, 128 * c:128 * (c + 1)], ident_b[:], r=[nk, "ident_b"], w=[pk])
            ACT(hT[:, :, 128 * i:128 * (i + 1)], pbb[:, :].rearrange("p (c t) -> p c t", c=8), AF.Copy,
                r=[pk], w=[("hT", i)])
        hT_keys2 = [("hT", i) for i in range(NT)]
        for c in range(8):
            TS(hT[:, c, :], hT[:, c, :], modT[:, GM2 + c:GM2 + c + 1], modT[:, SH2 + c:SH2 + c + 1], ALU.mult, ALU.add,
               r=hT_keys2 + ["modT"], w=hT_keys2)
        cw = sbD("cw", [128, 3, 44])
        cb = sbD("cb", [128, 44])
        DMA(cw[:], din("cwT", [128, 3, 44]), w=["cw"])
        DMA(cb[:], din("cbT", [128, 44]), w=["cb"])
        halo = sbD("halo", [128, 44, 2])
        MEMSET(halo[:], 0.0, w=[("halo", c) for c in range(44)])
        hid_t = gsbc2[:].rearrange("p a n -> p (a n)").bitcast(BF16).rearrange("p (j c t) -> p j c t", j=2, c=4)
        mixf = mixT[:].rearrange("p c t -> p (c t)").bitcast(F32)
        U = [[mixf[:, 516 * (3 * s_ + j):516 * (3 * s_ + j) + 514] for j in range(3)] for s_ in range(2)]
        UA = [[mixf[:, 3096 + 512 * (3 * s_ + j):3096 + 512 * (3 * s_ + j + 1)] for j in range(3)] for s_ in range(2)]
        SA = [mixf[:, 6168 + 512 * j:6168 + 512 * (j + 1)] for j in range(2)]
        pend_banks = set()
        ffn_ctr = [0]

        def ffn_pb(hold=False):
            for _ in range(8):
                j = ffn_ctr[0] % 8
                ffn_ctr[0] += 1
                if j not in pend_banks:
                    if hold:
                        pend_banks.add(j)
                    return PB[j], ("pb", j)
            raise RuntimeError("no free PSUM bank")

        jobs = []
        for gi, (c0, c1) in enumerate(GROUPS):
            for blk in range(4):
                for ci in range(c1 - c0):
                    jobs.append((gi, blk, ci))
        NJ = len(jobs)
        pps_of = {}
        wd_scaled = set()

        def st_up(k):
            gi, blk, ci = jobs[k]
            load_wu(gi)
            if blk == 0 and ci == 0:
                load_wu(gi + 1)
            c0, ncg, wub, wuk, wdb, wdk = grp(gi)
            bsl = slice(512 * blk, 512 * (blk + 1))
            res_ = []
            for s_ in range(2):
                pp, ppk = ffn_pb(hold=True)
                for kk_ in range(8):
                    MM(pp[:, :], wub[:, kk_, s_, ci, :], hT[:, kk_, bsl], kk_ == 0, kk_ == 7,
                       r=[("hT", 4 * blk + j) for j in range(4)] + [wuk], w=[ppk])
                res_.append((pp, ppk))
            pps_of[k] = res_

        def bufs(k):
            gi, blk, ci = jobs[k]
            c0 = GROUPS[gi][0]
            st = k % 3
            out = []
            for s_ in range(2):
                chidx = 22 * s_ + c0 + ci
                out.append((chidx, U[s_][st], ("U", s_, st), UA[s_][st], ("UA", s_, st), ("halo", chidx)))
            return out

        def st_A(k):
            info = bufs(k)
            pps = pps_of.pop(k)
            for (_pp, _ppk) in pps:
                pend_banks.discard(_ppk[1])
            for s_ in range(2):
                chidx, Ub, Uk, ua, uak, hkey = info[s_]
                CP(Ub[:, 0:2], halo[:, chidx, :], r=[hkey], w=[Uk], eng="pool")
            for s_ in range(2):
                chidx, Ub, Uk, ua, uak, hkey = info[s_]
                ACT(Ub[:, 2:514], pps[s_][0][:, :], AF.Copy, r=[pps[s_][1]], w=[Uk])
            for s_ in range(2):
                chidx, Ub, Uk, ua, uak, hkey = info[s_]
                ACT(ua, pps[s_][0][:, :], AF.Identity, bias=cb[:, chidx:chidx + 1],
                    scale=cw[:, 2, chidx:chidx + 1], r=[pps[s_][1], "cb", "cw"], w=[uak])
            for s_ in range(2):
                chidx, Ub, Uk, ua, uak, hkey = info[s_]
                CP(halo[:, chidx, :], Ub[:, 512:514], r=[Uk], w=[hkey], eng="pool")

        def st_B(k):
            info = bufs(k)
            for tap in (1, 0):
                for s_ in range(2):
                    chidx, Ub, Uk, ua, uak, hkey = info[s_]
                    STT(ua, Ub[:, tap:tap + 512], cw[:, tap, chidx:chidx + 1], ua, ALU.mult, ALU.add,
                        r=[Uk, uak, "cw"], w=[uak])

        def st_C(k):
            info = bufs(k)
            ACT(SA[k % 2], info[0][3], AF.Silu, r=[info[0][4]], w=[("SA", k % 2)])

        def blk_index(k):
            gi, blk, ci = jobs[k]
            return 4 * gi + blk

        def st_D(k):
            info = bufs(k)
            gi, blk, ci = jobs[k]
            bi = blk_index(k)
            TT(hid_t[:, bi % 2, ci, :], SA[k % 2], info[1][3], ALU.mult, r=[("SA", k % 2), info[1][4]],
               w=[("hid", bi % 2), "gsbc2"])

        def st_down(k_last):
            gi, blk, ci = jobs[k_last]
            c0, ncg, wub, wuk, wdb, wdk = grp(gi)
            bi = blk_index(k_last)
            if gi not in wd_scaled:
                wd_scaled.add(gi)
                for cj in range(ncg):
                    TT(wdb[:, cj, :], wdb[:, cj, :], gtbc[:, 1, :], ALU.mult, r=[wdk, "gtbc"], w=[wdk])
            hk = ("hid", bi % 2)
            for ii in range(4):
                i = 4 * blk + ii
                for hf in range(2):
                    cs_ = slice(512 * hf, 512 * (hf + 1))
                    pp, ppk = ffn_pb()
                    for cj in range(ncg):
                        MM(pp[:, :], hid_t[:, bi % 2, cj, 128 * ii:128 * (ii + 1)], wdb[:, cj, cs_], cj == 0, cj == ncg - 1,
                           r=[hk, wdk], w=[ppk])
                    TT(X1[:, i, cs_], pp[:, :], X1[:, i, cs_], ALU.add, r=[ppk, ("X1", i)], w=[("X1", i)])
            if blk == 3:
                load_wd(gi + 2)

        down_at = {}
        st_up(0)
        st_up(1)
        st_A(0)
        for k in range(NJ):
            if k + 2 < NJ:
                st_up(k + 2)
            st_B(k)
            if k + 1 < NJ:
                st_A(k + 1)
            st_C(k)
            if k >= 1:
                st_D(k - 1)
                last_of_blk = (blk_index(k - 1) != blk_index(k))
                if last_of_blk:
                    down_at[k + 1] = k - 1
            if k in down_at:
                st_down(down_at.pop(k))
        st_D(NJ - 1)
        for k_ in sorted(down_at):
            st_down(down_at[k_])
        st_down(NJ - 1)
        for i in range(NT):
            DMA(out_d[128 * i:128 * (i + 1), :], X1[:, i, :], r=[("X1", i)])

    S.emit()
    es.close()
    return nc


def _consts():
    f = np.float32
    c = {}
    a = np.arange(128)
    c["tri"] = (a[:, None] <= a[None, :]).astype(f)
    tt = np.arange(T)
    cur = tt // 64
    j = np.arange(32)
    am = np.zeros((T, 32), f)
    am[j[None, :] > cur[:, None]] = -10.0
    forced = (j[None] == 0) | (j[None] == cur[:, None]) | (j[None] == cur[:, None] - 1)
    am[forced] = 1e9
    c["am"] = am
    n_cmp = 127
    cst = np.arange(n_cmp) * 16
    sst = np.arange(32) * 64
    ovl = np.clip(np.minimum(cst[:, None] + 32, sst[None] + 64) - np.maximum(cst[:, None], sst[None]), 0, None) / 32
    c["Mov"] = ovl.astype(f)
    inv = (np.float32(500000.0) ** (-np.arange(8, dtype=f) * np.float32(2.0) / np.float32(16))).astype(f)
    c["invf"] = np.ascontiguousarray(np.broadcast_to(inv[None, :], (128, 8))).astype(f)
    nn = np.arange(128)
    c["cvalid"] = ((16 * nn[None, :] + 31 <= tt[:, None]) & (nn[None, :] < 127)).astype(f)
    c["bmask"] = ((a[:, None] <= a[None, :]) & (a[:, None] // 64 == a[None, :] // 64)).astype(f)
    c["Eblk"] = (np.arange(T)[None, :] // 64 == j[:, None]).astype(f)
    return c


CONSTS = _consts()


def make_inputs(inp, b):
    f = np.float32
    m = {}
    m["x"] = np.ascontiguousarray(inp["x"][b], dtype=f)
    m["cT"] = np.ascontiguousarray(np.asarray(inp["c"][b], dtype=f).reshape(8, 128).T)
    m["posT"] = np.ascontiguousarray(np.asarray(inp["positions"][b], dtype=np.int32).reshape(NT, 128).T)
    m["w_ada"] = np.ascontiguousarray(inp["w_ada"][0], dtype=f)
    m["b_ada"] = np.ascontiguousarray(inp["b_ada"], dtype=f).reshape(1, -1)
    m["norm1_g"] = np.ascontiguousarray(inp["norm1_g"], dtype=f).reshape(1, -1)
    m["w_in"] = np.ascontiguousarray(inp["w_in"][0], dtype=f)
    m["ident"] = np.eye(128, dtype=f)
    m.update(CONSTS)
    m["q_norm_g"] = np.ascontiguousarray(inp["q_norm_g"], dtype=f).reshape(1, 64)
    m["w_cmp1"] = np.ascontiguousarray(inp["w_cmp1"][0], dtype=f)
    m["w_cmp2"] = np.ascontiguousarray(inp["w_cmp2"][0], dtype=f)
    m["peT"] = np.ascontiguousarray(np.asarray(inp["pe_cmp"][0], dtype=f).transpose(2, 0, 1))
    m["lbT"] = np.ascontiguousarray(np.asarray(inp["lb_logits"], dtype=f).reshape(2, 4, 128).transpose(2, 0, 1))
    m["norm2_g"] = np.ascontiguousarray(inp["norm2_g"], dtype=f).reshape(1, -1)
    m["w_out"] = np.ascontiguousarray(inp["w_out"][0], dtype=f)
    m["hgng"] = np.ascontiguousarray(inp["hg_norm_g"], dtype=f).reshape(128, 1)
    m["w_up"] = np.ascontiguousarray(inp["w_up"][0], dtype=f)
    m["w_down"] = np.ascontiguousarray(inp["w_down"][0], dtype=f)
    m["cwT"] = np.ascontiguousarray(np.asarray(inp["conv_w"][0], dtype=f).reshape(3, 44, 128).transpose(2, 0, 1))
    m["cbT"] = np.ascontiguousarray(np.asarray(inp["conv_b"][0], dtype=f).reshape(44, 128).T)
    m["k_norm_g"] = np.ascontiguousarray(inp["k_norm_g"], dtype=f).reshape(1, 192)
    return m


def run(inp, dbg=(), cores=8):
    nc = build(dbg)
    in_maps = [make_inputs(inp, b) for b in range(cores)]
    res = run_bass_kernel_spmd(nc, in_maps, core_ids=list(range(cores)))
    return res


def kernel(**inputs):
    inp = {k: np.asarray(v) for k, v in inputs.items()}
    res = run(inp)
    return np.stack([r["out"] for r in res.results], axis=0).astype(np.float32)
```
